# Optimizing a Trainium2 kernel written in Bass

```python
import jax
import jax.numpy as jnp
from jax import lax
import numpy as np

D_MODEL = 2048
BATCH = 1
SEQ = 8192
DEPTH = 1
DEC_BATCH = 32
DEC_SEQ = 8
PAST_LEN = 16384
PAGE_SIZE = 128

HEAD_DIM = 128
D_MIX = D_MODEL
C_CONV = D_MIX // 2
N_HEADS = (D_MIX - C_CONV) // HEAD_DIM
N_KV = 2
GROUP = N_HEADS // N_KV
CONV_WIDTH = 31
CMP_STRIDE = 16
CMP_BLOCK = 2 * CMP_STRIDE
CMP_HIDDEN = HEAD_DIM
SEL_BLOCK = 64
N_SELECT = 16
WINDOW = 512
Q_BLOCK = 128
EPS = 1e-6
SCALE = HEAD_DIM ** -0.5
IN_SIZES = (C_CONV, C_CONV, C_CONV,
            N_HEADS * HEAD_DIM,
            N_KV * HEAD_DIM, N_KV * HEAD_DIM,
            N_KV * HEAD_DIM, N_KV * HEAD_DIM,
            N_KV * HEAD_DIM, N_KV * HEAD_DIM,
            3 * N_HEADS,
            N_HEADS * HEAD_DIM)
D_IN = sum(IN_SIZES)

kernel_name = 'hymba_conformer_nsa_alibi_step'


def _rmsnorm(x, g):
    xf = x.astype(jnp.float32)
    y = xf * lax.rsqrt(jnp.mean(xf * xf, axis=-1, keepdims=True) + EPS)
    return (y * g.astype(jnp.float32)).astype(x.dtype)


def _layernorm(x, g, b):
    xf = x.astype(jnp.float32)
    xc = xf - jnp.mean(xf, axis=-1, keepdims=True)
    y = xc * lax.rsqrt(jnp.mean(xc * xc, axis=-1, keepdims=True) + EPS)
    return (y * g.astype(jnp.float32) + b.astype(jnp.float32)).astype(x.dtype)


def _alibi_slopes():
    h = jnp.arange(1, N_HEADS + 1, dtype=jnp.float32)
    return jnp.exp2(-8.0 * h / N_HEADS).reshape(N_KV, GROUP)


def _in_proj(h, w_in, g_q, g_k_slc, g_k_win):
    b, t, _ = h.shape
    z = jnp.einsum('btd,de->bte', h, w_in)
    offs = np.cumsum(IN_SIZES)[:-1].tolist()
    ua, ub, zc, q, kc, vc, ks, vs, kw, vw, gt, zn = jnp.split(z, offs, axis=-1)
    kv = lambda a: a.reshape(b, t, N_KV, HEAD_DIM)
    q = _rmsnorm(q.reshape(b, t, N_KV, GROUP, HEAD_DIM), g_q)
    gates = jax.nn.sigmoid(gt.astype(jnp.float32)).reshape(b, t, N_KV, GROUP, 3).astype(h.dtype)
    return (ua, ub, zc, q, kv(kc), kv(vc), _rmsnorm(kv(ks), g_k_slc), kv(vs),
            _rmsnorm(kv(kw), g_k_win), kv(vw), gates, zn)


def _conv_module(ua, ub, zc, prefix, w_dw, b_dw, ln_g, ln_b, w_pw2, b_pw2):
    u = ua * jax.nn.sigmoid(ub)
    full = jnp.concatenate([prefix.astype(u.dtype), u], axis=1)
    y = lax.conv_general_dilated(full, w_dw[:, None, :].astype(u.dtype), window_strides=(1,), padding='VALID',
                                 dimension_numbers=('NWC', 'WIO', 'NWC'), feature_group_count=C_CONV)
    y = jax.nn.silu(_layernorm(y + b_dw, ln_g, ln_b))
    y = jnp.einsum('btc,ce->bte', y, w_pw2) + b_pw2
    return y * jax.nn.silu(zc), full[:, full.shape[1] - (CONV_WIDTH - 1):]


def _chunk_feats(rows, pe, w1):
    b, l, g, d = rows.shape
    c = rows.reshape(b, l // CMP_STRIDE, CMP_STRIDE, g, d)
    fa = jnp.einsum('bncgd,cde->bnge', c + pe[:CMP_STRIDE, None, :], w1[:CMP_STRIDE])
    fb = jnp.einsum('bncgd,cde->bnge', c + pe[CMP_STRIDE:, None, :], w1[CMP_STRIDE:])
    return fa, fb


def _compress(rows_list, pe, w1, w2):
    feats = [_chunk_feats(r, pe, w1) for r in rows_list]
    fa = jnp.concatenate([f[0] for f in feats], axis=1)
    fb = jnp.concatenate([f[1] for f in feats], axis=1)
    hid = jax.nn.silu(fa[:, :-1] + fb[:, 1:])
    return jnp.einsum('bnge,ed->bngd', hid, w2)


def _masked_softmax(s, mask):
    s = jnp.where(mask, s, -jnp.inf)
    m = jnp.max(s, axis=-1, keepdims=True)
    m = jnp.where(jnp.isfinite(m), m, 0.0)
    e = jnp.where(mask, jnp.exp(s - m), 0.0)
    return e / jnp.maximum(jnp.sum(e, axis=-1, keepdims=True), 1e-30)


def _cmp_branch(q, kc, vc, q_pos, slopes):
    end = jnp.arange(kc.shape[1]) * CMP_STRIDE + (CMP_BLOCK - 1)
    dist = q_pos[:, None] - end[None, :]
    s = jnp.einsum('bqgrd,bngd->bqgrn', q, kc).astype(jnp.float32) * SCALE
    s = s - slopes[:, :, None] * dist[:, None, None, :].astype(jnp.float32)
    p = _masked_softmax(s, (dist >= 0)[:, None, None, :])
    return jnp.einsum('bqgrn,bngd->bqgrd', p.astype(vc.dtype), vc), p


def _cmp_to_sel(imp, n_sel):
    r = SEL_BLOCK // CMP_STRIDE
    lead = CMP_BLOCK // CMP_STRIDE - 1
    total = lead + r * n_sel + r
    pp = jnp.pad(imp, ((0, 0), (0, 0), (0, 0), (lead, total - lead - imp.shape[-1])))
    out = jnp.zeros(imp.shape[:-1] + (n_sel,), imp.dtype)
    for o in range(-lead, r):
        start = o * CMP_STRIDE
        w = (min(start + CMP_BLOCK, SEL_BLOCK) - max(start, 0)) / CMP_BLOCK
        out = out + w * pp[..., lead + o: lead + o + r * n_sel: r]
    return out


def _select_blocks(p, q_pos, n_sel):
    imp = _cmp_to_sel(jnp.sum(p, axis=3), n_sel)
    j = jnp.arange(n_sel)[None, :]
    cur = (q_pos // SEL_BLOCK)[:, None]
    valid = j <= cur
    forced = (j == 0) | (j == cur) | (j == cur - 1)
    score = jnp.where(forced[:, None, :], jnp.inf, jnp.where(valid[:, None, :], imp, -jnp.inf))
    _, idx = lax.top_k(score, min(N_SELECT, n_sel))
    return idx


def _gathered_attend(q, k, v, k_pos, q_pos, slopes):
    dist = q_pos[None, :, None, None] - k_pos
    s = jnp.einsum('bqgrd,bqgsd->bqgrs', q, k).astype(jnp.float32) * SCALE
    s = s - slopes[:, :, None] * dist[:, :, :, None, :].astype(jnp.float32)
    p = _masked_softmax(s, (dist >= 0)[:, :, :, None, :])
    return jnp.einsum('bqgrs,bqgsd->bqgrd', p.astype(v.dtype), v)


def _band_attend(q, k, v, q_pos, k_pos, slopes):
    dist = q_pos[:, :, None] - k_pos[:, None, :]
    mask = (k_pos[:, None, :] >= 0) & (dist >= 0) & (dist < WINDOW)
    s = jnp.einsum('bnqgrd,bnsgd->bnqgrs', q, k).astype(jnp.float32) * SCALE
    s = s - slopes[:, :, None] * dist[:, :, None, None, :].astype(jnp.float32)
    p = _masked_softmax(s, mask[:, :, None, None, :])
    return jnp.einsum('bnqgrs,bnsgd->bnqgrd', p.astype(v.dtype), v)


def _prompt_selected(q, ks, vs, idx, slopes):
    b, t = q.shape[:2]
    n_sel = t // SEL_BLOCK
    nb = t // Q_BLOCK
    n_k = idx.shape[-1]
    kb = ks.reshape(b, n_sel, SEL_BLOCK, N_KV, HEAD_DIM)
    vb = vs.reshape(b, n_sel, SEL_BLOCK, N_KV, HEAD_DIM)
    bi = jnp.arange(b)[:, None, None, None]
    gi = jnp.arange(N_KV)[None, None, :, None]
    off = jnp.arange(SEL_BLOCK)
    shape = (b, Q_BLOCK, N_KV, n_k * SEL_BLOCK)

    def block(args):
        qb, ib, pb = args
        kg = kb[bi, ib, :, gi, :].reshape(shape + (HEAD_DIM,))
        vg = vb[bi, ib, :, gi, :].reshape(shape + (HEAD_DIM,))
        kpos = (ib[..., None] * SEL_BLOCK + off).reshape(shape)
        return _gathered_attend(qb, kg, vg, kpos, pb, slopes)

    xs = (q.reshape(b, nb, Q_BLOCK, N_KV, GROUP, HEAD_DIM).swapaxes(0, 1),
          idx.reshape(b, nb, Q_BLOCK, N_KV, n_k).swapaxes(0, 1),
          jnp.arange(t).reshape(nb, Q_BLOCK))
    out = lax.map(block, xs)
    return out.swapaxes(0, 1).reshape(b, t, N_KV, GROUP, HEAD_DIM)


def _sample_selected(q, ks, vs, cache_k, cache_v, page_table, idx, q_pos, slopes):
    db, ds = q.shape[:2]
    bpp = PAGE_SIZE // SEL_BLOCK
    nb_past = page_table.shape[1] * bpp
    n_tail = -(-ds // SEL_BLOCK)
    n_k = idx.shape[-1]
    bi = jnp.arange(db)[:, None, None, None]
    gi = jnp.arange(N_KV)[None, None, :, None]
    jp = jnp.minimum(idx, nb_past - 1)
    phys = page_table[bi, jp // bpp]
    jt = jnp.clip(idx - nb_past, 0, n_tail - 1)
    in_past = (idx < nb_past)[..., None, None]
    shape = (db, ds, N_KV, n_k * SEL_BLOCK)

    def gather(pool, new):
        pb = pool.reshape(pool.shape[0], bpp, SEL_BLOCK, N_KV, HEAD_DIM)
        tb = jnp.pad(new, ((0, 0), (0, n_tail * SEL_BLOCK - ds), (0, 0), (0, 0)))
        tb = tb.reshape(db, n_tail, SEL_BLOCK, N_KV, HEAD_DIM)
        g = jnp.where(in_past, pb[phys, jp % bpp, :, gi, :], tb[bi, jt, :, gi, :])
        return g.reshape(shape + (HEAD_DIM,))

    kpos = (idx[..., None] * SEL_BLOCK + jnp.arange(SEL_BLOCK)).reshape(shape)
    return _gathered_attend(q, gather(cache_k, ks), gather(cache_v, vs), kpos, q_pos, slopes)


def _prompt_window(q, kw, vw, slopes):
    b, t = q.shape[:2]
    nb = t // Q_BLOCK
    idx = jnp.arange(nb)[:, None] * Q_BLOCK + jnp.arange(WINDOW + Q_BLOCK)[None, :]
    padw = ((0, 0), (WINDOW, 0), (0, 0), (0, 0))
    kb = jnp.pad(kw, padw)[:, idx]
    vb = jnp.pad(vw, padw)[:, idx]
    qb = q.reshape(b, nb, Q_BLOCK, N_KV, GROUP, HEAD_DIM)
    o = _band_attend(qb, kb, vb, jnp.arange(t).reshape(nb, Q_BLOCK), idx - WINDOW, slopes)
    return o.reshape(b, t, N_KV, GROUP, HEAD_DIM)


def _merge(x, conv_o, o_c, o_s, o_w, gates, zn, w_out):
    b, t = x.shape[:2]
    o = gates[..., 0:1] * o_c + gates[..., 1:2] * o_s + gates[..., 2:3] * o_w
    o = o.reshape(b, t, N_HEADS * HEAD_DIM) * jax.nn.silu(zn)
    mixed = jnp.concatenate([conv_o, o], axis=-1)
    return x + jnp.einsum('bte,ed->btd', mixed, w_out)


def setup_inputs(seed: int = 0) -> dict:
    key = jax.random.key(seed)
    k = jax.random.split(key, 29)
    f32 = jnp.float32
    n_pages = PAST_LEN // PAGE_SIZE
    n_used = DEC_BATCH * n_pages
    n_phys = n_used + (n_used + 3) // 4
    w_buf = min(WINDOW, PAST_LEN)
    pool = (n_phys, PAGE_SIZE, N_KV, HEAD_DIM)

    def nrm(i, shape, scale):
        return jax.random.normal(k[i], shape, f32) * scale

    def gain(i, n):
        return 1.0 + nrm(i, (n,), 0.02)

    page_table = jax.random.permutation(k[9], n_phys)[:n_used].reshape(DEC_BATCH, n_pages).astype(jnp.int32)
    return {
        'x_prompt': nrm(0, (BATCH, SEQ, D_MODEL), 1.0),
        'x_sample': nrm(1, (DEC_BATCH, DEC_SEQ, D_MODEL), 1.0),
        'cache_k_cmp': nrm(2, pool, 1.0),
        'cache_v_cmp': nrm(3, pool, 1.0),
        'cache_k_slc': nrm(4, pool, 1.0),
        'cache_v_slc': nrm(5, pool, 1.0),
        'state_k_win': nrm(6, (DEC_BATCH, w_buf, N_KV, HEAD_DIM), 1.0),
        'state_v_win': nrm(7, (DEC_BATCH, w_buf, N_KV, HEAD_DIM), 1.0),
        'state_conv': nrm(8, (DEC_BATCH, CONV_WIDTH - 1, C_CONV), 0.5),
        'page_table': page_table,
        'g_norm': gain(10, D_MODEL),
        'w_in': nrm(11, (D_MODEL, D_IN), D_MODEL ** -0.5),
        'pe_cmp_k': nrm(12, (CMP_BLOCK, HEAD_DIM), 0.1),
        'w_cmp_k1': nrm(13, (CMP_BLOCK, HEAD_DIM, CMP_HIDDEN), (CMP_BLOCK * HEAD_DIM) ** -0.5),
        'w_cmp_k2': nrm(14, (CMP_HIDDEN, HEAD_DIM), CMP_HIDDEN ** -0.5),
        'pe_cmp_v': nrm(15, (CMP_BLOCK, HEAD_DIM), 0.1),
        'w_cmp_v1': nrm(16, (CMP_BLOCK, HEAD_DIM, CMP_HIDDEN), (CMP_BLOCK * HEAD_DIM) ** -0.5),
        'w_cmp_v2': nrm(17, (CMP_HIDDEN, HEAD_DIM), CMP_HIDDEN ** -0.5),
        'g_q': gain(18, HEAD_DIM),
        'g_k_cmp': gain(19, HEAD_DIM),
        'g_k_slc': gain(20, HEAD_DIM),
        'g_k_win': gain(21, HEAD_DIM),
        'w_dw': nrm(22, (CONV_WIDTH, C_CONV), CONV_WIDTH ** -0.5),
        'b_dw': nrm(23, (C_CONV,), 0.01),
        'ln_g': gain(24, C_CONV),
        'ln_b': nrm(25, (C_CONV,), 0.01),
        'w_pw2': nrm(26, (C_CONV, C_CONV), C_CONV ** -0.5),
        'b_pw2': nrm(27, (C_CONV,), 0.01),
        'w_out': nrm(28, (D_MIX, D_MODEL), D_MIX ** -0.5),
    }


def reference(x_prompt, x_sample, cache_k_cmp, cache_v_cmp, cache_k_slc, cache_v_slc, state_k_win, state_v_win,
              state_conv, page_table, g_norm, w_in, pe_cmp_k, w_cmp_k1, w_cmp_k2, pe_cmp_v, w_cmp_v1, w_cmp_v2,
              g_q, g_k_cmp, g_k_slc, g_k_win, w_dw, b_dw, ln_g, ln_b, w_pw2, b_pw2, w_out):
    slopes = _alibi_slopes()

    b, t, _ = x_prompt.shape
    ua, ub, zc, q, kc, vc, ks, vs, kw, vw, gates, zn = _in_proj(_rmsnorm(x_prompt, g_norm), w_in, g_q, g_k_slc, g_k_win)
    conv_o, conv_p = _conv_module(ua, ub, zc, jnp.zeros((b, CONV_WIDTH - 1, C_CONV), x_prompt.dtype),
                                  w_dw, b_dw, ln_g, ln_b, w_pw2, b_pw2)
    pos = jnp.arange(t)
    k_c = _rmsnorm(_compress([kc], pe_cmp_k, w_cmp_k1, w_cmp_k2), g_k_cmp)
    v_c = _compress([vc], pe_cmp_v, w_cmp_v1, w_cmp_v2)
    o_c, p_c = _cmp_branch(q, k_c, v_c, pos, slopes)
    idx = _select_blocks(p_c, pos, -(-t // SEL_BLOCK))
    o_s = _prompt_selected(q, ks, vs, idx, slopes)
    o_w = _prompt_window(q, kw, vw, slopes)
    y_prompt = _merge(x_prompt, conv_o, o_c, o_s, o_w, gates, zn, w_out)
    w_p = min(WINDOW, t)

    db, ds, _ = x_sample.shape
    past = page_table.shape[1] * PAGE_SIZE
    ua2, ub2, zc2, q2, kc2, vc2, ks2, vs2, kw2, vw2, gates2, zn2 = _in_proj(
        _rmsnorm(x_sample, g_norm), w_in, g_q, g_k_slc, g_k_win)
    conv_o2, conv_s = _conv_module(ua2, ub2, zc2, state_conv, w_dw, b_dw, ln_g, ln_b, w_pw2, b_pw2)
    spos = past + jnp.arange(ds)
    n_new = (ds // CMP_STRIDE) * CMP_STRIDE
    past_kc = cache_k_cmp[page_table].reshape(db, past, N_KV, HEAD_DIM)
    past_vc = cache_v_cmp[page_table].reshape(db, past, N_KV, HEAD_DIM)
    k_c2 = _rmsnorm(_compress([past_kc, kc2[:, :n_new]], pe_cmp_k, w_cmp_k1, w_cmp_k2), g_k_cmp)
    v_c2 = _compress([past_vc, vc2[:, :n_new]], pe_cmp_v, w_cmp_v1, w_cmp_v2)
    o_c2, p_c2 = _cmp_branch(q2, k_c2, v_c2, spos, slopes)
    idx2 = _select_blocks(p_c2, spos, -(-(past + ds) // SEL_BLOCK))
    o_s2 = _sample_selected(q2, ks2, vs2, cache_k_slc, cache_v_slc, page_table, idx2, spos, slopes)
    wb = state_k_win.shape[1]
    kw_all = jnp.concatenate([state_k_win.astype(kw2.dtype), kw2], axis=1)
    vw_all = jnp.concatenate([state_v_win.astype(vw2.dtype), vw2], axis=1)
    kpos_w = past - wb + jnp.arange(wb + ds)
    o_w2 = _band_attend(q2[:, None], kw_all[:, None], vw_all[:, None], spos[None], kpos_w[None], slopes)[:, 0]
    y_sample = _merge(x_sample, conv_o2, o_c2, o_s2, o_w2, gates2, zn2, w_out)
    n_all = wb + ds

    return (y_prompt, y_sample,
            kc, vc, ks, vs, kw[:, t - w_p:], vw[:, t - w_p:], conv_p,
            kc2, vc2, ks2, vs2, kw_all[:, n_all - wb:], vw_all[:, n_all - wb:], conv_s)
```

```python
import contextlib
import numpy as np
import ml_dtypes
import concourse.bass as bass
import concourse.mybir as mybir
from concourse.bass_utils import run_bass_kernel_spmd

F32 = mybir.dt.float32
BF16 = mybir.dt.bfloat16
I32 = mybir.dt.int32
AF = mybir.ActivationFunctionType
ALU = mybir.AluOpType
AX = mybir.AxisListType
NPBF = ml_dtypes.bfloat16

NCORES = 8
D = 2048
T = 8192
HD = 128
NQB = 64
NSLOT = 8
DIN = 6680
EPS = 1e-6
SCALE = HD ** -0.5
PAST = 16384
NPAGE = 128
NPHYS = 5120
SLOPES = [2.0 ** (-(h + 1)) for h in range(8)]
NEG = -30000.0
C_UA, C_UB, C_ZC, C_Q, C_KC, C_KS, C_KW, C_GT, C_ZN = 0, 1024, 2048, 3072, 4096, 4608, 5120, 5632, 5656
TOKC = 8 * 160 + 32
NOWN = 1024 + 32
NS_DMA = 8


class Tok:
    __slots__ = ("w", "r")

    def __init__(self):
        self.w = None
        self.r = {}


class V:
    def __init__(self, ap, toks):
        self.ap = ap
        self.toks = toks

    def __getitem__(self, i):
        return V(self.ap[i], self.toks)

    def re(self, pat, **kw):
        return V(self.ap.rearrange(pat, **kw), self.toks)

    def bc(self, shape):
        return V(self.ap.broadcast_to(shape), self.toks)

    def us(self, dim):
        return V(self.ap.unsqueeze(dim), self.toks)

    def cast(self, dt):
        return V(self.ap.bitcast(dt), self.toks)

    def tk(self, tok):
        return V(self.ap, [tok])


class Ring:
    def __init__(self, items):
        self.items = items
        self.i = 0

    def next(self):
        it = self.items[self.i % len(self.items)]
        self.i += 1
        return it


def run_pipeline(jobs):
    if not jobs:
        return
    ns = max(len(j) for j in jobs)
    for step in range(len(jobs) + ns - 1):
        for st in range(ns):
            t = step - st
            if 0 <= t < len(jobs) and st < len(jobs[t]):
                jobs[t][st]()

ENG = ("pe", "act", "dve", "pool", "sp")


class Rec:
    def __init__(self):
        self.nc = bass.Bass("TRN2", target_bir_lowering=False)
        self.gst = contextlib.ExitStack()
        self.ops = {e: [] for e in ENG}
        self.cnt = {e: 0 for e in ENG}
        self.seen = {e: {} for e in ENG}
        self.dman = {e: 0 for e in ENG}
        self.sems = {}
        self.slot_last = {}
        for e in ("pe", "act", "dve", "pool"):
            self.sems[e] = self.gst.enter_context(self.nc.semaphore("c_" + e))
        for q in ("sp", "pool", "act"):
            for i in range(NS_DMA):
                nm = "d_%s_%d" % (q, i)
                self.sems[nm] = self.gst.enter_context(self.nc.semaphore(nm))
        self.uid = 0
        self.stack = self.gst

    def sb(self, name, shape, dt):
        self.uid += 1
        t = self.stack.enter_context(self.nc.sbuf_tensor("%s_%d" % (name, self.uid), list(shape), dt))
        return V(t[tuple(slice(None) for _ in shape)], [Tok()])

    def ring(self, name, n, shape, dt):
        return Ring([self.sb(name + str(i), shape, dt) for i in range(n)])

    def psum(self, name):
        t = self.gst.enter_context(self.nc.psum_tensor(name, [128, 512], F32))
        return V(t[:, :], [Tok()])

    def dram(self, name, shape, dt, kind):
        t = self.nc.dram_tensor(name, list(shape), dt, kind=kind)
        return V(t.ap(), [Tok()])

    @contextlib.contextmanager
    def phase(self):
        old = self.stack
        with contextlib.ExitStack() as st:
            self.stack = st
            yield
            self.flush()
        self.stack = old

    def _rec(self, eng, fn, R, W, dma=False):
        waits = {}

        def need(dep):
            if dep is None:
                return
            k, v = dep
            if eng == "pe" and k == "pe":
                return
            if waits.get(k, 0) < v:
                waits[k] = v

        for t in R:
            need(t.w)
        for t in W:
            need(t.w)
            for k, v in t.r.items():
                need((k, v))
        if dma:
            n = self.dman[eng]
            self.dman[eng] += 1
            slot = "d_%s_%d" % (eng, n % NS_DMA)
            rnd = n // NS_DMA
            if rnd > 0:
                need((slot, 16 * rnd))
            dep = (slot, 16 * (rnd + 1))
            inc = (slot, 16)
            self.slot_last[slot] = dep[1]
        else:
            self.cnt[eng] += 1
            dep = (eng, self.cnt[eng])
            inc = (eng, 1)
        wl = []
        seen = self.seen[eng]
        for k, v in waits.items():
            if seen.get(k, 0) < v:
                seen[k] = v
                wl.append((k, v))
        self.ops[eng].append((fn, wl, inc))
        for t in R:
            if t.r.get(dep[0], 0) < dep[1]:
                t.r[dep[0]] = dep[1]
        for t in W:
            t.w = dep
            t.r = {}

    def flush(self, final=False):
        if final:
            wl = [(k, v) for k, v in self.slot_last.items()]
            self.ops["sp"].append((None, wl, None))
        ops_all = self.ops
        self.ops = {e: [] for e in ENG}
        sems = self.sems
        with self.nc.Block() as blk:
            for eng, deco in (("pe", blk.tensor), ("act", blk.scalar), ("dve", blk.vector),
                              ("pool", blk.gpsimd), ("sp", blk.sync)):
                ops = ops_all[eng]

                def body(e, ops=ops):
                    for fn, wl, inc in ops:
                        for k, v in wl:
                            e.wait_ge(sems[k], v)
                        if fn is not None:
                            fn(e).then_inc(sems[inc[0]], inc[1])

                deco(body)

    @staticmethod
    def _s(x):
        return x.ap if isinstance(x, V) else x

    @staticmethod
    def _t(*xs):
        out = []
        for x in xs:
            if isinstance(x, V):
                out += x.toks
        return out

    def mm(self, out, lhsT, rhs, start=True, stop=True):
        self._rec("pe", lambda e: e.matmul(out=out.ap, lhsT=lhsT.ap, rhs=rhs.ap, start=start, stop=stop),
                  self._t(lhsT, rhs), out.toks)

    def tr(self, out, in_, ident):
        self._rec("pe", lambda e: e.transpose(out=out.ap, in_=in_.ap, identity=ident.ap),
                  self._t(in_, ident), out.toks)

    def act(self, out, in_, func, bias=None, scale=None, accum=None):
        kw = {}
        if bias is not None:
            kw["bias"] = self._s(bias)
        if scale is not None:
            kw["scale"] = self._s(scale)
        if accum is not None:
            kw["accum_out"] = accum.ap
        self._rec("act", lambda e: e.activation(out=out.ap, in_=in_.ap, func=func, **kw),
                  self._t(in_, bias, scale), self._t(out, accum))

    def ts(self, eng, out, in0, s1, s2, op0, op1=None, accum=None):
        kw = {}
        if op1 is not None:
            kw["op1"] = op1
        if accum is not None:
            kw["accum_out"] = accum.ap
        a1, a2 = self._s(s1), self._s(s2)
        self._rec(eng, lambda e: e.tensor_scalar(out=out.ap, in0=in0.ap, scalar1=a1, scalar2=a2, op0=op0, **kw),
                  self._t(in0, s1, s2), self._t(out, accum))

    def tt(self, eng, out, in0, in1, op):
        self._rec(eng, lambda e: e.tensor_tensor(out=out.ap, in0=in0.ap, in1=in1.ap, op=op),
                  self._t(in0, in1), out.toks)

    def stt(self, eng, out, in0, sc, in1, op0, op1):
        a = self._s(sc)
        self._rec(eng, lambda e: e.scalar_tensor_tensor(out=out.ap, in0=in0.ap, scalar=a, in1=in1.ap, op0=op0, op1=op1),
                  self._t(in0, sc, in1), out.toks)

    def cp(self, eng, out, in_):
        if eng == "act":
            self._rec("act", lambda e: e.copy(out=out.ap, in_=in_.ap), in_.toks, out.toks)
        else:
            self._rec(eng, lambda e: e.tensor_copy(out=out.ap, in_=in_.ap), in_.toks, out.toks)

    def red(self, out, in_, op=ALU.add, axis=AX.X):
        self._rec("dve", lambda e: e.tensor_reduce(out=out.ap, in_=in_.ap, axis=axis, op=op), in_.toks, out.toks)

    def recip(self, out, in_):
        self._rec("dve", lambda e: e.reciprocal(out=out.ap, in_=in_.ap), in_.toks, out.toks)

    def memset(self, eng, out, val):
        self._rec(eng, lambda e: e.memset(out.ap, val), [], out.toks)

    def max8(self, out, in_):
        self._rec("dve", lambda e: e.max(out=out.ap, in_=in_.ap), in_.toks, out.toks)

    def mrep(self, out, rep, vals, imm):
        self._rec("dve", lambda e: e.match_replace(out=out.ap, in_to_replace=rep.ap, in_values=vals.ap, imm_value=imm),
                  self._t(rep, vals), out.toks)

    def dma(self, out, in_, q="sp"):
        self._rec(q, lambda e: e.dma_start(out=out.ap, in_=in_.ap), in_.toks, out.toks, dma=True)

    def gather(self, out, in_, idx):
        self._rec("pool", lambda e: e.indirect_dma_start(
            out=out.ap, out_offset=None, in_=in_.ap,
            in_offset=bass.IndirectOffsetOnAxis(ap=idx.ap, axis=0)),
            self._t(in_, idx), out.toks, dma=True)


class Prog:
    def __init__(self, stages):
        self.stages = stages
        k = self.k = Rec()
        self.din = {}
        self.dout = {}
        self.ps = Ring([k.psum("ps%d" % i) for i in range(8)])
        self.build()

    def inp(self, name, shape, dt=F32):
        v = self.k.dram(name, shape, dt, "ExternalInput")
        self.din[name] = v
        return v

    def outp(self, name, shape, dt=F32):
        v = self.k.dram(name, shape, dt, "ExternalOutput")
        self.dout[name] = v
        return v

    def scr(self, name, shape, dt):
        return self.k.dram(name, shape, dt, "Internal")

    def rstd_from_ss(self, st, R, n, inv_n):
        k = self.k
        k.act(st[:R, n:2 * n], st[:R, 0:n], AF.Ln, scale=inv_n, bias=self.epsc[:R, 0:1])
        k.act(st[:R, 2 * n:3 * n], st[:R, n:2 * n], AF.Exp, scale=-0.5)
        return st[:R, 2 * n:3 * n]

    def silu_from(self, eng, out, src, R, n, bias=None):
        k = self.k
        e = self.etmp.next()
        if bias is None:
            k.act(e[:R, :n], src, AF.Exp, scale=-1.0)
            k.ts("dve", e[:R, :n], e[:R, :n], 1.0, None, ALU.add)
            k.recip(e[:R, :n], e[:R, :n])
            k.tt("dve", out, src, e[:R, :n], ALU.mult)
        else:
            z = self.etmp.next()
            k.ts("dve", z[:R, :n], src, bias, None, ALU.add)
            k.act(e[:R, :n], z[:R, :n], AF.Exp, scale=-1.0)
            k.ts("dve", e[:R, :n], e[:R, :n], 1.0, None, ALU.add)
            k.recip(e[:R, :n], e[:R, :n])
            k.tt("dve", out, z[:R, :n], e[:R, :n], ALU.mult)

    def front(self, src, R, dst):
        xt = self.front_load(src, R)
        self.front_rest(xt, R, dst)

    def front_load(self, src, R):
        xt = self.xt_ring.next()
        self.k.dma(xt[:R], src)
        return xt

    def front_rest(self, xt, R, dst):
        k = self.k
        xs = self.xs_ring.next()
        st = self.st_ring.next()
        k.act(xs[:R], xt[:R], AF.Square, accum=st[:R, 0:1])
        rs = self.rstd_from_ss(st, R, 1, 1.0 / D)
        k.ts("dve", xs[:R], xt[:R], rs, None, ALU.mult)
        for half in range(2):
            pb = self.ps.next().cast(BF16)
            for j in range(8):
                kc = half * 8 + j
                k.tr(pb[:, j * R:(j + 1) * R], xs[:R, kc * 128:(kc + 1) * 128], self.ident[:R, :R])
            k.cp("act" if half == 0 else "dve", dst[:, half * 8:(half + 1) * 8, :],
                 pb[:, 0:8 * R].re("p (a b) -> p a b", a=8))

    def wload(self, c0, wd, scale_g=True, src=None, nkc=16):
        k = self.k
        wb = self.wb_ring.next()
        src = self.din["w_in"] if src is None else src
        for q4 in range(nkc // 4):
            stg = self.wstg_ring.next()
            k.dma(stg[:, :, :wd], src[q4 * 512:(q4 + 1) * 512, c0:c0 + wd].re("(a p) c -> p a c", p=128),
                  q="sp")
            for a in range(4):
                kc = q4 * 4 + a
                if scale_g:
                    k.ts("pool", wb[:, kc, :wd], stg[:, a, :wd], self.gcol[:, kc:kc + 1], 1.0, ALU.mult, ALU.mult)
                else:
                    k.cp("pool", wb[:, kc, :wd], stg[:, a, :wd])
        return wb

    def normed(self, pb, R, nh, gidx, out_f32=None, out_bf=None, mul=None):
        k = self.k
        sq = self.etmp.next()
        st = self.st_ring.next()
        k.act(sq[:R, :nh * 128], pb, AF.Square)
        k.red(st[:R, 0:nh], sq[:R, :nh * 128].re("p (h d) -> p h d", h=nh))
        rs = self.rstd_from_ss(st, R, nh, 1.0 / HD)
        tmp = self.etmp.next()
        t3 = tmp[:R, :nh * 128].re("p (h d) -> p h d", h=nh)
        k.tt("dve", t3, pb.re("p (h d) -> p h d", h=nh), rs.us(2).bc([R, nh, 128]), ALU.mult)
        g = self.gvec[:R, gidx:gidx + 1, :].bc([R, nh, 128])
        if out_f32 is not None:
            k.tt("dve", out_f32.re("p (h d) -> p h d", h=nh), t3, g, ALU.mult)
            if out_bf is not None:
                k.cp("dve", out_bf, out_f32)
        else:
            k.tt("dve", out_bf.re("p (h d) -> p h d", h=nh), t3, g, ALU.mult)

    def build(self):
        k = self.k
        inp, outp = self.inp, self.outp
        xB = inp("xB", [8, 160, D])
        xS = inp("xS", [32, D])
        w_in = inp("w_in", [D, DIN])
        gcol_d = inp("gcol", [128, 16])
        cvec_d = inp("cvec", [35, 1024])
        gvec_d = inp("gvec", [128, 512])
        w_pw2 = inp("w_pw2", [1024, 1024])
        ident_d = inp("ident", [128, 128], BF16)
        identf_d = inp("identf", [128, 128])
        sconv_d = inp("sconv", [120, 1024])

        o_kc = outp("o_kc", [8, 128, 256]); o_vc = outp("o_vc", [8, 128, 256])
        o_ks = outp("o_ks", [8, 128, 256]); o_vs = outp("o_vs", [8, 128, 256])
        o_kw = outp("o_kw", [8, 128, 256]); o_vw = outp("o_vw", [8, 128, 256])
        o_convp = outp("o_convp", [30, 1024])
        o_kc2 = outp("o_kc2", [32, 256]); o_vc2 = outp("o_vc2", [32, 256])
        o_ks2 = outp("o_ks2", [32, 256]); o_vs2 = outp("o_vs2", [32, 256])
        o_convs = outp("o_convs", [120, 1024])
        o_kwin = outp("o_kwin", [4, 512, 256]); o_vwin = outp("o_vwin", [4, 512, 256])
        o_y = outp("o_y", [8, 128, D])
        o_y2 = outp("o_y2", [32, D])

        mixT = self.scr("mixT", [D, NOWN], BF16)
        qT_scr = self.scr("qT_scr", [8, 128, 8, 128], BF16)
        sz_scr = self.scr("sz_scr", [NOWN, 1024], BF16)

        self.ident = k.sb("ident", [128, 128], BF16)
        self.identf = k.sb("identf", [128, 128], F32)
        self.gcol = k.sb("gcol", [128, 16], F32)
        self.gvec = k.sb("gvec", [128, 4, 128], F32)
        self.cvT = k.sb("cvT", [128, 8, 35], F32)
        self.epsc = k.sb("epsc", [128, 1], F32)
        ksT_own = k.sb("ksT_own", [128, 8, 2, 128], BF16)
        Vs_own = k.sb("Vs_own", [128, 8, 2, 130], BF16)
        KwT_own = k.sb("KwT_own", [128, 8, 2, 128], BF16)
        Vw_own = k.sb("Vw_own", [128, 8, 2, 130], BF16)
        gates = k.sb("gates", [128, 8, 24], F32)
        q2T = k.sb("q2T", [128, 4, 8, 8], BF16)
        ks2T = k.sb("ks2T", [128, 4, 2, 8], BF16)
        kw2T = k.sb("kw2T", [128, 4, 2, 8], BF16)
        V2s = k.sb("V2s", [8, 4, 2, 130], BF16)
        V2w = k.sb("V2w", [8, 4, 2, 130], BF16)
        gates2 = k.sb("gates2", [8, 4, 24], F32)

        self.pers = dict(ksT_own=ksT_own, Vs_own=Vs_own, KwT_own=KwT_own, Vw_own=Vw_own, gates=gates, q2T=q2T,
                         ks2T=ks2T, kw2T=kw2T, V2s=V2s, V2w=V2w, gates2=gates2)
        self.dr = dict(mixT=mixT, qT_scr=qT_scr, sz_scr=sz_scr, xB=xB, xS=xS, o_y=o_y, o_y2=o_y2, o_kwin=o_kwin, o_vwin=o_vwin)
        with k.phase():
            k.dma(self.ident, ident_d)
            k.dma(self.identf, identf_d)
            k.dma(self.gcol, gcol_d)
            k.dma(self.gvec, gvec_d.re("p (a d) -> p a d", a=4))
            k.memset("dve", self.epsc, EPS)
            k.ts("dve", self.gvec[:, 0, :], self.gvec[:, 0, :], SCALE, None, ALU.mult)
            cv = k.sb("cv", [35, 1024], F32)
            k.dma(cv, cvec_d)
            for j in range(8):
                pb = self.ps.next()
                k.tr(pb[:, 0:35], cv[:, j * 128:(j + 1) * 128], self.identf[:35, :35])
                k.cp("dve", self.cvT[:, j, :], pb[:, 0:35])
            for t in (Vs_own, Vw_own, V2s, V2w):
                R = 8 if t in (V2s, V2w) else 128
                k.memset("dve", t[:R].re("p a g c -> p (a g) c")[:, :, 128:129], 1.0)
                k.memset("dve", t[:R].re("p a g c -> p (a g) c")[:, :, 129:130], 0.0)

        if self.stages < 1:
            k.flush(final=True)
            return

        with contextlib.ExitStack() as bst:
            old = k.stack
            k.stack = bst
            xTB = k.sb("xTB", [128, 16, TOKC], BF16)
            yT = k.sb("yT", [128, 8, NOWN], BF16)
            k.stack = old
            with k.phase():
                self.xt_ring = k.ring("xt", 2, [128, D], F32)
                self.xs_ring = k.ring("xs", 2, [128, D], BF16)
                self.st_ring = k.ring("st", 4, [128, 16], F32)
                self.wb_ring = k.ring("wb", 2, [128, 16, 512], BF16)
                self.wstg_ring = k.ring("wstg", 2, [128, 4, 512], F32)
                self.etmp = k.ring("etmp", 3, [128, 512], F32)
                uT_ring = k.ring("uT", 2, [128, 1280 + 4 * 38], F32)
                acc_ring = k.ring("acc", 2, [128, NOWN], F32)
                ptmp = k.sb("ptmp", [128, NOWN], F32)
                scT = k.sb("scT", [128, 8, 120], F32)
                uP = k.sb("uP", [128, 8, 30], F32)
                uS = k.sb("uS", [128, 8, 4, 30], F32)
                sc_in = k.sb("sc_in", [120, 1024], F32)
                k.dma(sc_in, sconv_d)
                for j in range(8):
                    pb = self.ps.next()
                    k.tr(pb[:, 0:120], sc_in[:, j * 128:(j + 1) * 128], self.identf[:120, :120])
                    k.cp("dve", scT[:, j, :], pb[:, 0:120])
                for s in range(8):
                    self.front(xB[s, 32:160, :], 128, xTB[:, :, s * 160 + 32:s * 160 + 160])
                    self.front(xB[s, 0:32, :], 32, xTB[:, :, s * 160:s * 160 + 32])
                self.front(xS[:, :], 32, xTB[:, :, 1280:1312])
                GROUPS = [(0, 480), (480, 480), (960, 352)]
                for half in range(2):
                    wa = self.wload(C_UA + 512 * half, 512)
                    wbk = self.wload(C_UB + 512 * half, 512)
                    for sub in range(4):
                        j = half * 4 + sub
                        uT = uT_ring.next()
                        for (c0, n) in GROUPS:
                            pa = self.ps.next()
                            pbb = self.ps.next()
                            for kc in range(16):
                                k.mm(pa[:, :n], wa[:, kc, sub * 128:(sub + 1) * 128], xTB[:, kc, c0:c0 + n],
                                     start=(kc == 0), stop=(kc == 15))
                            for kc in range(16):
                                k.mm(pbb[:, :n], wbk[:, kc, sub * 128:(sub + 1) * 128], xTB[:, kc, c0:c0 + n],
                                     start=(kc == 0), stop=(kc == 15))
                            e = self.etmp.next()
                            k.act(e[:, :n], pbb[:, :n], AF.Exp, scale=-1.0)
                            k.ts("dve", e[:, :n], e[:, :n], 1.0, None, ALU.add)
                            k.recip(e[:, :n], e[:, :n])
                            if c0 < 960:
                                k.tt("dve", uT[:, c0:c0 + n], pa[:, :n], e[:, :n], ALU.mult)
                            else:
                                k.tt("dve", uT[:, 960:1280], pa[:, :320], e[:, :320], ALU.mult)
                                k.tt("dve", uT[:, 1280:1432].re("p (b t) -> p b t", t=38)[:, :, 30:38],
                                     pa[:, 320:352].re("p (b t) -> p b t", t=8),
                                     e[:, 320:352].re("p (b t) -> p b t", t=8), ALU.mult)
                        eng = "pool" if (j % 4 == 3) else "dve"
                        k.cp(eng, uT[:, 1280:1432].re("p (b t) -> p b t", t=38)[:, :, 0:30],
                             scT[:, j, :].re("p (b t) -> p b t", t=30))
                        k.cp(eng, uP[:, j, :], uT[:, 7 * 160 + 130:7 * 160 + 160])
                        k.cp(eng, uS[:, j, :, :], uT[:, 1280:1432].re("p (b t) -> p b t", t=38)[:, :, 8:38])
                        acc = acc_ring.next()
                        a_own = acc[:, 0:1024].re("p (s t) -> p s t", t=128)
                        a_smp = acc[:, 1024:1056].re("p (b t) -> p b t", t=8)
                        u_own = uT[:, 0:1280].re("p (s t) -> p s t", t=160)
                        u_smp = uT[:, 1280:1432].re("p (b t) -> p b t", t=38)
                        wd = self.cvT[:, j, :]
                        k.ts(eng, a_own, u_own[:, :, 2:130], wd[:, 0:1], wd[:, 31:32], ALU.mult, ALU.add)
                        k.ts(eng, a_smp, u_smp[:, :, 0:8], wd[:, 0:1], wd[:, 31:32], ALU.mult, ALU.add)
                        for kk in range(1, 31):
                            if eng == "dve":
                                k.stt(eng, a_own, u_own[:, :, 2 + kk:130 + kk], wd[:, kk:kk + 1], a_own, ALU.mult, ALU.add)
                                k.stt(eng, a_smp, u_smp[:, :, kk:kk + 8], wd[:, kk:kk + 1], a_smp, ALU.mult, ALU.add)
                            else:
                                tp = ptmp[:, 0:1024].re("p (s t) -> p s t", t=128)
                                tq = ptmp[:, 1024:1056].re("p (b t) -> p b t", t=8)
                                k.ts(eng, tp, u_own[:, :, 2 + kk:130 + kk], wd[:, kk:kk + 1], 1.0, ALU.mult, ALU.mult)
                                k.tt(eng, a_own, a_own, tp, ALU.add)
                                k.ts(eng, tq, u_smp[:, :, kk:kk + 8], wd[:, kk:kk + 1], 1.0, ALU.mult, ALU.mult)
                                k.tt(eng, a_smp, a_smp, tq, ALU.add)
                        k.cp(eng, yT[:, j, :], acc)
                cps = k.sb("cps", [128, 1024], F32)
                for j in range(8):
                    pb = self.ps.next()
                    k.tr(pb[0:30, 0:128], uP[:, j, :], self.identf)
                    k.cp("dve", cps[0:30, j * 128:(j + 1) * 128], pb[0:30, 0:128])
                k.dma(o_convp, cps[0:30, :])
                cps2 = k.sb("cps2", [128, 1024], F32)
                for j in range(8):
                    pb = self.ps.next()
                    k.tr(pb[0:120, 0:128], uS[:, j, :, :].re("p b t -> p (b t)"), self.identf)
                    k.cp("dve", cps2[0:120, j * 128:(j + 1) * 128], pb[0:120, 0:128])
                k.dma(o_convs, cps2[0:120, :])

            if self.stages < 2:
                k.flush(final=True)
                return
            with k.phase():
                self.st_ring = k.ring("st", 4, [128, 16], F32)
                self.wb_ring = k.ring("wb", 2, [128, 16, 512], BF16)
                self.wstg_ring = k.ring("wstg", 2, [128, 4, 512], F32)
                self.etmp = k.ring("etmp", 4, [128, 512], F32)
                ones_b = k.sb("ones_b", [128, 128], BF16)
                k.memset("dve", ones_b, 1.0 / 1024)
                mean = k.sb("mean", [128, NOWN], F32)
                rstd = k.sb("rstd", [128, NOWN], F32)
                ysq_ring = k.ring("ysq", 2, [128, 512], BF16)
                LG = [(0, 512), (512, 512), (1024, 32)]
                for (c0, n) in LG:
                    pm = self.ps.next()
                    pq = self.ps.next()
                    for j in range(8):
                        k.mm(pm[:, :n], ones_b, yT[:, j, c0:c0 + n], start=(j == 0), stop=(j == 7))
                    for j in range(8):
                        ysq = ysq_ring.next()
                        k.tt("pool", ysq[:, :n], yT[:, j, c0:c0 + n], yT[:, j, c0:c0 + n], ALU.mult)
                        k.mm(pq[:, :n], ones_b, ysq[:, :n], start=(j == 0), stop=(j == 7))
                    k.cp("act", mean[:, c0:c0 + n], pm[:, :n])
                    e = self.etmp.next()
                    k.tt("dve", e[:, :n], mean[:, c0:c0 + n], mean[:, c0:c0 + n], ALU.mult)
                    k.tt("dve", e[:, :n], pq[:, :n], e[:, :n], ALU.subtract)
                    k.act(e[:, :n], e[:, :n], AF.Ln, bias=self.epsc[:, 0:1])
                    k.act(rstd[:, c0:c0 + n], e[:, :n], AF.Exp, scale=-0.5)
                for j in range(8):
                    for (c0, n) in LG:
                        t = self.etmp.next()
                        k.tt("dve", t[:, :n], yT[:, j, c0:c0 + n], mean[:, c0:c0 + n], ALU.subtract)
                        k.tt("dve", t[:, :n], t[:, :n], rstd[:, c0:c0 + n], ALU.mult)
                        k.ts("dve", t[:, :n], t[:, :n], self.cvT[:, j, 32:33], self.cvT[:, j, 33:34], ALU.mult, ALU.add)
                        self.silu_from("dve", yT[:, j, c0:c0 + n], t[:, :n], 128, n)
                Wp = k.sb("Wp", [128, 8, 1024], BF16)
                for hh in range(2):
                    for q4 in range(2):
                        stg = self.wstg_ring.next()
                        k.dma(stg, w_pw2[q4 * 512:(q4 + 1) * 512, hh * 512:(hh + 1) * 512].re("(a p) c -> p a c", p=128))
                        k.cp("pool", Wp[:, q4 * 4:(q4 + 1) * 4, hh * 512:(hh + 1) * 512], stg)
                szc_ring = k.ring("szc", 2, [128, NOWN], BF16)
                mixc_ring = k.ring("mixc", 2, [128, 512], BF16)
                GROUPS = [(0, 480), (480, 480), (960, 352)]
                for half in range(2):
                    wz = self.wload(C_ZC + 512 * half, 512)
                    for sub in range(4):
                        e_ = half * 4 + sub
                        szc = szc_ring.next()
                        for gi, (c0, n) in enumerate(GROUPS):
                            pz = self.ps.next()
                            for kc in range(16):
                                k.mm(pz[:, :n], wz[:, kc, sub * 128:(sub + 1) * 128], xTB[:, kc, c0:c0 + n],
                                     start=(kc == 0), stop=(kc == 15))
                            ns = 3 if gi < 2 else 2
                            zt = self.etmp.next()
                            self.silu_from("dve", zt[:, :n], pz[:, :n], 128, n)
                            k.cp("pool", szc[:, gi * 384:gi * 384 + ns * 128].re("p (s t) -> p s t", t=128),
                                 zt[:, :ns * 160].re("p (s t) -> p s t", t=160)[:, :, 32:160])
                            if gi == 2:
                                k.cp("pool", szc[:, 1024:1056], zt[:, 320:352])
                        for (c0, n) in LG:
                            po = self.ps.next()
                            for j in range(8):
                                k.mm(po[:, :n], Wp[:, j, e_ * 128:(e_ + 1) * 128], yT[:, j, c0:c0 + n],
                                     start=(j == 0), stop=(j == 7))
                            mc = mixc_ring.next()
                            k.stt("dve", mc[:, :n], po[:, :n], self.cvT[:, e_, 34:35], szc[:, c0:c0 + n], ALU.add, ALU.mult)
                            k.dma(mixT[e_ * 128:(e_ + 1) * 128, c0:c0 + n], mc[:, :n], q="pool")

                stg_ring = k.ring("ostg", 3, [128, 512], F32)
                bfo_ring = k.ring("bfo", 3, [128, 512], BF16)
                units = [(s, s * 160 + 32, 128) for s in range(8)] + [(8 + b, 1280 + 8 * b, 8) for b in range(4)]

                tm_list = []

                def tm_block(c0, wd, handler):
                    tm_list.append((c0, wd, handler))

                def tm_run(wb, wd, handler):
                    for (u, col, R) in units:
                        pb = self.ps.next()
                        for kc in range(16):
                            k.mm(pb[:R, :wd], xTB[:, kc, col:col + R], wb[:, kc, :wd], start=(kc == 0), stop=(kc == 15))
                        handler(u, R, pb)

                def to_T(src_bf, R, nh, dst):
                    pt = self.ps.next().cast(BF16)
                    for h in range(nh):
                        k.tr(pt[:, h * R:(h + 1) * R], src_bf[:R, h * 128:(h + 1) * 128], self.ident[:R, :R])
                    k.cp("act", dst, pt[:, 0:nh * R].re("p (h t) -> p h t", h=nh))

                for g in range(2):
                    def h_q(u, R, pb, g=g):
                        bf = bfo_ring.next()
                        self.normed(pb[:R, :512], R, 4, 0, out_bf=bf[:R, :])
                        if u < 8:
                            qsb = bfo_ring.next()
                            to_T(bf, R, 4, qsb[:, :].re("p (h t) -> p h t", h=4))
                            k.dma(qT_scr[u, :, g * 4:(g + 1) * 4, :], qsb[:, :].re("p (h t) -> p h t", h=4), q="act")
                        else:
                            to_T(bf, R, 4, q2T[:, u - 8, g * 4:(g + 1) * 4, :])
                    tm_block(C_Q + 512 * g, 512, h_q)

                def h_kcvc(u, R, pb):
                    stg = stg_ring.next()
                    k.cp("act", stg[:R, :], pb[:R, :512])
                    if u < 8:
                        k.dma(o_kc[u], stg[:, 0:256], q="act"); k.dma(o_vc[u], stg[:, 256:512], q="act")
                    else:
                        b = u - 8
                        k.dma(o_kc2[b * 8:(b + 1) * 8, :], stg[:8, 0:256]); k.dma(o_vc2[b * 8:(b + 1) * 8, :], stg[:8, 256:512])
                tm_block(C_KC, 512, h_kcvc)

                def mk_kv(gidx, oK, oV, oK2, oV2, KT_own, V_own, K2T, V2):
                    def h(u, R, pb):
                        stg = stg_ring.next()
                        bf = bfo_ring.next()
                        self.normed(pb[:R, 0:256], R, 2, gidx, out_f32=stg[:R, 0:256], out_bf=bf[:R, 0:256])
                        k.cp("act", stg[:R, 256:512], pb[:R, 256:512])
                        if u < 8:
                            k.dma(oK[u], stg[:, 0:256], q="pool"); k.dma(oV[u], stg[:, 256:512], q="act")
                            to_T(bf, R, 2, KT_own[:, u, :, :])
                            k.cp("pool", V_own[:, u, :, 0:128], stg[:, 256:512].re("p (g d) -> p g d", g=2))
                        else:
                            b = u - 8
                            if oK2 is not None:
                                k.dma(oK2[b * 8:(b + 1) * 8, :], stg[:8, 0:256]); k.dma(oV2[b * 8:(b + 1) * 8, :], stg[:8, 256:512])
                            else:
                                self.win_new_rows(b, stg)
                            to_T(bf, R, 2, K2T[:, b, :, :])
                            k.cp("pool", V2[:8, b, :, 0:128], stg[:8, 256:512].re("p (g d) -> p g d", g=2))
                    return h
                tm_block(C_KS, 512, mk_kv(1, o_ks, o_vs, o_ks2, o_vs2, ksT_own, Vs_own, ks2T, V2s))
                def _wnr(b, stg):
                    k.dma(o_kwin[b, 504:512, :], stg[:8, 0:256])
                    k.dma(o_vwin[b, 504:512, :], stg[:8, 256:512])
                self.win_new_rows = _wnr
                tm_block(C_KW, 512, mk_kv(2, o_kw, o_vw, None, None, KwT_own, Vw_own, kw2T, V2w))

                def h_gt(u, R, pb):
                    e = self.etmp.next()
                    k.act(e[:R, :24], pb[:R, :24], AF.Exp, scale=-1.0)
                    k.ts("dve", e[:R, :24], e[:R, :24], 1.0, None, ALU.add)
                    if u < 8:
                        k.recip(gates[:, u, :], e[:, :24])
                    else:
                        k.recip(gates2[:8, u - 8, :], e[:8, :24])
                tm_block(C_GT, 24, h_gt)

                for half in range(2):
                    def h_zn(u, R, pb, half=half):
                        bf = bfo_ring.next()
                        self.silu_from("dve", bf[:R, :], pb[:R, :512], R, 512)
                        r0 = u * 128 if u < 8 else 1024 + (u - 8) * 8
                        k.dma(sz_scr[r0:r0 + R, half * 512:(half + 1) * 512], bf[:R, :], q="pool")
                    tm_block(C_ZN + 512 * half, 512, h_zn)
                wb_next = self.wload(tm_list[0][0], tm_list[0][1])
                for i_, (c0_, wd_, hd_) in enumerate(tm_list):
                    wb_cur = wb_next
                    if i_ + 1 < len(tm_list):
                        wb_next = self.wload(tm_list[i_ + 1][0], tm_list[i_ + 1][1])
                    tm_run(wb_cur, wd_, hd_)

        if self.stages >= 3:
            self.build2()
        k.flush(final=True)


    def finish_branch(self, OB, oacc, s, g, b, rd_keep=None):
        k = self.k
        gates = self.pers["gates"]
        st = self.st_ring.next()
        for r in range(4):
            h = g * 4 + r
            k.ts("dve", st[:, r:r + 1], OB[r][:, 128:129], 1e-37, None, ALU.max)
        k.recip(st[:, 4:8], st[:, 0:4])
        if rd_keep is not None:
            k.cp("dve", rd_keep, st[:, 4:8])
        for r in range(4):
            h = g * 4 + r
            k.tt("dve", st[:, 8 + r:9 + r], st[:, 4 + r:5 + r], gates[:, s, h * 3 + b:h * 3 + b + 1], ALU.mult)
            oh = oacc[:, h * 128:(h + 1) * 128]
            if b == 0:
                k.ts("dve", oh, OB[r][:, 0:128], st[:, 8 + r:9 + r], None, ALU.mult)
            else:
                k.stt("dve", oh, OB[r][:, 0:128], st[:, 8 + r:9 + r], oh, ALU.mult, ALU.add)

    def build2(self):
        k = self.k
        inp = self.inp
        P = self.pers
        dr = self.dr
        mixT, qT_scr, sz_scr, xB, xS, o_y, o_y2 = (dr[n] for n in ("mixT", "qT_scr", "sz_scr", "xB", "xS", "o_y", "o_y2"))
        ksT_own, Vs_own, KwT_own, Vw_own = P["ksT_own"], P["Vs_own"], P["KwT_own"], P["Vw_own"]
        xA = inp("xA", [T, D])
        xH = inp("xH", [8, 4, 128, D])
        w_out = inp("w_out", [D, D])
        w1 = [inp("w_cmp_k1", [32, 128, 128]), inp("w_cmp_v1", [32, 128, 128])]
        w2 = [inp("w_cmp_k2", [128, 128]), inp("w_cmp_v2", [128, 128])]
        pe = [inp("pe_cmp_k", [32, 128]), inp("pe_cmp_v", [32, 128])]
        selbias_d = inp("selbias", [128, 288 * 8])
        dbias_d = inp("dbias", [128, 8])
        winbias_d = inp("winbias", [128, 8 * 5 * 8])
        cmpbias_d = inp("cmpbias", [128, 8 * 4 * 8])
        cmpmask_d = inp("cmpmask", [128, 8 * 4 * 128], BF16)
        forced_d = inp("forced", [128, 8 * 128])
        notown_d = inp("notown", [128, 8 * 128])
        ctab_d = inp("ctab", [128, 4 * 128], BF16)
        i4_d = inp("i4", [128, 512], BF16)
        tri_d = inp("tri", [128, 2 * 128], BF16)
        kcT_scr = self.scr("kcT_scr", [4, 128, T], BF16)
        kwT_scr = self.scr("kwT_scr", [8, 128, 2, 512], BF16)
        vw_scr = self.scr("vw_scr", [8, 128, 4, 2, 130], BF16)

        with contextlib.ExitStack() as cst:
            old = k.stack
            k.stack = cst
            KsT = k.sb("KsT", [128, 2, T], BF16)
            Vs = k.sb("Vs", [128, 64, 2, 130], BF16)
            KcT = k.sb("KcT", [128, 2, 512], BF16)
            Vc = k.sb("Vc", [128, 4, 2, 130], BF16)
            k.stack = old
            with k.phase():
                self.xt_ring = k.ring("xt", 3, [128, D], F32)
                self.xs_ring = k.ring("xs", 2, [128, D], BF16)
                self.st_ring = k.ring("st", 4, [128, 16], F32)
                self.wb_ring = k.ring("wb", 3, [128, 16, 512], BF16)
                self.wstg_ring = k.ring("wstg", 1, [128, 4, 512], F32)
                self.etmp = k.ring("etmp", 3, [128, 512], F32)
                for t_ in (Vs, Vc):
                    k.memset("dve", t_.re("p a g c -> p (a g) c")[:, :, 128:129], 1.0)
                    k.memset("dve", t_.re("p a g c -> p (a g) c")[:, :, 129:130], 0.0)
                W0 = self.wload(C_KC, 512)
                W1 = self.wload(C_KS, 512)
                Ww = self.wload(C_KW, 512)
                xT_ring = k.ring("xT", 2, [128, 16, 128], BF16)
                kcsb_ring = k.ring("kcsb", 2, [128, 4, 128], BF16)
                pab_ring = k.ring("pab", 2, [128, 512], BF16)
                bf_ring = k.ring("bfa", 2, [128, 256], BF16)
                kwsb_ring = k.ring("kwsb", 2, [128, 2, 128], BF16)
                vwsb_ring = k.ring("vwsb", 2, [128, 2, 130], BF16)
                for v_ in vwsb_ring.items:
                    k.memset("dve", v_[:, :, 128:129], 1.0)
                    k.memset("dve", v_[:, :, 129:130], 0.0)
                nA = 64 if self.stages >= 4 else 2
                jobs = []

                def mkA(t):
                    box = {}

                    def sL():
                        box["xt"] = self.front_load(xA[t * 128:(t + 1) * 128, :], 128)

                    def s0():
                        box["xT"] = xT_ring.next()
                        self.front_rest(box["xt"], 128, box["xT"])

                    def s1():
                        xT = box["xT"]
                        pa = self.ps.next()
                        for kc in range(16):
                            k.mm(pa, xT[:, kc, :], W0[:, kc, :], start=(kc == 0), stop=(kc == 15))
                        pab = pab_ring.next()
                        k.cp("act", pab, pa)
                        pt2 = self.ps.next().cast(BF16)
                        for cb in range(4):
                            k.tr(pt2[:, cb * 128:(cb + 1) * 128], pab[:, cb * 128:(cb + 1) * 128], self.ident)
                        kcsb = kcsb_ring.next()
                        k.cp("dve", kcsb, pt2[:, 0:512].re("p (a t) -> p a t", a=4))
                        k.dma(kcT_scr[:, :, t * 128:(t + 1) * 128].re("a p t -> p a t"), kcsb, q="pool")
                        pb = self.ps.next()
                        for kc in range(16):
                            k.mm(pb, xT[:, kc, :], W1[:, kc, :], start=(kc == 0), stop=(kc == 15))
                        bf = bf_ring.next()
                        self.normed(pb[:, 0:256], 128, 2, 1, out_bf=bf)
                        pt = self.ps.next().cast(BF16)
                        for g in range(2):
                            k.tr(pt[:, g * 128:(g + 1) * 128], bf[:, g * 128:(g + 1) * 128], self.ident)
                        k.cp("act", KsT[:, :, t * 128:(t + 1) * 128], pt[:, 0:256].re("p (g t) -> p g t", g=2))
                        k.cp("dve", Vs[:, t, :, 0:128], pb[:, 256:512].re("p (g d) -> p g d", g=2))
                    return [sL, s0, s1]

                def mkH(s_, o):
                    box = {}

                    def sL():
                        box["xt"] = self.front_load(xH[s_, o], 128)

                    def s0():
                        box["xT"] = xT_ring.next()
                        self.front_rest(box["xt"], 128, box["xT"])

                    def s1():
                        xT = box["xT"]
                        pb = self.ps.next()
                        for kc in range(16):
                            k.mm(pb, xT[:, kc, :], Ww[:, kc, :], start=(kc == 0), stop=(kc == 15))
                        bf = bf_ring.next()
                        self.normed(pb[:, 0:256], 128, 2, 2, out_bf=bf)
                        pt = self.ps.next().cast(BF16)
                        for g in range(2):
                            k.tr(pt[:, g * 128:(g + 1) * 128], bf[:, g * 128:(g + 1) * 128], self.ident)
                        kwsb = kwsb_ring.next()
                        k.cp("act", kwsb, pt[:, 0:256].re("p (g t) -> p g t", g=2))
                        k.dma(kwT_scr[s_, :, :, o * 128:(o + 1) * 128], kwsb, q="act")
                        vwsb = vwsb_ring.next()
                        k.cp("dve", vwsb[:, :, 0:128], pb[:, 256:512].re("p (g d) -> p g d", g=2))
                        k.dma(vw_scr[s_, :, o, :, :], vwsb, q="pool")
                    return [sL, s0, s1]
                for t in range(nA):
                    jobs.append(mkA(t))
                for s_ in range(8):
                    for o in range(4):
                        jobs.append(mkH(s_, o))
                run_pipeline(jobs)

            with k.phase():
                self.st_ring = k.ring("st", 4, [128, 16], F32)
                self.etmp = k.ring("etmp", 4, [128, 512], F32)
                stgW = k.sb("stgW", [128, 32, 128], F32)
                W1b = [k.sb("W1k", [128, 32, 128], BF16), k.sb("W1v", [128, 32, 128], BF16)]
                W2b = [k.sb("W2k", [128, 128], BF16), k.sb("W2v", [128, 128], BF16)]
                peT = k.sb("peT", [128, 2, 32], BF16)
                biasv = k.sb("biasv", [128, 2], F32)
                gkc = k.sb("gkc", [128, 1], F32)
                ones_m = k.sb("ones_m", [128, 128], BF16)
                k.memset("dve", ones_m, 1.0 / 128)
                for kv in range(2):
                    k.dma(stgW, w1[kv].re("c d e -> d c e"))
                    k.cp("pool", W1b[kv], stgW)
                    stg2 = self.etmp.next()
                    k.dma(stg2[:, 0:128], w2[kv])
                    k.cp("pool", W2b[kv], stg2[:, 0:128])
                    pes = self.etmp.next()
                    k.dma(pes[0:32, 0:128], pe[kv])
                    pt = self.ps.next()
                    k.tr(pt[:, 0:32], pes[0:32, 0:128], self.identf[:32, :32])
                    k.cp("dve", peT[:, kv, :], pt[:, 0:32])
                    pbias = self.ps.next()
                    for c in range(32):
                        k.mm(pbias[:, 0:2], W1b[kv][:, c, :], peT[:, kv, c:c + 1].bc([128, 2]), start=(c == 0), stop=(c == 31))
                    k.cp("dve", biasv[:, kv:kv + 1], pbias[:, 0:1])
                pt = self.ps.next()
                k.tr(pt[:, 0:128], self.gvec[:, 3, :], self.identf)
                k.cp("dve", gkc, pt[:, 0:1])
                self.cmp_w = (W1b, W2b, biasv, gkc, ones_m)
                kc_ring = k.ring("kcl", 2, [128, T], BF16)
                fb_sb = k.sb("fb_sb", [128, 512], F32)
                hid_ring = k.ring("hid", 2, [128, 512], BF16)
                sq_ring = k.ring("sq", 2, [128, 512], BF16)
                for kv in range(2):
                    for g in range(2):
                        kcl = kc_ring.next()
                        k.dma(kcl, kcT_scr[kv * 2 + g])
                        kview = kcl.re("p (n c) -> p n c", c=16)
                        pfa = self.ps.next()
                        pfb = self.ps.next()
                        for c in range(16):
                            k.mm(pfa, W1b[kv][:, c, :], kview[:, :, c], start=(c == 0), stop=(c == 15))
                        for c in range(16):
                            k.mm(pfb, W1b[kv][:, 16 + c, :], kview[:, :, c], start=(c == 0), stop=(c == 15))
                        k.cp("act", fb_sb, pfb)
                        t_ = self.etmp.next()
                        k.tt("dve", t_[:, 0:511], pfa[:, 0:511], fb_sb[:, 1:512], ALU.add)
                        h = hid_ring.next()
                        k.memset("dve", h[:, 511:512], 0.0)
                        self.silu_from("dve", h[:, 0:511], t_[:, 0:511], 128, 511, bias=biasv[:, kv:kv + 1])
                        if kv == 0:
                            pk = self.ps.next()
                            k.mm(pk, W2b[0], h)
                            sq = sq_ring.next()
                            k.act(sq, pk, AF.Square)
                            pss = self.ps.next()
                            k.mm(pss, ones_m, sq)
                            e = self.etmp.next()
                            k.act(e, pss, AF.Ln, bias=self.epsc[:, 0:1])
                            k.act(e, e, AF.Exp, scale=-0.5)
                            k.tt("dve", e, pk, e, ALU.mult)
                            k.ts("dve", KcT[:, g, :], e, gkc[:, 0:1], None, ALU.mult)
                        else:
                            pv = self.ps.next()
                            for nt in range(4):
                                k.mm(pv[:, nt * 128:(nt + 1) * 128], h[:, nt * 128:(nt + 1) * 128], W2b[1])
                            k.cp("dve", Vc[:, :, g, 0:128], pv.re("p (a d) -> p a d", a=4))

            with k.phase():
                self.st_ring = k.ring("st", 6, [128, 16], F32)
                self.etmp = k.ring("etmp", 3, [128, 512], F32)

                def tab(name, d, shape, dt=F32, pat=None, **kw):
                    t_ = k.sb(name, shape, dt)
                    k.dma(t_, d.re(pat, **kw) if pat else d)
                    return t_
                selbias = tab("selbias", selbias_d, [128, 288, 8], pat="p (a h) -> p a h", h=8)
                dbias = tab("dbias", dbias_d, [128, 8])
                winbias = tab("winbias", winbias_d, [128, 8, 5, 8], pat="p (s o h) -> p s o h", s=8, o=5)
                cmpbias = tab("cmpbias", cmpbias_d, [128, 8, 4, 8], pat="p (s o h) -> p s o h", s=8, o=4)
                cmpmask = tab("cmpmask", cmpmask_d, [128, 8, 4, 128], BF16, pat="p (s o q) -> p s o q", s=8, o=4)
                forced = tab("forced", forced_d, [128, 8, 128], pat="p (s j) -> p s j", s=8)
                notown = tab("notown", notown_d, [128, 8, 128], pat="p (s j) -> p s j", s=8)
                ctab = tab("ctab", ctab_d, [128, 4, 128], BF16, pat="p (a j) -> p a j", a=4)
                i4 = tab("i4", i4_d, [128, 512], BF16)
                tri = tab("tri", tri_d, [128, 2, 128], BF16, pat="p (a q) -> p a q", a=2)
                QT_ring = k.ring("QT", 2, [128, 8, 128], BF16)
                sz_ring = k.ring("sz", 2, [128, 1024], BF16)
                KwH_ring = k.ring("KwH", 2, [128, 2, 512], BF16)
                VwH_ring = k.ring("VwH", 2, [128, 4, 2, 130], BF16)
                ET_items = []
                for i_ in range(4):
                    base = k.sb("ET%d" % i_, [128, 512], BF16)
                    toks = [Tok() for _ in range(4)]
                    ET_items.append((V(base.ap, toks), [V(base.ap[:, r * 128:(r + 1) * 128], [toks[r]]) for r in range(4)]))
                ET_ring = Ring(ET_items)
                ETc_ring = k.ring("ETc", 2, [128, 4, 512], BF16)
                oacc_ring = k.ring("oacc", 2, [128, 1024], F32)
                mexp_ring = k.ring("mexp", 2, [128, 1024], BF16)
                sc_ring = k.ring("sc", 4, [128, 128], F32)
                m8_ring = k.ring("m8", 2, [128, 16], F32)
                rdc_ring = k.ring("rdc", 2, [128, 4], F32)
                mix_ring = k.ring("mix", 2, [128, 1024], BF16)
                mixa_ring = k.ring("mixa", 2, [128, 8, 128], BF16)
                OB = self.ps.items[0:4]
                STr = Ring(self.ps.items[4:6])
                MS = Ring(self.ps.items[6:8])
                offs = [sum(8 * s2 + 8 for s2 in range(s)) for s in range(8)]
                nS = 8 if self.stages >= 4 else 1
                for s in range(nS):
                    QT = QT_ring.next()
                    k.dma(QT, qT_scr[s])
                    sz = sz_ring.next()
                    k.dma(sz, sz_scr[s * 128:(s + 1) * 128, :])
                    KwH = KwH_ring.next()
                    k.dma(KwH, kwT_scr[s])
                    VwH = VwH_ring.next()
                    k.dma(VwH, vw_scr[s])
                    oacc = oacc_ring.next()
                    for g in range(2):
                        Qg = QT[:, g * 4:(g + 1) * 4, :].re("p h t -> p (h t)")
                        ETc0 = ETc_ring.next()
                        ctoks = [Tok() for _ in range(4)]
                        ETc = V(ETc0.ap, ctoks + ETc0.toks)
                        jobs = []

                        def mk_c(nt, g=g, Qg=Qg, ETc0=ETc0, ctoks=ctoks):
                            box = {}

                            def s0():
                                box["st"] = STr.next()
                                k.mm(box["st"], KcT[:, g, nt * 128:(nt + 1) * 128], Qg)

                            def s1():
                                ev = V(ETc0.ap[:, nt, :], [ctoks[nt]])
                                for r in range(4):
                                    h = g * 4 + r
                                    k.act(ev[:, r * 128:(r + 1) * 128], box["st"][:, r * 128:(r + 1) * 128], AF.Exp,
                                          bias=cmpbias[:, s, nt, h:h + 1])
                                e3 = ev.re("p (r q) -> p r q", r=4)
                                k.tt("pool", e3, e3, cmpmask[:, s, nt, :].us(1).bc([128, 4, 128]), ALU.mult)
                            return [s0, s1]
                        for nt in range(4):
                            jobs.append(mk_c(nt))
                        run_pipeline(jobs)
                        for r in range(4):
                            for nt in range(4):
                                k.mm(OB[r][:, 0:129], ETc[:, nt, r * 128:(r + 1) * 128], Vc[:, nt, g, 0:129],
                                     start=(nt == 0), stop=(nt == 3))
                        imp = MS.next()
                        for r in range(4):
                            for nt in range(4):
                                k.mm(imp[:, r * 128:(r + 1) * 128], ETc[:, nt, r * 128:(r + 1) * 128], ctab[:, nt, :],
                                     start=(nt == 0), stop=(nt == 3))
                        rdc = rdc_ring.next()
                        self.finish_branch(OB, oacc, s, g, 0, rd_keep=rdc)
                        sc = sc_ring.next()
                        sc2 = sc_ring.next()
                        m8 = m8_ring.next()
                        k.ts("dve", sc, imp[:, 0:128], rdc[:, 0:1], None, ALU.mult)
                        for r in range(1, 4):
                            k.stt("dve", sc, imp[:, r * 128:(r + 1) * 128], rdc[:, r:r + 1], sc, ALU.mult, ALU.add)
                        k.tt("dve", sc, sc, forced[:, s, :], ALU.add)
                        k.max8(m8[:, 0:8], sc)
                        k.mrep(sc2, m8[:, 0:8], sc, -3.0e38)
                        k.max8(m8[:, 8:16], sc2)
                        k.ts("dve", sc2, sc, m8[:, 15:16], None, ALU.is_ge)
                        k.tt("dve", sc2, sc2, notown[:, s, :], ALU.mult)
                        k.ts("dve", sc2, sc2, 30000.0, -30000.0, ALU.mult, ALU.add)
                        nT = 8 * s + 8
                        cur = {}

                        def mk_tile(lhsK, t_mask, biasfn, trimask, Vt, start, stop, g=g, Qg=Qg, cur=cur):
                            box = {}

                            def s0():
                                if t_mask is not None and t_mask % 8 == 0:
                                    cur["mexp"] = mexp_ring.next()
                                    c8 = t_mask // 8
                                    k.cp("pool", cur["mexp"].re("p (j e) -> p j e", e=64),
                                         sc2[:, c8 * 16:(c8 + 1) * 16].us(2).bc([128, 16, 64]))
                                st_ = STr.next()
                                box["st"] = st_
                                k.mm(st_, lhsK, Qg, start=True, stop=(t_mask is None))
                                if t_mask is not None:
                                    k.mm(st_, cur["mexp"][:, (t_mask % 8) * 128:(t_mask % 8 + 1) * 128], i4, start=False, stop=True)

                            def s1():
                                whole, parts = ET_ring.next()
                                box["ET"] = parts
                                for r in range(4):
                                    k.act(parts[r], box["st"][:, r * 128:(r + 1) * 128], AF.Exp, bias=biasfn(g * 4 + r))
                                if trimask is not None:
                                    e3 = whole.re("p (r q) -> p r q", r=4)
                                    k.tt("pool", e3, e3, tri[:, trimask, :].us(1).bc([128, 4, 128]), ALU.mult)

                            def s2():
                                for r in range(4):
                                    k.mm(OB[r][:, 0:129], box["ET"][r], Vt, start=start, stop=stop)
                            return [s0, s1, s2]

                        jobs = []
                        for t in range(nT):
                            jobs.append(mk_tile(KsT[:, g, t * 128:(t + 1) * 128], t,
                                                (lambda h, t=t: selbias[:, offs[s] + t, h:h + 1]), None,
                                                Vs[:, t, g, 0:129], (t == 0), False))
                        jobs.append(mk_tile(ksT_own[:, s, g, :], None, (lambda h: dbias[:, h:h + 1]), 0,
                                            Vs_own[:, s, g, 0:129], False, True))
                        run_pipeline(jobs)
                        self.finish_branch(OB, oacc, s, g, 1)
                        jobs = []
                        for o in range(5):
                            lhs = KwH[:, g, o * 128:(o + 1) * 128] if o < 4 else KwT_own[:, s, g, :]
                            Vt = VwH[:, o, g, 0:129] if o < 4 else Vw_own[:, s, g, 0:129]
                            jobs.append(mk_tile(lhs, None, (lambda h, o=o: winbias[:, s, o, h:h + 1]),
                                                (1 if o == 0 else (0 if o == 4 else None)), Vt, (o == 0), (o == 4)))
                        run_pipeline(jobs)
                        self.finish_branch(OB, oacc, s, g, 2)
                    mix = mix_ring.next()
                    k.tt("dve", mix, oacc, sz, ALU.mult)
                    mixa = mixa_ring.next()
                    for half in range(2):
                        pt = MS.next().cast(BF16)
                        for e4 in range(4):
                            e_ = half * 4 + e4
                            k.tr(pt[:, e4 * 128:(e4 + 1) * 128], mix[:, e_ * 128:(e_ + 1) * 128], self.ident)
                        k.cp("act", mixa[:, half * 4:(half + 1) * 4, :], pt[:, 0:512].re("p (a t) -> p a t", a=4))
                    k.dma(mixT[1024:2048, s * 128:(s + 1) * 128].re("(a p) t -> p a t", p=128), mixa, q="act")

        if self.stages >= 6:
            self.build3()

        with k.phase():
            self.wb_ring = k.ring("wb", 2, [128, 16, 512], BF16)
            self.wstg_ring = k.ring("wstg", 2, [128, 4, 512], F32)
            mixsb = k.sb("mixsb", [128, 16, NOWN], BF16)
            for e in range(16):
                k.dma(mixsb[:, e, :], mixT[e * 128:(e + 1) * 128, :])
            xr_ring = k.ring("xr", 3, [128, 512], F32)
            yo_ring = k.ring("yo", 3, [128, 512], F32)
            Wo_next = self.wload(0, 512, scale_g=False, src=w_out)
            for ct in range(4):
                Wo = Wo_next
                if ct + 1 < 4:
                    Wo_next = self.wload((ct + 1) * 512, 512, scale_g=False, src=w_out)
                for u in range(9):
                    R = 128 if u < 8 else 32
                    col = u * 128
                    pb = self.ps.next()
                    for e in range(16):
                        k.mm(pb[:R, :], mixsb[:, e, col:col + R], Wo[:, e, :], start=(e == 0), stop=(e == 15))
                    xr = xr_ring.next()
                    src = xB[u, 32:160, ct * 512:(ct + 1) * 512] if u < 8 else xS[:, ct * 512:(ct + 1) * 512]
                    k.dma(xr[:R], src)
                    yo = yo_ring.next()
                    k.tt("dve", yo[:R], pb[:R, :], xr[:R], ALU.add)
                    dst = o_y[u][:, ct * 512:(ct + 1) * 512] if u < 8 else o_y2[:, ct * 512:(ct + 1) * 512]
                    k.dma(dst, yo[:R], q="act")


    def build3(self):
        k = self.k
        inp = self.inp
        P = self.pers
        dr = self.dr
        mixT, sz_scr = dr["mixT"], dr["sz_scr"]
        q2T, ks2T, kw2T, V2s, V2w, gates2 = (P[n] for n in ("q2T", "ks2T", "kw2T", "V2s", "V2w", "gates2"))
        pt_d = inp("ptab", [4, 128], I32)
        caches = [inp(n, [NPHYS * 8, 4096]) for n in ("cache_k_cmp", "cache_v_cmp", "cache_k_slc", "cache_v_slc")]
        skw_d = inp("skw", [4, 512, 256])
        svw_d = inp("svw", [4, 512, 256])
        o_kwin, o_vwin = dr["o_kwin"], dr["o_vwin"]
        w1 = [self.din["w_cmp_k1"], self.din["w_cmp_v1"]]
        w2 = [self.din["w_cmp_k2"], self.din["w_cmp_v2"]]
        pe = [self.din["pe_cmp_k"], self.din["pe_cmp_v"]]
        marr_d = inp("marr", [128, 8])
        c2bias_d = inp("c2bias", [128, 64])
        c2tab_d = inp("c2tab", [128, 8 * 256], BF16)
        forced2_d = inp("forced2", [8, 256])
        abias_d = inp("abias", [128, 128 * 8])
        swbias_d = inp("swbias", [128, 32])
        swmask_d = inp("swmask", [128, 32], BF16)
        tbias_d = inp("tbias", [8, 8])
        tmask_d = inp("tmask", [8, 8], BF16)
        i48_d = inp("i48", [8, 32], BF16)
        selr_d = inp("selr", [32, 32])
        selsum_d = inp("selsum", [32, 8])

        with k.phase():
            self.st_ring = k.ring("st", 6, [128, 16], F32)
            self.etmp = k.ring("etmp", 4, [128, 512], F32)

            def tab(name, d, shape, dt=F32, pat=None, **kw):
                t_ = k.sb(name, shape, dt)
                k.dma(t_, d.re(pat, **kw) if pat else d)
                return t_
            marr = tab("marr", marr_d, [128, 8])
            c2bias = tab("c2bias", c2bias_d, [128, 8, 8], pat="p (m h) -> p m h", m=8)
            c2tab = tab("c2tab", c2tab_d, [128, 8, 256], BF16, pat="p (m j) -> p m j", m=8)
            forced2 = tab("forced2", forced2_d, [8, 256])
            abias = tab("abias", abias_d, [128, 128, 8], pat="p (r h) -> p r h", h=8)
            swbias = tab("swbias", swbias_d, [128, 4, 8], pat="p (w h) -> p w h", h=8)
            swmask = tab("swmask", swmask_d, [128, 4, 8], BF16, pat="p (w i) -> p w i", i=8)
            tbias = tab("tbias", tbias_d, [8, 8])
            tmask = tab("tmask", tmask_d, [8, 8], BF16)
            i48 = tab("i48", i48_d, [8, 32], BF16)
            selr = tab("selr", selr_d, [32, 4, 8], pat="p (r i) -> p r i", r=4)
            selsum = tab("selsum", selsum_d, [32, 8])
            X_ring = k.ring("X", 3, [128, 16, 2, 128], F32)
            W1b = [k.sb("W1k", [128, 32, 128], BF16), k.sb("W1v", [128, 32, 128], BF16)]
            W2b = [k.sb("W2k", [128, 128], BF16), k.sb("W2v", [128, 128], BF16)]
            peT = k.sb("peT", [128, 2, 32], BF16)
            biasv = k.sb("biasv", [128, 2], F32)
            gkc = k.sb("gkc", [128, 1], F32)
            ones_m = k.sb("ones_m", [128, 128], BF16)
            k.memset("dve", ones_m, 1.0 / 128)
            for kv in range(2):
                stgW = X_ring.next().re("p a g d -> p (a g) d")
                k.dma(stgW, w1[kv].re("c d e -> d c e"))
                k.cp("pool", W1b[kv], stgW)
                stg2 = self.etmp.next()
                k.dma(stg2[:, 0:128], w2[kv])
                k.cp("pool", W2b[kv], stg2[:, 0:128])
                pes = self.etmp.next()
                k.dma(pes[0:32, 0:128], pe[kv])
                pt = self.ps.next()
                k.tr(pt[:, 0:32], pes[0:32, 0:128], self.identf[:32, :32])
                k.cp("dve", peT[:, kv, :], pt[:, 0:32])
                pbias = self.ps.next()
                for c in range(32):
                    k.mm(pbias[:, 0:2], W1b[kv][:, c, :], peT[:, kv, c:c + 1].bc([128, 2]), start=(c == 0), stop=(c == 31))
                k.cp("dve", biasv[:, kv:kv + 1], pbias[:, 0:1])
            pt = self.ps.next()
            k.tr(pt[:, 0:128], self.gvec[:, 3, :], self.identf)
            k.cp("dve", gkc, pt[:, 0:1])
            pti = k.sb("pti", [128, 4], I32)
            for bl in range(4):
                k.dma(pti[:, bl:bl + 1], pt_d[bl].re("(p o) -> p o", o=1))
            ptf = k.sb("ptf", [128, 4], F32)
            k.cp("dve", ptf, pti)
            k.ts("dve", ptf, ptf, 8.0, None, ALU.mult)
            idxf = k.sb("idxf", [128, 4, 8], F32)
            idx8 = k.sb("idx8", [128, 4, 8], I32)
            for bl in range(4):
                k.ts("dve", idxf[:, bl, :], marr, ptf[:, bl:bl + 1], None, ALU.add)
            k.cp("dve", idx8, idxf)

            XT_ring = k.ring("XT", 2, [128, 16, 2, 128], BF16)
            Vb_ring = k.ring("Vb", 2, [128, 16, 2, 130], BF16)
            for v_ in Vb_ring.items:
                k.memset("dve", v_.re("p a g c -> p (a g) c")[:, :, 128:129], 1.0)
                k.memset("dve", v_.re("p a g c -> p (a g) c")[:, :, 129:130], 0.0)
            Fsb = k.sb("Fsb", [128, 2, 2, 8, 128], F32)
            hid_ring = k.ring("hid", 2, [128, 1024], BF16)
            sq_ring = k.ring("sq", 2, [128, 512], BF16)
            Kc2T = k.sb("Kc2T", [128, 2, 1024], BF16)
            Vc2 = k.sb("Vc2", [128, 8, 2, 130], BF16)
            k.memset("dve", Vc2.re("p a g c -> p (a g) c")[:, :, 128:129], 1.0)
            k.memset("dve", Vc2.re("p a g c -> p (a g) c")[:, :, 129:130], 0.0)
            stsb_ring = k.ring("stsb", 2, [128, 512], F32)
            ET_ring = k.ring("ET2", 2, [128, 512], BF16)
            o2_ring = k.ring("o2acc", 1, [8, 1024], F32)
            on_ring = k.ring("onorm", 2, [32, 128], F32)
            impn = k.sb("impn", [32, 256], F32)
            sc_ring = k.ring("sc2", 3, [8, 264], F32)
            m8 = k.sb("m8b", [8, 16], F32)
            Mb2h = k.sb("Mb2h", [8, 2, 2, 128], BF16)
            skw = k.sb("skw", [128, 4, 2, 128], F32)
            svw = k.sb("svw", [128, 4, 2, 128], F32)
            Vwb = k.sb("Vwb", [128, 4, 2, 130], BF16)
            k.memset("dve", Vwb.re("p a g c -> p (a g) c")[:, :, 128:129], 1.0)
            k.memset("dve", Vwb.re("p a g c -> p (a g) c")[:, :, 129:130], 0.0)
            KwT2 = k.sb("KwT2", [128, 2, 4, 128], BF16)
            sz2_ring = k.ring("sz2", 1, [8, 1024], BF16)
            mix2_ring = k.ring("mix2", 1, [8, 1024], BF16)
            mix2a_ring = k.ring("mix2a", 1, [128, 8, 8], BF16)
            OC = self.ps.items[0]
            OG = self.ps.items[1:3]
            STr = Ring(self.ps.items[3:5])
            MS = Ring(self.ps.items[5:8])

            def q2g(bl, g):
                return q2T[:, bl, g * 4:(g + 1) * 4, :].re("p h i -> p (h i)")

            def finish2(Og, o2acc, bl, g, b, keep=None):
                st = self.st_ring.next()
                k.ts("dve", st[:32, 0:1], Og[:32, 128:129], 1e-37, None, ALU.max)
                k.recip(st[:32, 1:2], st[:32, 0:1])
                if keep is not None:
                    k.cp("dve", keep, st[:32, 1:2])
                on = on_ring.next()
                k.ts("dve", on, Og[:32, 0:128], st[:32, 1:2], None, ALU.mult)
                pr = MS.next()
                for r in range(4):
                    k.mm(pr[:8, r * 128:(r + 1) * 128], selr[:, r, :], on)
                for r in range(4):
                    h = g * 4 + r
                    oh = o2acc[:, h * 128:(h + 1) * 128]
                    gt = gates2[:8, bl, h * 3 + b:h * 3 + b + 1]
                    if b == 0:
                        k.ts("dve", oh, pr[:8, r * 128:(r + 1) * 128], gt, None, ALU.mult)
                    else:
                        k.stt("dve", oh, pr[:8, r * 128:(r + 1) * 128], gt, oh, ALU.mult, ALU.add)

            def transposes_to(Xv, dstT, evac_i, gs=(0, 1)):
                n = 0
                for g in gs:
                    for r4 in range(4):
                        pb = MS.next()
                        for a in range(4):
                            k.tr(pb[:, a * 128:(a + 1) * 128], Xv[:, r4 * 4 + a, g, :], self.identf)
                        k.cp("act" if (n + evac_i) % 2 == 0 else "dve", dstT[:, r4 * 4:(r4 + 1) * 4, g, :],
                             pb.re("p (a q) -> p a q", a=4))
                        n += 1

            for bl in range(4):
                def post_cmp(kv):
                    for g in range(2):
                        h = hid_ring.next()
                        for half in range(2):
                            t_ = self.etmp.next()
                            if half == 0:
                                k.tt("dve", t_.re("p (m q) -> p m q", m=4), Fsb[:, 0, g, 0:4, :], Fsb[:, 1, g, 1:5, :], ALU.add)
                            else:
                                k.tt("dve", t_[:, 0:384].re("p (m q) -> p m q", m=3), Fsb[:, 0, g, 4:7, :], Fsb[:, 1, g, 5:8, :], ALU.add)
                                k.tt("dve", t_[:, 384:511], Fsb[:, 0, g, 7, 0:127], Fsb[:, 1, g, 0, 1:128], ALU.add)
                                k.memset("dve", t_[:, 511:512], 0.0)
                            self.silu_from("dve", h[:, half * 512:(half + 1) * 512], t_, 128, 512, bias=biasv[:, kv:kv + 1])
                        k.memset("dve", h[:, 1023:1024], 0.0)
                        if kv == 0:
                            for half in range(2):
                                hs = h[:, half * 512:(half + 1) * 512]
                                pk = MS.next()
                                k.mm(pk, W2b[0], hs)
                                sq = sq_ring.next()
                                k.act(sq, pk, AF.Square)
                                pss = MS.next()
                                k.mm(pss, ones_m, sq)
                                e = self.etmp.next()
                                k.act(e, pss, AF.Ln, bias=self.epsc[:, 0:1])
                                k.act(e, e, AF.Exp, scale=-0.5)
                                k.tt("dve", e, pk, e, ALU.mult)
                                k.ts("dve", Kc2T[:, g, half * 512:(half + 1) * 512], e, gkc[:, 0:1], None, ALU.mult)
                        else:
                            for half in range(2):
                                pv = MS.next()
                                for a in range(4):
                                    m = half * 4 + a
                                    k.mm(pv[:, a * 128:(a + 1) * 128], h[:, m * 128:(m + 1) * 128], W2b[1])
                                k.cp("dve", Vc2[:, half * 4:(half + 1) * 4, g, 0:128], pv.re("p (a d) -> p a d", a=4))

                def mk_cmp(kv, m, bl=bl):
                    box = {}

                    def sG():
                        box["X"] = X_ring.next()
                        k.gather(box["X"].re("p a g d -> p (a g d)"), caches[kv], idx8[:, bl, m:m + 1])

                    def s0():
                        box["XT"] = XT_ring.next()
                        transposes_to(box["X"], box["XT"], m)

                    def s1():
                        XT = box["XT"]
                        pf = STr.next()
                        for ab in range(2):
                            for g in range(2):
                                for c in range(16):
                                    k.mm(pf[:, (ab * 2 + g) * 128:(ab * 2 + g + 1) * 128], W1b[kv][:, ab * 16 + c, :],
                                         XT[:, c, g, :], start=(c == 0), stop=(c == 15))
                        k.cp("dve" if m % 2 == 0 else "act", Fsb[:, :, :, m, :], pf.re("p (a g q) -> p a g q", a=2, g=2))
                        if m == 7:
                            post_cmp(kv)
                    return [sG, s0, s1]
                run_pipeline([mk_cmp(kv, m) for kv in range(2) for m in range(8)])
                o2acc = o2_ring.next()
                k.dma(skw.re("p w g d -> p w (g d)"), skw_d[bl].re("(w p) c -> p w c", p=128))
                k.dma(svw.re("p w g d -> p w (g d)"), svw_d[bl].re("(w p) c -> p w c", p=128))
                for (src, dst) in ((skw, o_kwin), (svw, o_vwin)):
                    k.dma(dst[bl, 0:120, :], src[8:128, 0, :, :].re("p g d -> p (g d)"))
                    k.dma(dst[bl, 120:504, :].re("(w p) c -> p w c", p=128), src[:, 1:4, :, :].re("p w g d -> p w (g d)"))
                for g in range(2):
                    pb = MS.next()
                    for w in range(4):
                        k.tr(pb[:, w * 128:(w + 1) * 128], skw[:, w, g, :], self.identf)
                    k.cp("act", KwT2[:, g, :, :], pb.re("p (w q) -> p w q", w=4))
                k.cp("pool", Vwb[:, :, :, 0:128], svw)
                for g in range(2):
                    qg = q2g(bl, g)
                    st_ = STr.next()
                    for m in range(8):
                        k.mm(st_[:, m * 32:(m + 1) * 32], Kc2T[:, g, m * 128:(m + 1) * 128], qg)
                    sb_ = stsb_ring.next()
                    k.tt("dve", sb_[:, 0:256].re("p (m r i) -> p m r i", m=8, r=4), st_[:, 0:256].re("p (m r i) -> p m r i", m=8, r=4),
                         c2bias[:, :, g * 4:(g + 1) * 4].us(3).bc([128, 8, 4, 8]), ALU.add)
                    ETc = ET_ring.next()
                    k.act(ETc[:, 0:256], sb_[:, 0:256], AF.Exp)
                    for m in range(8):
                        k.mm(OC[:32, 0:129], ETc[:, m * 32:(m + 1) * 32], Vc2[:, m, g, 0:129], start=(m == 0), stop=(m == 7))
                    pim = MS.next()
                    for m in range(8):
                        k.mm(pim[:32, 0:256], ETc[:, m * 32:(m + 1) * 32], c2tab[:, m, :], start=(m == 0), stop=(m == 7))
                    rdk = self.st_ring.next()
                    finish2(OC, o2acc, bl, g, 0, keep=rdk[:32, 15:16])
                    k.ts("dve", impn, pim[:32, 0:256], rdk[:32, 15:16], None, ALU.mult)
                    pi2 = MS.next()
                    k.mm(pi2[:8, 0:256], selsum, impn)
                    sc = sc_ring.next()
                    sc2 = sc_ring.next()
                    k.tt("dve", sc[:, 0:256], pi2[:8, 0:256], forced2, ALU.add)
                    k.max8(m8[:, 0:8], sc[:, 0:256])
                    k.mrep(sc2[:, 0:256], m8[:, 0:8], sc[:, 0:256], -3.0e38)
                    k.max8(m8[:, 8:16], sc2[:, 0:256])
                    k.ts("dve", sc2[:, 0:256], sc[:, 0:256], m8[:, 14:15], None, ALU.is_ge)
                    k.ts("dve", sc2[:, 0:256], sc2[:, 0:256], 30000.0, -30000.0, ALU.mult, ALU.add)
                    k.cp("dve", Mb2h[:, g, :, :], sc2[:, 0:256].re("i (p h) -> i h p", h=2))
                shared = {}

                def mk_sel(m, g, bl=bl, shared=shared):
                    box = {}
                    qg = q2g(bl, g)
                    hb = m // 4

                    def sG():
                        if g == 0:
                            Xk = X_ring.next()
                            k.gather(Xk.re("p a g d -> p (a g d)"), caches[2], idx8[:, bl, m:m + 1])
                            Xv = X_ring.next()
                            k.gather(Xv.re("p a g d -> p (a g d)"), caches[3], idx8[:, bl, m:m + 1])
                            shared[m] = [Xk, Xv, None, None]

                    def s0():
                        if g == 0:
                            Xk, Xv = shared[m][0], shared[m][1]
                            Vb = Vb_ring.next()
                            k.cp("act", Vb[:, :, :, 0:128], Xv)
                            KT = XT_ring.next()
                            transposes_to(Xk, KT, m)
                            shared[m][2] = Vb
                            shared[m][3] = KT
                        KT = shared[m][3]
                        st_ = STr.next()
                        box["st"] = st_
                        for row in range(16):
                            k.mm(st_[:, row * 32:(row + 1) * 32], KT[:, row, g, :], qg, start=True, stop=False)
                            k.mm(st_[:, row * 32:(row + 1) * 32], Mb2h[:, g, hb, :], i48, start=False, stop=True)

                    def s1():
                        sb_ = stsb_ring.next()
                        k.tt("dve", sb_.re("p (a r i) -> p a r i", a=16, r=4), box["st"].re("p (a r i) -> p a r i", a=16, r=4),
                             abias[:, m * 16:(m + 1) * 16, g * 4:(g + 1) * 4].us(3).bc([128, 16, 4, 8]), ALU.add)
                        box["ET"] = ET_ring.next()
                        k.act(box["ET"], sb_, AF.Exp)

                    def s2():
                        Vb = shared[m][2]
                        for row in range(16):
                            k.mm(OG[g][:32, 0:129], box["ET"][:, row * 32:(row + 1) * 32], Vb[:, row, g, 0:129],
                                 start=(m == 0 and row == 0), stop=False)
                    return [sG, s0, s1, s2]
                run_pipeline([mk_sel(m, g) for m in range(8) for g in range(2)])
                for g in range(2):
                    qg = q2g(bl, g)
                    st_ = STr.next()
                    k.mm(st_[:8, 0:32], ks2T[:, bl, g, :], qg)
                    sb_ = stsb_ring.next()
                    k.tt("dve", sb_[:8, 0:32].re("p (r i) -> p r i", r=4), st_[:8, 0:32].re("p (r i) -> p r i", r=4),
                         tbias[:, g * 4:(g + 1) * 4].us(2).bc([8, 4, 8]), ALU.add)
                    ET = ET_ring.next()
                    k.act(ET[:8, 0:32], sb_[:8, 0:32], AF.Exp)
                    k.tt("dve", ET[:8, 0:32].re("p (r i) -> p r i", r=4), ET[:8, 0:32].re("p (r i) -> p r i", r=4),
                         tmask.us(1).bc([8, 4, 8]), ALU.mult)
                    k.mm(OG[g][:32, 0:129], ET[:8, 0:32], V2s[:8, bl, g, 0:129], start=False, stop=True)
                    finish2(OG[g], o2acc, bl, g, 1)
                    st_ = STr.next()
                    for w in range(4):
                        k.mm(st_[:, w * 32:(w + 1) * 32], KwT2[:, g, w, :], qg)
                    sb_ = stsb_ring.next()
                    k.tt("dve", sb_[:, 0:128].re("p (w r i) -> p w r i", w=4, r=4), st_[:, 0:128].re("p (w r i) -> p w r i", w=4, r=4),
                         swbias[:, :, g * 4:(g + 1) * 4].us(3).bc([128, 4, 4, 8]), ALU.add)
                    ET = ET_ring.next()
                    k.act(ET[:, 0:128], sb_[:, 0:128], AF.Exp)
                    k.tt("dve", ET[:, 0:128].re("p (w r i) -> p w r i", w=4, r=4), ET[:, 0:128].re("p (w r i) -> p w r i", w=4, r=4),
                         swmask.us(2).bc([128, 4, 4, 8]), ALU.mult)
                    for w in range(4):
                        k.mm(OC[:32, 0:129], ET[:, w * 32:(w + 1) * 32], Vwb[:, w, g, 0:129], start=(w == 0), stop=False)
                    st_ = STr.next()
                    k.mm(st_[:8, 0:32], kw2T[:, bl, g, :], qg)
                    sb_ = stsb_ring.next()
                    k.tt("dve", sb_[:8, 0:32].re("p (r i) -> p r i", r=4), st_[:8, 0:32].re("p (r i) -> p r i", r=4),
                         tbias[:, g * 4:(g + 1) * 4].us(2).bc([8, 4, 8]), ALU.add)
                    ET = ET_ring.next()
                    k.act(ET[:8, 0:32], sb_[:8, 0:32], AF.Exp)
                    k.tt("dve", ET[:8, 0:32].re("p (r i) -> p r i", r=4), ET[:8, 0:32].re("p (r i) -> p r i", r=4),
                         tmask.us(1).bc([8, 4, 8]), ALU.mult)
                    k.mm(OC[:32, 0:129], ET[:8, 0:32], V2w[:8, bl, g, 0:129], start=False, stop=True)
                    finish2(OC, o2acc, bl, g, 2)
                sz2 = sz2_ring.next()
                k.dma(sz2, sz_scr[1024 + bl * 8:1024 + bl * 8 + 8, :])
                mix2 = mix2_ring.next()
                k.tt("dve", mix2, o2acc, sz2, ALU.mult)
                pt = MS.next().cast(BF16)
                for e_ in range(8):
                    k.tr(pt[:, e_ * 8:(e_ + 1) * 8], mix2[:8, e_ * 128:(e_ + 1) * 128], self.ident[:8, :8])
                mix2a = mix2a_ring.next()
                k.cp("act", mix2a, pt[:, 0:64].re("p (a t) -> p a t", a=8))
                k.dma(mixT[1024:2048, 1024 + bl * 8:1024 + bl * 8 + 8].re("(a p) t -> p a t", p=128), mix2a)


def static_tables():
    f32 = np.float32
    sl = np.array(SLOPES, f32)
    kk = np.arange(128)
    dbias = (sl[None, :] * (kk[:, None] - 127)).astype(f32)
    n = np.arange(512)
    j = np.arange(128)
    o = n[:, None] - 4 * j[None, :]
    C = np.where((o == -1) | (o == 3), 0.5, np.where((o >= 0) & (o <= 2), 1.0, 0.0))
    C[511:, :] = 0.0
    ctab = C.reshape(4, 128, 128).transpose(1, 0, 2).reshape(128, 512)
    i4 = np.tile(np.eye(128, dtype=f32), (1, 4))
    tri = np.stack([(kk[:, None] <= kk[None, :]), (kk[:, None] > kk[None, :])], axis=1).astype(f32).reshape(128, 256)
    return {"dbias": dbias, "ctab": ctab.astype(NPBF), "i4": i4.astype(NPBF), "tri": tri.astype(NPBF)}


def core_tables(c):
    f32 = np.float32
    sl = np.array(SLOPES, f32)
    kk = np.arange(128)
    selb = np.zeros((128, 288, 8), f32)
    winb = np.zeros((128, 8, 5, 8), f32)
    cmpb = np.zeros((128, 8, 4, 8), f32)
    cmpm = np.zeros((128, 8, 4, 128), f32)
    forced = np.zeros((128, 8, 128), f32)
    notown = np.zeros((128, 8, 128), f32)
    off = 0
    q = np.arange(128)
    j = np.arange(128)
    for s in range(8):
        qb = c + 8 * s
        for t in range(8 * s + 8):
            if t < qb:
                selb[:, off + t, :] = sl[None, :] * (kk[:, None] - 127 - 128 * (qb - t))
        off += 8 * s + 8
        for o in range(5):
            if o == 4 or qb - 4 + o >= 0:
                winb[:, s, o, :] = sl[None, :] * (kk[:, None] - 127 - 128 * (4 - o))
            else:
                winb[:, s, o, :] = NEG
        for nt in range(4):
            n = nt * 128 + kk
            end = 16 * n + 31
            d = end - (128 * qb + 127)
            v = sl[None, :] * d[:, None]
            bad = (n >= 511) | (d > 0)
            v[bad, :] = NEG
            cmpb[:, s, nt, :] = v
            cmpm[:, s, nt, :] = ((end[:, None] <= 128 * qb + q[None, :]) & (n[:, None] < 511)).astype(f32)
        cur = (128 * qb + q) // 64
        fm = (j[None, :] == 0) | (j[None, :] == cur[:, None]) | (j[None, :] == cur[:, None] - 1)
        inv = j[None, :] > cur[:, None]
        forced[:, s, :] = np.where(fm, 1e30, np.where(inv, -1e30, 0.0))
        notown[:, s, :] = (j[None, :] < 2 * qb).astype(f32)
    return {"selbias": selb.reshape(128, -1), "winbias": winb.reshape(128, -1), "cmpbias": cmpb.reshape(128, -1),
            "cmpmask": cmpm.reshape(128, -1).astype(NPBF), "forced": forced.reshape(128, -1),
            "notown": notown.reshape(128, -1)}


def sample_tables():
    f32 = np.float32
    sl = np.array(SLOPES, f32)
    p = np.arange(128)
    ref = PAST + 7
    marr = np.tile(np.arange(8, dtype=f32)[None, :], (128, 1))
    m = np.arange(8)
    n = 8 * p[:, None] + m[None, :]
    end = 16 * n + 31
    c2b = (sl[None, None, :] * (end[:, :, None] - ref)).astype(f32)
    c2b[n >= 1023] = NEG
    j = np.arange(256)
    o = n[:, :, None] - 4 * j[None, None, :]
    C = np.where((o == -1) | (o == 3), 0.5, np.where((o >= 0) & (o <= 2), 1.0, 0.0))
    C[n >= 1023] = 0.0
    forced2 = np.zeros((8, 256), f32)
    forced2[:, 0] = 1e30
    forced2[:, 255] = 1e30
    row = np.arange(128)
    ab = (sl[None, None, :] * ((128 * p[:, None] + row[None, :])[:, :, None] - ref)).astype(f32)
    w = np.arange(4)
    wpos = w[None, :] * 128 + p[:, None]
    swb = (sl[None, None, :] * (wpos[:, :, None] - 519)).astype(f32)
    i = np.arange(8)
    swm = (wpos[:, :, None] > i[None, None, :]).astype(f32)
    tb = (sl[None, :] * (i[:, None] - 7)).astype(f32)
    tm = (i[:, None] <= i[None, :]).astype(f32)
    i48 = np.tile(np.eye(8, dtype=f32), (1, 4))
    selr = np.eye(32, dtype=f32)
    selsum = np.tile(np.eye(8, dtype=f32), (4, 1))
    return {"marr": marr, "c2bias": c2b.reshape(128, 64), "c2tab": C.reshape(128, -1).astype(NPBF), "forced2": forced2,
            "abias": ab.reshape(128, -1), "swbias": swb.reshape(128, 32), "swmask": swm.reshape(128, 32).astype(NPBF),
            "tbias": tb, "tmask": tm.astype(NPBF), "i48": i48.astype(NPBF), "selr": selr, "selsum": selsum}


_PROG = {}


def get_prog(stages):
    if stages not in _PROG:
        _PROG[stages] = Prog(stages)
    return _PROG[stages]


STAGES = 6


def kernel(**inputs):
    f32 = np.float32
    xp = np.asarray(inputs["x_prompt"], f32)[0]
    xsm = np.asarray(inputs["x_sample"], f32)
    prog = get_prog(STAGES)
    ident = np.eye(128, dtype=f32)
    gcol = np.ascontiguousarray(np.asarray(inputs["g_norm"], f32).reshape(16, 128).T)
    cvec = np.concatenate([np.asarray(inputs["w_dw"], f32), np.asarray(inputs["b_dw"], f32)[None],
                           np.asarray(inputs["ln_g"], f32)[None], np.asarray(inputs["ln_b"], f32)[None],
                           np.asarray(inputs["b_pw2"], f32)[None]], axis=0)
    gvec = np.stack([np.asarray(inputs[n], f32) for n in ("g_q", "g_k_slc", "g_k_win", "g_k_cmp")], axis=0)
    xpad = np.concatenate([np.zeros((512, D), f32), xp], axis=0)
    in_maps = []
    statics = static_tables()
    for c in range(NCORES):
        xB = np.stack([xpad[512 + (c + 8 * s) * 128 - 32: 512 + (c + 8 * s) * 128 + 128] for s in range(8)], axis=0)
        m = {
            "xB": np.ascontiguousarray(xB),
            "xS": np.ascontiguousarray(xsm[4 * c:4 * c + 4].reshape(32, D)),
            "w_in": np.asarray(inputs["w_in"], f32),
            "gcol": gcol, "cvec": np.ascontiguousarray(cvec), "gvec": np.ascontiguousarray(np.broadcast_to(gvec.reshape(1, 512), (128, 512))),
            "w_pw2": np.asarray(inputs["w_pw2"], f32),
            "ident": ident.astype(NPBF), "identf": ident,
            "sconv": np.ascontiguousarray(np.asarray(inputs["state_conv"], f32)[4 * c:4 * c + 4].reshape(120, 1024)),
        }
        if STAGES >= 3:
            m["xA"] = xp
            m["xH"] = np.ascontiguousarray(np.stack([np.stack([xpad[512 + (c + 8 * s - 4 + o) * 128: 512 + (c + 8 * s - 3 + o) * 128]
                                                     for o in range(4)], axis=0) for s in range(8)], axis=0))
            for nm in ("w_out", "w_cmp_k1", "w_cmp_v1", "w_cmp_k2", "w_cmp_v2", "pe_cmp_k", "pe_cmp_v"):
                m[nm] = np.asarray(inputs[nm], f32)
            m.update(statics)
            m.update(core_tables(c))
        if STAGES >= 6:
            m["ptab"] = np.ascontiguousarray(np.asarray(inputs["page_table"], np.int32)[4 * c:4 * c + 4])
            for nm in ("cache_k_cmp", "cache_v_cmp", "cache_k_slc", "cache_v_slc"):
                m[nm] = np.asarray(inputs[nm], f32).reshape(NPHYS * 8, 4096)
            m["skw"] = np.ascontiguousarray(np.asarray(inputs["state_k_win"], f32)[4 * c:4 * c + 4].reshape(4, 512, 256))
            m["svw"] = np.ascontiguousarray(np.asarray(inputs["state_v_win"], f32)[4 * c:4 * c + 4].reshape(4, 512, 256))
            m.update(sample_tables())
        in_maps.append({kk: m[kk] for kk in prog.din})
    res = run_bass_kernel_spmd(prog.k.nc, in_maps, core_ids=list(range(NCORES)))
    R = res.results

    def gat(name):
        return [np.asarray(R[c][name]) for c in range(NCORES)]

    def prow(name, width):
        out = np.zeros((T, width), f32)
        arr = gat(name)
        for c in range(NCORES):
            for s in range(8):
                qb = c + 8 * s
                out[qb * 128:(qb + 1) * 128] = arr[c][s]
        return out

    def srow(name, width):
        return np.concatenate([a.reshape(4, 8, width) for a in gat(name)], axis=0)

    y_prompt = prow("o_y", D)[None]
    y_sample = srow("o_y2", D)
    kc = prow("o_kc", 256).reshape(1, T, 2, 128); vc = prow("o_vc", 256).reshape(1, T, 2, 128)
    ks = prow("o_ks", 256).reshape(1, T, 2, 128); vs = prow("o_vs", 256).reshape(1, T, 2, 128)
    kw = prow("o_kw", 256).reshape(1, T, 2, 128); vw = prow("o_vw", 256).reshape(1, T, 2, 128)
    conv_p = np.asarray(R[7]["o_convp"])[None]
    kc2 = srow("o_kc2", 256).reshape(32, 8, 2, 128); vc2 = srow("o_vc2", 256).reshape(32, 8, 2, 128)
    ks2 = srow("o_ks2", 256).reshape(32, 8, 2, 128); vs2 = srow("o_vs2", 256).reshape(32, 8, 2, 128)
    conv_s = np.concatenate([a.reshape(4, 30, 1024) for a in gat("o_convs")], axis=0)
    kwin2 = np.concatenate(gat("o_kwin"), axis=0).reshape(32, 512, 2, 128)
    vwin2 = np.concatenate(gat("o_vwin"), axis=0).reshape(32, 512, 2, 128)
    return (y_prompt, y_sample, kc, vc, ks, vs, kw[:, T - 512:], vw[:, T - 512:], conv_p,
            kc2, vc2, ks2, vs2, kwin2, vwin2, conv_s)
```

```python
import contextlib
import numpy as np
import ml_dtypes
import concourse.bass as bass
import concourse.mybir as mybir
from concourse.bass_utils import run_bass_kernel_spmd

F32 = mybir.dt.float32
BF16 = mybir.dt.bfloat16
I32 = mybir.dt.int32
AF = mybir.ActivationFunctionType
ALU = mybir.AluOpType
AX = mybir.AxisListType
NPBF = ml_dtypes.bfloat16

NCORES = 8
D = 2048
T = 8192
HD = 128
NQB = 64
NSLOT = 8
DIN = 6680
EPS = 1e-6
SCALE = HD ** -0.5
PAST = 16384
NPAGE = 128
NPHYS = 5120
SLOPES = [2.0 ** (-(h + 1)) for h in range(8)]
NEG = -30000.0
C_UA, C_UB, C_ZC, C_Q, C_KC, C_KS, C_KW, C_GT, C_ZN = 0, 1024, 2048, 3072, 4096, 4608, 5120, 5632, 5656
TOKC = 8 * 160 + 32
NOWN = 1024 + 32
NS_DMA = 8


class Tok:
    __slots__ = ("w", "r")

    def __init__(self):
        self.w = None
        self.r = {}


class V:
    def __init__(self, ap, toks):
        self.ap = ap
        self.toks = toks

    def __getitem__(self, i):
        return V(self.ap[i], self.toks)

    def re(self, pat, **kw):
        return V(self.ap.rearrange(pat, **kw), self.toks)

    def bc(self, shape):
        return V(self.ap.broadcast_to(shape), self.toks)

    def us(self, dim):
        return V(self.ap.unsqueeze(dim), self.toks)

    def cast(self, dt):
        return V(self.ap.bitcast(dt), self.toks)

    def tk(self, tok):
        return V(self.ap, [tok])


class Ring:
    def __init__(self, items):
        self.items = items
        self.i = 0

    def next(self):
        it = self.items[self.i % len(self.items)]
        self.i += 1
        return it


def run_pipeline(jobs):
    if not jobs:
        return
    ns = max(len(j) for j in jobs)
    for step in range(len(jobs) + ns - 1):
        for st in range(ns):
            t = step - st
            if 0 <= t < len(jobs) and st < len(jobs[t]):
                jobs[t][st]()

ENG = ("pe", "act", "dve", "pool", "sp")


class Rec:
    def __init__(self):
        self.nc = bass.Bass("TRN2", target_bir_lowering=False)
        self.gst = contextlib.ExitStack()
        self.ops = {e: [] for e in ENG}
        self.cnt = {e: 0 for e in ENG}
        self.seen = {e: {} for e in ENG}
        self.dman = {e: 0 for e in ENG}
        self.sems = {}
        self.slot_last = {}
        for e in ("pe", "act", "dve", "pool"):
            self.sems[e] = self.gst.enter_context(self.nc.semaphore("c_" + e))
        for q in ("sp", "pool", "act"):
            for i in range(NS_DMA):
                nm = "d_%s_%d" % (q, i)
                self.sems[nm] = self.gst.enter_context(self.nc.semaphore(nm))
        self.uid = 0
        self.stack = self.gst

    def sb(self, name, shape, dt):
        self.uid += 1
        t = self.stack.enter_context(self.nc.sbuf_tensor("%s_%d" % (name, self.uid), list(shape), dt))
        return V(t[tuple(slice(None) for _ in shape)], [Tok()])

    def ring(self, name, n, shape, dt):
        return Ring([self.sb(name + str(i), shape, dt) for i in range(n)])

    def psum(self, name):
        t = self.gst.enter_context(self.nc.psum_tensor(name, [128, 512], F32))
        return V(t[:, :], [Tok()])

    def dram(self, name, shape, dt, kind):
        t = self.nc.dram_tensor(name, list(shape), dt, kind=kind)
        return V(t.ap(), [Tok()])

    @contextlib.contextmanager
    def phase(self):
        old = self.stack
        with contextlib.ExitStack() as st:
            self.stack = st
            yield
            self.flush()
        self.stack = old

    def _rec(self, eng, fn, R, W, dma=False):
        waits = {}

        def need(dep):
            if dep is None:
                return
            k, v = dep
            if eng == "pe" and k == "pe":
                return
            if waits.get(k, 0) < v:
                waits[k] = v

        for t in R:
            need(t.w)
        for t in W:
            need(t.w)
            for k, v in t.r.items():
                need((k, v))
        if dma:
            n = self.dman[eng]
            self.dman[eng] += 1
            slot = "d_%s_%d" % (eng, n % NS_DMA)
            rnd = n // NS_DMA
            if rnd > 0:
                need((slot, 16 * rnd))
            dep = (slot, 16 * (rnd + 1))
            inc = (slot, 16)
            self.slot_last[slot] = dep[1]
        else:
            self.cnt[eng] += 1
            dep = (eng, self.cnt[eng])
            inc = (eng, 1)
        wl = []
        seen = self.seen[eng]
        for k, v in waits.items():
            if seen.get(k, 0) < v:
                seen[k] = v
                wl.append((k, v))
        self.ops[eng].append((fn, wl, inc))
        for t in R:
            if t.r.get(dep[0], 0) < dep[1]:
                t.r[dep[0]] = dep[1]
        for t in W:
            t.w = dep
            t.r = {}

    def flush(self, final=False):
        if final:
            wl = [(k, v) for k, v in self.slot_last.items()]
            self.ops["sp"].append((None, wl, None))
        ops_all = self.ops
        self.ops = {e: [] for e in ENG}
        sems = self.sems
        with self.nc.Block() as blk:
            for eng, deco in (("pe", blk.tensor), ("act", blk.scalar), ("dve", blk.vector),
                              ("pool", blk.gpsimd), ("sp", blk.sync)):
                ops = ops_all[eng]

                def body(e, ops=ops):
                    for fn, wl, inc in ops:
                        for k, v in wl:
                            e.wait_ge(sems[k], v)
                        if fn is not None:
                            fn(e).then_inc(sems[inc[0]], inc[1])

                deco(body)

    @staticmethod
    def _s(x):
        return x.ap if isinstance(x, V) else x

    @staticmethod
    def _t(*xs):
        out = []
        for x in xs:
            if isinstance(x, V):
                out += x.toks
        return out

    def mm(self, out, lhsT, rhs, start=True, stop=True):
        self._rec("pe", lambda e: e.matmul(out=out.ap, lhsT=lhsT.ap, rhs=rhs.ap, start=start, stop=stop),
                  self._t(lhsT, rhs), out.toks)

    def tr(self, out, in_, ident):
        self._rec("pe", lambda e: e.transpose(out=out.ap, in_=in_.ap, identity=ident.ap),
                  self._t(in_, ident), out.toks)

    def act(self, out, in_, func, bias=None, scale=None, accum=None):
        kw = {}
        if bias is not None:
            kw["bias"] = self._s(bias)
        if scale is not None:
            kw["scale"] = self._s(scale)
        if accum is not None:
            kw["accum_out"] = accum.ap
        self._rec("act", lambda e: e.activation(out=out.ap, in_=in_.ap, func=func, **kw),
                  self._t(in_, bias, scale), self._t(out, accum))

    def ts(self, eng, out, in0, s1, s2, op0, op1=None, accum=None):
        kw = {}
        if op1 is not None:
            kw["op1"] = op1
        if accum is not None:
            kw["accum_out"] = accum.ap
        a1, a2 = self._s(s1), self._s(s2)
        self._rec(eng, lambda e: e.tensor_scalar(out=out.ap, in0=in0.ap, scalar1=a1, scalar2=a2, op0=op0, **kw),
                  self._t(in0, s1, s2), self._t(out, accum))

    def tt(self, eng, out, in0, in1, op):
        self._rec(eng, lambda e: e.tensor_tensor(out=out.ap, in0=in0.ap, in1=in1.ap, op=op),
                  self._t(in0, in1), out.toks)

    def stt(self, eng, out, in0, sc, in1, op0, op1):
        a = self._s(sc)
        self._rec(eng, lambda e: e.scalar_tensor_tensor(out=out.ap, in0=in0.ap, scalar=a, in1=in1.ap, op0=op0, op1=op1),
                  self._t(in0, sc, in1), out.toks)

    def cp(self, eng, out, in_):
        if eng == "act":
            self._rec("act", lambda e: e.copy(out=out.ap, in_=in_.ap), in_.toks, out.toks)
        else:
            self._rec(eng, lambda e: e.tensor_copy(out=out.ap, in_=in_.ap), in_.toks, out.toks)

    def red(self, out, in_, op=ALU.add, axis=AX.X):
        self._rec("dve", lambda e: e.tensor_reduce(out=out.ap, in_=in_.ap, axis=axis, op=op), in_.toks, out.toks)

    def recip(self, out, in_):
        self._rec("dve", lambda e: e.reciprocal(out=out.ap, in_=in_.ap), in_.toks, out.toks)

    def memset(self, eng, out, val):
        self._rec(eng, lambda e: e.memset(out.ap, val), [], out.toks)

    def max8(self, out, in_):
        self._rec("dve", lambda e: e.max(out=out.ap, in_=in_.ap), in_.toks, out.toks)

    def mrep(self, out, rep, vals, imm):
        self._rec("dve", lambda e: e.match_replace(out=out.ap, in_to_replace=rep.ap, in_values=vals.ap, imm_value=imm),
                  self._t(rep, vals), out.toks)

    def dma(self, out, in_, q="sp"):
        self._rec(q, lambda e: e.dma_start(out=out.ap, in_=in_.ap), in_.toks, out.toks, dma=True)

    def gather(self, out, in_, idx):
        self._rec("pool", lambda e: e.indirect_dma_start(
            out=out.ap, out_offset=None, in_=in_.ap,
            in_offset=bass.IndirectOffsetOnAxis(ap=idx.ap, axis=0)),
            self._t(in_, idx), out.toks, dma=True)


class Prog:
    def __init__(self, stages):
        self.stages = stages
        k = self.k = Rec()
        self.din = {}
        self.dout = {}
        self.ps = Ring([k.psum("ps%d" % i) for i in range(8)])
        self.build()

    def inp(self, name, shape, dt=F32):
        v = self.k.dram(name, shape, dt, "ExternalInput")
        self.din[name] = v
        return v

    def outp(self, name, shape, dt=F32):
        v = self.k.dram(name, shape, dt, "ExternalOutput")
        self.dout[name] = v
        return v

    def scr(self, name, shape, dt):
        return self.k.dram(name, shape, dt, "Internal")

    def rstd_from_ss(self, st, R, n, inv_n):
        k = self.k
        k.act(st[:R, n:2 * n], st[:R, 0:n], AF.Ln, scale=inv_n, bias=self.epsc[:R, 0:1])
        k.act(st[:R, 2 * n:3 * n], st[:R, n:2 * n], AF.Exp, scale=-0.5)
        return st[:R, 2 * n:3 * n]

    def silu_from(self, eng, out, src, R, n, bias=None):
        k = self.k
        e = self.etmp.next()
        if bias is None:
            k.act(e[:R, :n], src, AF.Exp, scale=-1.0)
            k.ts("dve", e[:R, :n], e[:R, :n], 1.0, None, ALU.add)
            k.recip(e[:R, :n], e[:R, :n])
            k.tt("dve", out, src, e[:R, :n], ALU.mult)
        else:
            z = self.etmp.next()
            k.ts("dve", z[:R, :n], src, bias, None, ALU.add)
            k.act(e[:R, :n], z[:R, :n], AF.Exp, scale=-1.0)
            k.ts("dve", e[:R, :n], e[:R, :n], 1.0, None, ALU.add)
            k.recip(e[:R, :n], e[:R, :n])
            k.tt("dve", out, z[:R, :n], e[:R, :n], ALU.mult)

    def front(self, src, R, dst):
        xt = self.front_load(src, R)
        self.front_rest(xt, R, dst)

    def front_load(self, src, R):
        xt = self.xt_ring.next()
        self.k.dma(xt[:R], src)
        return xt

    def front_rest(self, xt, R, dst):
        self.front_T(self.front_norm(xt, R), R, dst)

    def front_norm(self, xt, R):
        k = self.k
        xs = self.xs_ring.next()
        st = self.st_ring.next()
        k.act(xs[:R], xt[:R], AF.Square, accum=st[:R, 0:1])
        rs = self.rstd_from_ss(st, R, 1, 1.0 / D)
        k.ts("dve", xs[:R], xt[:R], rs, None, ALU.mult)
        return xs

    def front_T(self, xs, R, dst):
        k = self.k
        for half in range(2):
            pb = self.ps.next().cast(BF16)
            for j in range(8):
                kc = half * 8 + j
                k.tr(pb[:, j * R:(j + 1) * R], xs[:R, kc * 128:(kc + 1) * 128], self.ident[:R, :R])
            k.cp("act" if half == 0 else "dve", dst[:, half * 8:(half + 1) * 8, :],
                 pb[:, 0:8 * R].re("p (a b) -> p a b", a=8))

    def wload(self, c0, wd, scale_g=True, src=None, nkc=16):
        k = self.k
        wb = self.wb_ring.next()
        src = self.din["w_in"] if src is None else src
        for q4 in range(nkc // 4):
            stg = self.wstg_ring.next()
            k.dma(stg[:, :, :wd], src[q4 * 512:(q4 + 1) * 512, c0:c0 + wd].re("(a p) c -> p a c", p=128),
                  q="sp")
            for a in range(4):
                kc = q4 * 4 + a
                if scale_g:
                    k.ts("pool", wb[:, kc, :wd], stg[:, a, :wd], self.gcol[:, kc:kc + 1], 1.0, ALU.mult, ALU.mult)
                else:
                    k.cp("pool", wb[:, kc, :wd], stg[:, a, :wd])
        return wb

    def normed(self, pb, R, nh, gidx, out_f32=None, out_bf=None, mul=None):
        k = self.k
        sq = self.etmp.next()
        st = self.st_ring.next()
        k.act(sq[:R, :nh * 128], pb, AF.Square)
        k.red(st[:R, 0:nh], sq[:R, :nh * 128].re("p (h d) -> p h d", h=nh))
        rs = self.rstd_from_ss(st, R, nh, 1.0 / HD)
        tmp = self.etmp.next()
        t3 = tmp[:R, :nh * 128].re("p (h d) -> p h d", h=nh)
        k.tt("dve", t3, pb.re("p (h d) -> p h d", h=nh), rs.us(2).bc([R, nh, 128]), ALU.mult)
        g = self.gvec[:R, gidx:gidx + 1, :].bc([R, nh, 128])
        if out_f32 is not None:
            k.tt("dve", out_f32.re("p (h d) -> p h d", h=nh), t3, g, ALU.mult)
            if out_bf is not None:
                k.cp("dve", out_bf, out_f32)
        else:
            k.tt("dve", out_bf.re("p (h d) -> p h d", h=nh), t3, g, ALU.mult)

    def build(self):
        k = self.k
        inp, outp = self.inp, self.outp
        xB = inp("xB", [8, 160, D])
        xS = inp("xS", [32, D])
        w_in = inp("w_in", [D, DIN])
        gcol_d = inp("gcol", [128, 16])
        cvec_d = inp("cvec", [35, 1024])
        gvec_d = inp("gvec", [128, 512])
        w_pw2 = inp("w_pw2", [1024, 1024])
        ident_d = inp("ident", [128, 128], BF16)
        identf_d = inp("identf", [128, 128])
        sconv_d = inp("sconv", [120, 1024])

        o_kc = outp("o_kc", [8, 128, 256]); o_vc = outp("o_vc", [8, 128, 256])
        o_ks = outp("o_ks", [8, 128, 256]); o_vs = outp("o_vs", [8, 128, 256])
        o_kw = outp("o_kw", [8, 128, 256]); o_vw = outp("o_vw", [8, 128, 256])
        o_convp = outp("o_convp", [30, 1024])
        o_kc2 = outp("o_kc2", [32, 256]); o_vc2 = outp("o_vc2", [32, 256])
        o_ks2 = outp("o_ks2", [32, 256]); o_vs2 = outp("o_vs2", [32, 256])
        o_convs = outp("o_convs", [120, 1024])
        o_kwin = outp("o_kwin", [4, 512, 256]); o_vwin = outp("o_vwin", [4, 512, 256])
        o_y = outp("o_y", [8, 128, D])
        o_y2 = outp("o_y2", [32, D])

        mixT = self.scr("mixT", [D, NOWN], BF16)
        qT_scr = self.scr("qT_scr", [8, 128, 8, 128], BF16)
        sz_scr = self.scr("sz_scr", [NOWN, 1024], BF16)

        self.ident = k.sb("ident", [128, 128], BF16)
        self.identf = k.sb("identf", [128, 128], F32)
        self.gcol = k.sb("gcol", [128, 16], F32)
        self.gvec = k.sb("gvec", [128, 4, 128], F32)
        self.cvT = k.sb("cvT", [128, 8, 35], F32)
        self.epsc = k.sb("epsc", [128, 1], F32)
        ksT_own = k.sb("ksT_own", [128, 8, 2, 128], BF16)
        Vs_own = k.sb("Vs_own", [128, 8, 2, 130], BF16)
        KwT_own = k.sb("KwT_own", [128, 8, 2, 128], BF16)
        Vw_own = k.sb("Vw_own", [128, 8, 2, 130], BF16)
        gates = k.sb("gates", [128, 8, 24], F32)
        q2T = k.sb("q2T", [128, 4, 8, 8], BF16)
        ks2T = k.sb("ks2T", [128, 4, 2, 8], BF16)
        kw2T = k.sb("kw2T", [128, 4, 2, 8], BF16)
        V2s = k.sb("V2s", [8, 4, 2, 130], BF16)
        V2w = k.sb("V2w", [8, 4, 2, 130], BF16)
        gates2 = k.sb("gates2", [8, 4, 24], F32)

        self.pers = dict(ksT_own=ksT_own, Vs_own=Vs_own, KwT_own=KwT_own, Vw_own=Vw_own, gates=gates, q2T=q2T,
                         ks2T=ks2T, kw2T=kw2T, V2s=V2s, V2w=V2w, gates2=gates2)
        self.dr = dict(mixT=mixT, qT_scr=qT_scr, sz_scr=sz_scr, xB=xB, xS=xS, o_y=o_y, o_y2=o_y2, o_kwin=o_kwin, o_vwin=o_vwin)
        with k.phase():
            k.dma(self.ident, ident_d)
            k.dma(self.identf, identf_d)
            k.dma(self.gcol, gcol_d)
            k.dma(self.gvec, gvec_d.re("p (a d) -> p a d", a=4))
            k.memset("dve", self.epsc, EPS)
            k.ts("dve", self.gvec[:, 0, :], self.gvec[:, 0, :], SCALE, None, ALU.mult)
            cv = k.sb("cv", [35, 1024], F32)
            k.dma(cv, cvec_d)
            for j in range(8):
                pb = self.ps.next()
                k.tr(pb[:, 0:35], cv[:, j * 128:(j + 1) * 128], self.identf[:35, :35])
                k.cp("dve", self.cvT[:, j, :], pb[:, 0:35])
            for t in (Vs_own, Vw_own, V2s, V2w):
                R = 8 if t in (V2s, V2w) else 128
                k.memset("dve", t[:R].re("p a g c -> p (a g) c")[:, :, 128:129], 1.0)
                k.memset("dve", t[:R].re("p a g c -> p (a g) c")[:, :, 129:130], 0.0)

        if self.stages < 1:
            k.flush(final=True)
            return

        with contextlib.ExitStack() as bst:
            old = k.stack
            k.stack = bst
            xTB = k.sb("xTB", [128, 16, TOKC], BF16)
            yT = k.sb("yT", [128, 8, NOWN], BF16)
            k.stack = old
            with k.phase():
                self.xt_ring = k.ring("xt", 2, [128, D], F32)
                self.xs_ring = k.ring("xs", 2, [128, D], BF16)
                self.st_ring = k.ring("st", 4, [128, 16], F32)
                self.wb_ring = k.ring("wb", 2, [128, 16, 512], BF16)
                self.wstg_ring = k.ring("wstg", 2, [128, 4, 512], F32)
                self.etmp = k.ring("etmp", 3, [128, 512], F32)
                uT_ring = k.ring("uT", 2, [128, 1280 + 4 * 38], F32)
                acc_ring = k.ring("acc", 2, [128, NOWN], F32)
                ptmp = k.sb("ptmp", [128, NOWN], F32)
                scT = k.sb("scT", [128, 8, 120], F32)
                uP = k.sb("uP", [128, 8, 30], F32)
                uS = k.sb("uS", [128, 8, 4, 30], F32)
                sc_in = k.sb("sc_in", [120, 1024], F32)
                k.dma(sc_in, sconv_d)
                for j in range(8):
                    pb = self.ps.next()
                    k.tr(pb[:, 0:120], sc_in[:, j * 128:(j + 1) * 128], self.identf[:120, :120])
                    k.cp("dve", scT[:, j, :], pb[:, 0:120])
                for s in range(8):
                    self.front(xB[s, 32:160, :], 128, xTB[:, :, s * 160 + 32:s * 160 + 160])
                    self.front(xB[s, 0:32, :], 32, xTB[:, :, s * 160:s * 160 + 32])
                self.front(xS[:, :], 32, xTB[:, :, 1280:1312])
                GROUPS = [(0, 480), (480, 480), (960, 352)]
                for half in range(2):
                    wa = self.wload(C_UA + 512 * half, 512)
                    wbk = self.wload(C_UB + 512 * half, 512)
                    for sub in range(4):
                        j = half * 4 + sub
                        uT = uT_ring.next()
                        for (c0, n) in GROUPS:
                            pa = self.ps.next()
                            pbb = self.ps.next()
                            for kc in range(16):
                                k.mm(pa[:, :n], wa[:, kc, sub * 128:(sub + 1) * 128], xTB[:, kc, c0:c0 + n],
                                     start=(kc == 0), stop=(kc == 15))
                            for kc in range(16):
                                k.mm(pbb[:, :n], wbk[:, kc, sub * 128:(sub + 1) * 128], xTB[:, kc, c0:c0 + n],
                                     start=(kc == 0), stop=(kc == 15))
                            e = self.etmp.next()
                            k.act(e[:, :n], pbb[:, :n], AF.Exp, scale=-1.0)
                            k.ts("dve", e[:, :n], e[:, :n], 1.0, None, ALU.add)
                            k.recip(e[:, :n], e[:, :n])
                            if c0 < 960:
                                k.tt("dve", uT[:, c0:c0 + n], pa[:, :n], e[:, :n], ALU.mult)
                            else:
                                k.tt("dve", uT[:, 960:1280], pa[:, :320], e[:, :320], ALU.mult)
                                k.tt("dve", uT[:, 1280:1432].re("p (b t) -> p b t", t=38)[:, :, 30:38],
                                     pa[:, 320:352].re("p (b t) -> p b t", t=8),
                                     e[:, 320:352].re("p (b t) -> p b t", t=8), ALU.mult)
                        eng = "pool" if (j % 4 == 3) else "dve"
                        k.cp(eng, uT[:, 1280:1432].re("p (b t) -> p b t", t=38)[:, :, 0:30],
                             scT[:, j, :].re("p (b t) -> p b t", t=30))
                        k.cp(eng, uP[:, j, :], uT[:, 7 * 160 + 130:7 * 160 + 160])
                        k.cp(eng, uS[:, j, :, :], uT[:, 1280:1432].re("p (b t) -> p b t", t=38)[:, :, 8:38])
                        acc = acc_ring.next()
                        a_own = acc[:, 0:1024].re("p (s t) -> p s t", t=128)
                        a_smp = acc[:, 1024:1056].re("p (b t) -> p b t", t=8)
                        u_own = uT[:, 0:1280].re("p (s t) -> p s t", t=160)
                        u_smp = uT[:, 1280:1432].re("p (b t) -> p b t", t=38)
                        wd = self.cvT[:, j, :]
                        k.ts(eng, a_own, u_own[:, :, 2:130], wd[:, 0:1], wd[:, 31:32], ALU.mult, ALU.add)
                        k.ts(eng, a_smp, u_smp[:, :, 0:8], wd[:, 0:1], wd[:, 31:32], ALU.mult, ALU.add)
                        for kk in range(1, 31):
                            if eng == "dve":
                                k.stt(eng, a_own, u_own[:, :, 2 + kk:130 + kk], wd[:, kk:kk + 1], a_own, ALU.mult, ALU.add)
                                k.stt(eng, a_smp, u_smp[:, :, kk:kk + 8], wd[:, kk:kk + 1], a_smp, ALU.mult, ALU.add)
                            else:
                                tp = ptmp[:, 0:1024].re("p (s t) -> p s t", t=128)
                                tq = ptmp[:, 1024:1056].re("p (b t) -> p b t", t=8)
                                k.ts(eng, tp, u_own[:, :, 2 + kk:130 + kk], wd[:, kk:kk + 1], 1.0, ALU.mult, ALU.mult)
                                k.tt(eng, a_own, a_own, tp, ALU.add)
                                k.ts(eng, tq, u_smp[:, :, kk:kk + 8], wd[:, kk:kk + 1], 1.0, ALU.mult, ALU.mult)
                                k.tt(eng, a_smp, a_smp, tq, ALU.add)
                        k.cp(eng, yT[:, j, :], acc)
                cps = k.sb("cps", [128, 1024], F32)
                for j in range(8):
                    pb = self.ps.next()
                    k.tr(pb[0:30, 0:128], uP[:, j, :], self.identf)
                    k.cp("dve", cps[0:30, j * 128:(j + 1) * 128], pb[0:30, 0:128])
                k.dma(o_convp, cps[0:30, :])
                cps2 = k.sb("cps2", [128, 1024], F32)
                for j in range(8):
                    pb = self.ps.next()
                    k.tr(pb[0:120, 0:128], uS[:, j, :, :].re("p b t -> p (b t)"), self.identf)
                    k.cp("dve", cps2[0:120, j * 128:(j + 1) * 128], pb[0:120, 0:128])
                k.dma(o_convs, cps2[0:120, :])

            if self.stages < 2:
                k.flush(final=True)
                return
            with k.phase():
                self.st_ring = k.ring("st", 4, [128, 16], F32)
                self.wb_ring = k.ring("wb", 2, [128, 16, 512], BF16)
                self.wstg_ring = k.ring("wstg", 2, [128, 4, 512], F32)
                self.etmp = k.ring("etmp", 4, [128, 512], F32)
                ones_b = k.sb("ones_b", [128, 128], BF16)
                k.memset("dve", ones_b, 1.0 / 1024)
                mean = k.sb("mean", [128, NOWN], F32)
                rstd = k.sb("rstd", [128, NOWN], F32)
                ysq_ring = k.ring("ysq", 2, [128, 512], BF16)
                LG = [(0, 512), (512, 512), (1024, 32)]
                for (c0, n) in LG:
                    pm = self.ps.next()
                    pq = self.ps.next()
                    for j in range(8):
                        k.mm(pm[:, :n], ones_b, yT[:, j, c0:c0 + n], start=(j == 0), stop=(j == 7))
                    for j in range(8):
                        ysq = ysq_ring.next()
                        k.tt("pool", ysq[:, :n], yT[:, j, c0:c0 + n], yT[:, j, c0:c0 + n], ALU.mult)
                        k.mm(pq[:, :n], ones_b, ysq[:, :n], start=(j == 0), stop=(j == 7))
                    k.cp("act", mean[:, c0:c0 + n], pm[:, :n])
                    e = self.etmp.next()
                    k.tt("dve", e[:, :n], mean[:, c0:c0 + n], mean[:, c0:c0 + n], ALU.mult)
                    k.tt("dve", e[:, :n], pq[:, :n], e[:, :n], ALU.subtract)
                    k.act(e[:, :n], e[:, :n], AF.Ln, bias=self.epsc[:, 0:1])
                    k.act(rstd[:, c0:c0 + n], e[:, :n], AF.Exp, scale=-0.5)
                for j in range(8):
                    for (c0, n) in LG:
                        t = self.etmp.next()
                        k.tt("dve", t[:, :n], yT[:, j, c0:c0 + n], mean[:, c0:c0 + n], ALU.subtract)
                        k.tt("dve", t[:, :n], t[:, :n], rstd[:, c0:c0 + n], ALU.mult)
                        k.ts("dve", t[:, :n], t[:, :n], self.cvT[:, j, 32:33], self.cvT[:, j, 33:34], ALU.mult, ALU.add)
                        self.silu_from("dve", yT[:, j, c0:c0 + n], t[:, :n], 128, n)
                Wp = k.sb("Wp", [128, 8, 1024], BF16)
                for hh in range(2):
                    for q4 in range(2):
                        stg = self.wstg_ring.next()
                        k.dma(stg, w_pw2[q4 * 512:(q4 + 1) * 512, hh * 512:(hh + 1) * 512].re("(a p) c -> p a c", p=128))
                        k.cp("pool", Wp[:, q4 * 4:(q4 + 1) * 4, hh * 512:(hh + 1) * 512], stg)
                szc_ring = k.ring("szc", 2, [128, NOWN], BF16)
                mixc_ring = k.ring("mixc", 2, [128, 512], BF16)
                GROUPS = [(0, 480), (480, 480), (960, 352)]
                for half in range(2):
                    wz = self.wload(C_ZC + 512 * half, 512)
                    for sub in range(4):
                        e_ = half * 4 + sub
                        szc = szc_ring.next()
                        for gi, (c0, n) in enumerate(GROUPS):
                            pz = self.ps.next()
                            for kc in range(16):
                                k.mm(pz[:, :n], wz[:, kc, sub * 128:(sub + 1) * 128], xTB[:, kc, c0:c0 + n],
                                     start=(kc == 0), stop=(kc == 15))
                            ns = 3 if gi < 2 else 2
                            zt = self.etmp.next()
                            self.silu_from("dve", zt[:, :n], pz[:, :n], 128, n)
                            k.cp("pool", szc[:, gi * 384:gi * 384 + ns * 128].re("p (s t) -> p s t", t=128),
                                 zt[:, :ns * 160].re("p (s t) -> p s t", t=160)[:, :, 32:160])
                            if gi == 2:
                                k.cp("pool", szc[:, 1024:1056], zt[:, 320:352])
                        for (c0, n) in LG:
                            po = self.ps.next()
                            for j in range(8):
                                k.mm(po[:, :n], Wp[:, j, e_ * 128:(e_ + 1) * 128], yT[:, j, c0:c0 + n],
                                     start=(j == 0), stop=(j == 7))
                            mc = mixc_ring.next()
                            k.stt("dve", mc[:, :n], po[:, :n], self.cvT[:, e_, 34:35], szc[:, c0:c0 + n], ALU.add, ALU.mult)
                            k.dma(mixT[e_ * 128:(e_ + 1) * 128, c0:c0 + n], mc[:, :n], q="pool")

                stg_ring = k.ring("ostg", 3, [128, 512], F32)
                bfo_ring = k.ring("bfo", 3, [128, 512], BF16)
                units = [(s, s * 160 + 32, 128) for s in range(8)] + [(8 + b, 1280 + 8 * b, 8) for b in range(4)]

                tm_list = []

                def tm_block(c0, wd, handler):
                    tm_list.append((c0, wd, handler))

                def tm_run(wb, wd, handler):
                    for (u, col, R) in units:
                        pb = self.ps.next()
                        for kc in range(16):
                            k.mm(pb[:R, :wd], xTB[:, kc, col:col + R], wb[:, kc, :wd], start=(kc == 0), stop=(kc == 15))
                        handler(u, R, pb)

                def to_T(src_bf, R, nh, dst):
                    pt = self.ps.next().cast(BF16)
                    for h in range(nh):
                        k.tr(pt[:, h * R:(h + 1) * R], src_bf[:R, h * 128:(h + 1) * 128], self.ident[:R, :R])
                    k.cp("act", dst, pt[:, 0:nh * R].re("p (h t) -> p h t", h=nh))

                for g in range(2):
                    def h_q(u, R, pb, g=g):
                        bf = bfo_ring.next()
                        self.normed(pb[:R, :512], R, 4, 0, out_bf=bf[:R, :])
                        if u < 8:
                            qsb = bfo_ring.next()
                            to_T(bf, R, 4, qsb[:, :].re("p (h t) -> p h t", h=4))
                            k.dma(qT_scr[u, :, g * 4:(g + 1) * 4, :], qsb[:, :].re("p (h t) -> p h t", h=4), q="act")
                        else:
                            to_T(bf, R, 4, q2T[:, u - 8, g * 4:(g + 1) * 4, :])
                    tm_block(C_Q + 512 * g, 512, h_q)

                def h_kcvc(u, R, pb):
                    stg = stg_ring.next()
                    k.cp("act", stg[:R, :], pb[:R, :512])
                    if u < 8:
                        k.dma(o_kc[u], stg[:, 0:256], q="act"); k.dma(o_vc[u], stg[:, 256:512], q="act")
                    else:
                        b = u - 8
                        k.dma(o_kc2[b * 8:(b + 1) * 8, :], stg[:8, 0:256]); k.dma(o_vc2[b * 8:(b + 1) * 8, :], stg[:8, 256:512])
                tm_block(C_KC, 512, h_kcvc)

                def mk_kv(gidx, oK, oV, oK2, oV2, KT_own, V_own, K2T, V2):
                    def h(u, R, pb):
                        stg = stg_ring.next()
                        bf = bfo_ring.next()
                        self.normed(pb[:R, 0:256], R, 2, gidx, out_f32=stg[:R, 0:256], out_bf=bf[:R, 0:256])
                        k.cp("act", stg[:R, 256:512], pb[:R, 256:512])
                        if u < 8:
                            k.dma(oK[u], stg[:, 0:256], q="pool"); k.dma(oV[u], stg[:, 256:512], q="act")
                            to_T(bf, R, 2, KT_own[:, u, :, :])
                            k.cp("pool", V_own[:, u, :, 0:128], stg[:, 256:512].re("p (g d) -> p g d", g=2))
                        else:
                            b = u - 8
                            if oK2 is not None:
                                k.dma(oK2[b * 8:(b + 1) * 8, :], stg[:8, 0:256]); k.dma(oV2[b * 8:(b + 1) * 8, :], stg[:8, 256:512])
                            else:
                                self.win_new_rows(b, stg)
                            to_T(bf, R, 2, K2T[:, b, :, :])
                            k.cp("pool", V2[:8, b, :, 0:128], stg[:8, 256:512].re("p (g d) -> p g d", g=2))
                    return h
                tm_block(C_KS, 512, mk_kv(1, o_ks, o_vs, o_ks2, o_vs2, ksT_own, Vs_own, ks2T, V2s))
                def _wnr(b, stg):
                    k.dma(o_kwin[b, 504:512, :], stg[:8, 0:256])
                    k.dma(o_vwin[b, 504:512, :], stg[:8, 256:512])
                self.win_new_rows = _wnr
                tm_block(C_KW, 512, mk_kv(2, o_kw, o_vw, None, None, KwT_own, Vw_own, kw2T, V2w))

                def h_gt(u, R, pb):
                    e = self.etmp.next()
                    k.act(e[:R, :24], pb[:R, :24], AF.Exp, scale=-1.0)
                    k.ts("dve", e[:R, :24], e[:R, :24], 1.0, None, ALU.add)
                    if u < 8:
                        k.recip(gates[:, u, :], e[:, :24])
                    else:
                        k.recip(gates2[:8, u - 8, :], e[:8, :24])
                tm_block(C_GT, 24, h_gt)

                for half in range(2):
                    def h_zn(u, R, pb, half=half):
                        bf = bfo_ring.next()
                        self.silu_from("dve", bf[:R, :], pb[:R, :512], R, 512)
                        r0 = u * 128 if u < 8 else 1024 + (u - 8) * 8
                        k.dma(sz_scr[r0:r0 + R, half * 512:(half + 1) * 512], bf[:R, :], q="pool")
                    tm_block(C_ZN + 512 * half, 512, h_zn)
                wb_next = self.wload(tm_list[0][0], tm_list[0][1])
                for i_, (c0_, wd_, hd_) in enumerate(tm_list):
                    wb_cur = wb_next
                    if i_ + 1 < len(tm_list):
                        wb_next = self.wload(tm_list[i_ + 1][0], tm_list[i_ + 1][1])
                    tm_run(wb_cur, wd_, hd_)

        if self.stages >= 3:
            self.build2()
        k.flush(final=True)


    def finish_branch(self, OB, oacc, s, g, b, rd_keep=None):
        k = self.k
        gates = self.pers["gates"]
        st = self.st_ring.next()
        for r in range(4):
            h = g * 4 + r
            k.ts("dve", st[:, r:r + 1], OB[r][:, 128:129], 1e-37, None, ALU.max)
        k.recip(st[:, 4:8], st[:, 0:4])
        if rd_keep is not None:
            k.cp("dve", rd_keep, st[:, 4:8])
        for r in range(4):
            h = g * 4 + r
            k.tt("dve", st[:, 8 + r:9 + r], st[:, 4 + r:5 + r], gates[:, s, h * 3 + b:h * 3 + b + 1], ALU.mult)
            oh = oacc[:, h * 128:(h + 1) * 128]
            if b == 0:
                k.ts("dve", oh, OB[r][:, 0:128], st[:, 8 + r:9 + r], None, ALU.mult)
            else:
                k.stt("dve", oh, OB[r][:, 0:128], st[:, 8 + r:9 + r], oh, ALU.mult, ALU.add)

    def build2(self):
        k = self.k
        inp = self.inp
        P = self.pers
        dr = self.dr
        mixT, qT_scr, sz_scr, xB, xS, o_y, o_y2 = (dr[n] for n in ("mixT", "qT_scr", "sz_scr", "xB", "xS", "o_y", "o_y2"))
        ksT_own, Vs_own, KwT_own, Vw_own = P["ksT_own"], P["Vs_own"], P["KwT_own"], P["Vw_own"]
        xA = inp("xA", [T, D])
        xH = inp("xH", [8, 4, 128, D])
        w_out = inp("w_out", [D, D])
        w1 = [inp("w_cmp_k1", [32, 128, 128]), inp("w_cmp_v1", [32, 128, 128])]
        w2 = [inp("w_cmp_k2", [128, 128]), inp("w_cmp_v2", [128, 128])]
        pe = [inp("pe_cmp_k", [32, 128]), inp("pe_cmp_v", [32, 128])]
        selbias_d = inp("selbias", [128, 288 * 8])
        dbias_d = inp("dbias", [128, 8])
        winbias_d = inp("winbias", [128, 8 * 5 * 8])
        cmpbias_d = inp("cmpbias", [128, 8 * 4 * 8])
        cmpmask_d = inp("cmpmask", [128, 8 * 4 * 128], BF16)
        forced_d = inp("forced", [128, 8 * 128])
        notown_d = inp("notown", [128, 8 * 128])
        ctab_d = inp("ctab", [128, 4 * 128], BF16)
        i4_d = inp("i4", [128, 512], BF16)
        tri_d = inp("tri", [128, 2 * 128], BF16)
        kcT_scr = self.scr("kcT_scr", [4, 128, T], BF16)
        kwT_scr = self.scr("kwT_scr", [8, 128, 2, 512], BF16)
        vw_scr = self.scr("vw_scr", [8, 128, 4, 2, 130], BF16)

        with contextlib.ExitStack() as cst:
            old = k.stack
            k.stack = cst
            KsT = k.sb("KsT", [128, 2, T], BF16)
            Vs = k.sb("Vs", [128, 64, 2, 130], BF16)
            KcT = k.sb("KcT", [128, 2, 512], BF16)
            Vc = k.sb("Vc", [128, 4, 2, 130], BF16)
            k.stack = old
            with k.phase():
                self.xt_ring = k.ring("xt", 3, [128, D], F32)
                self.xs_ring = k.ring("xs", 3, [128, D], BF16)
                self.st_ring = k.ring("st", 8, [128, 16], F32)
                self.wb_ring = k.ring("wb", 3, [128, 16, 512], BF16)
                self.wstg_ring = k.ring("wstg", 1, [128, 4, 512], F32)
                self.etmp = k.ring("etmp", 2, [128, 512], F32)
                for t_ in (Vs, Vc):
                    k.memset("dve", t_.re("p a g c -> p (a g) c")[:, :, 128:129], 1.0)
                    k.memset("dve", t_.re("p a g c -> p (a g) c")[:, :, 129:130], 0.0)
                W0 = self.wload(C_KC, 512)
                W1 = self.wload(C_KS, 512)
                Ww = self.wload(C_KW, 512)
                xT_ring = k.ring("xT", 2, [128, 16, 128], BF16)
                kcsb_ring = k.ring("kcsb", 2, [128, 4, 128], BF16)
                pab_ring = k.ring("pab", 2, [128, 512], BF16)
                bf_ring = k.ring("bfa", 3, [128, 256], BF16)
                kwsb_ring = k.ring("kwsb", 2, [128, 2, 128], BF16)
                vwsb_ring = k.ring("vwsb", 2, [128, 2, 130], BF16)
                for v_ in vwsb_ring.items:
                    k.memset("dve", v_[:, :, 128:129], 1.0)
                    k.memset("dve", v_[:, :, 129:130], 0.0)
                nA = 64 if self.stages >= 4 else 2
                jobs = []

                def mkA(t):
                    box = {}

                    def sL():
                        box["xt"] = self.front_load(xA[t * 128:(t + 1) * 128, :], 128)

                    def sF():
                        box["xs"] = self.front_norm(box["xt"], 128)

                    def s0():
                        box["xT"] = xT_ring.next()
                        self.front_T(box["xs"], 128, box["xT"])

                    def s1():
                        xT = box["xT"]
                        pa = self.ps.next()
                        for kc in range(16):
                            k.mm(pa, xT[:, kc, :], W0[:, kc, :], start=(kc == 0), stop=(kc == 15))
                        pab = pab_ring.next()
                        k.cp("act", pab, pa)
                        pb = self.ps.next()
                        for kc in range(16):
                            k.mm(pb, xT[:, kc, :], W1[:, kc, :], start=(kc == 0), stop=(kc == 15))
                        pt2 = self.ps.next().cast(BF16)
                        for cb in range(4):
                            k.tr(pt2[:, cb * 128:(cb + 1) * 128], pab[:, cb * 128:(cb + 1) * 128], self.ident)
                        kcsb = kcsb_ring.next()
                        k.cp("dve", kcsb, pt2[:, 0:512].re("p (a t) -> p a t", a=4))
                        k.dma(kcT_scr[:, :, t * 128:(t + 1) * 128].re("a p t -> p a t"), kcsb, q="pool")
                        box["bf"] = bf_ring.next()
                        self.normed(pb[:, 0:256], 128, 2, 1, out_bf=box["bf"])
                        k.cp("dve", Vs[:, t, :, 0:128], pb[:, 256:512].re("p (g d) -> p g d", g=2))

                    def s2():
                        bf = box["bf"]
                        pt = self.ps.next().cast(BF16)
                        for g in range(2):
                            k.tr(pt[:, g * 128:(g + 1) * 128], bf[:, g * 128:(g + 1) * 128], self.ident)
                        k.cp("act", KsT[:, :, t * 128:(t + 1) * 128], pt[:, 0:256].re("p (g t) -> p g t", g=2))
                    return [sL, sF, s0, s1, s2]

                def mkH(s_, o):
                    box = {}

                    def sL():
                        box["xt"] = self.front_load(xH[s_, o], 128)

                    def sF():
                        box["xs"] = self.front_norm(box["xt"], 128)

                    def s0():
                        box["xT"] = xT_ring.next()
                        self.front_T(box["xs"], 128, box["xT"])

                    def s1():
                        xT = box["xT"]
                        pb = self.ps.next()
                        for kc in range(16):
                            k.mm(pb, xT[:, kc, :], Ww[:, kc, :], start=(kc == 0), stop=(kc == 15))
                        box["bf"] = bf_ring.next()
                        self.normed(pb[:, 0:256], 128, 2, 2, out_bf=box["bf"])
                        vwsb = vwsb_ring.next()
                        k.cp("dve", vwsb[:, :, 0:128], pb[:, 256:512].re("p (g d) -> p g d", g=2))
                        k.dma(vw_scr[s_, :, o, :, :], vwsb, q="pool")

                    def s2():
                        bf = box["bf"]
                        pt = self.ps.next().cast(BF16)
                        for g in range(2):
                            k.tr(pt[:, g * 128:(g + 1) * 128], bf[:, g * 128:(g + 1) * 128], self.ident)
                        kwsb = kwsb_ring.next()
                        k.cp("act", kwsb, pt[:, 0:256].re("p (g t) -> p g t", g=2))
                        k.dma(kwT_scr[s_, :, :, o * 128:(o + 1) * 128], kwsb, q="act")
                    return [sL, sF, s0, s1, s2]
                for t in range(nA):
                    jobs.append(mkA(t))
                for s_ in range(8):
                    for o in range(4):
                        jobs.append(mkH(s_, o))
                run_pipeline(jobs)

            with k.phase():
                self.st_ring = k.ring("st", 4, [128, 16], F32)
                self.etmp = k.ring("etmp", 4, [128, 512], F32)
                stgW = k.sb("stgW", [128, 32, 128], F32)
                W1b = [k.sb("W1k", [128, 32, 128], BF16), k.sb("W1v", [128, 32, 128], BF16)]
                W2b = [k.sb("W2k", [128, 128], BF16), k.sb("W2v", [128, 128], BF16)]
                peT = k.sb("peT", [128, 2, 32], BF16)
                biasv = k.sb("biasv", [128, 2], F32)
                gkc = k.sb("gkc", [128, 1], F32)
                ones_m = k.sb("ones_m", [128, 128], BF16)
                k.memset("dve", ones_m, 1.0 / 128)
                for kv in range(2):
                    k.dma(stgW, w1[kv].re("c d e -> d c e"))
                    k.cp("pool", W1b[kv], stgW)
                    stg2 = self.etmp.next()
                    k.dma(stg2[:, 0:128], w2[kv])
                    k.cp("pool", W2b[kv], stg2[:, 0:128])
                    pes = self.etmp.next()
                    k.dma(pes[0:32, 0:128], pe[kv])
                    pt = self.ps.next()
                    k.tr(pt[:, 0:32], pes[0:32, 0:128], self.identf[:32, :32])
                    k.cp("dve", peT[:, kv, :], pt[:, 0:32])
                    pbias = self.ps.next()
                    for c in range(32):
                        k.mm(pbias[:, 0:2], W1b[kv][:, c, :], peT[:, kv, c:c + 1].bc([128, 2]), start=(c == 0), stop=(c == 31))
                    k.cp("dve", biasv[:, kv:kv + 1], pbias[:, 0:1])
                pt = self.ps.next()
                k.tr(pt[:, 0:128], self.gvec[:, 3, :], self.identf)
                k.cp("dve", gkc, pt[:, 0:1])
                self.cmp_w = (W1b, W2b, biasv, gkc, ones_m)
                kc_ring = k.ring("kcl", 2, [128, T], BF16)
                fb_sb = k.sb("fb_sb", [128, 512], F32)
                hid_ring = k.ring("hid", 2, [128, 512], BF16)
                sq_ring = k.ring("sq", 2, [128, 512], BF16)
                for kv in range(2):
                    for g in range(2):
                        kcl = kc_ring.next()
                        k.dma(kcl, kcT_scr[kv * 2 + g])
                        kview = kcl.re("p (n c) -> p n c", c=16)
                        pfa = self.ps.next()
                        pfb = self.ps.next()
                        for c in range(16):
                            k.mm(pfa, W1b[kv][:, c, :], kview[:, :, c], start=(c == 0), stop=(c == 15))
                        for c in range(16):
                            k.mm(pfb, W1b[kv][:, 16 + c, :], kview[:, :, c], start=(c == 0), stop=(c == 15))
                        k.cp("act", fb_sb, pfb)
                        t_ = self.etmp.next()
                        k.tt("dve", t_[:, 0:511], pfa[:, 0:511], fb_sb[:, 1:512], ALU.add)
                        h = hid_ring.next()
                        k.memset("dve", h[:, 511:512], 0.0)
                        self.silu_from("dve", h[:, 0:511], t_[:, 0:511], 128, 511, bias=biasv[:, kv:kv + 1])
                        if kv == 0:
                            pk = self.ps.next()
                            k.mm(pk, W2b[0], h)
                            sq = sq_ring.next()
                            k.act(sq, pk, AF.Square)
                            pss = self.ps.next()
                            k.mm(pss, ones_m, sq)
                            e = self.etmp.next()
                            k.act(e, pss, AF.Ln, bias=self.epsc[:, 0:1])
                            k.act(e, e, AF.Exp, scale=-0.5)
                            k.tt("dve", e, pk, e, ALU.mult)
                            k.ts("dve", KcT[:, g, :], e, gkc[:, 0:1], None, ALU.mult)
                        else:
                            pv = self.ps.next()
                            for nt in range(4):
                                k.mm(pv[:, nt * 128:(nt + 1) * 128], h[:, nt * 128:(nt + 1) * 128], W2b[1])
                            k.cp("dve", Vc[:, :, g, 0:128], pv.re("p (a d) -> p a d", a=4))

            with k.phase():
                self.st_ring = k.ring("st", 6, [128, 16], F32)
                self.etmp = k.ring("etmp", 3, [128, 512], F32)

                def tab(name, d, shape, dt=F32, pat=None, **kw):
                    t_ = k.sb(name, shape, dt)
                    k.dma(t_, d.re(pat, **kw) if pat else d)
                    return t_
                selbias = tab("selbias", selbias_d, [128, 288, 8], pat="p (a h) -> p a h", h=8)
                dbias = tab("dbias", dbias_d, [128, 8])
                winbias = tab("winbias", winbias_d, [128, 8, 5, 8], pat="p (s o h) -> p s o h", s=8, o=5)
                cmpbias = tab("cmpbias", cmpbias_d, [128, 8, 4, 8], pat="p (s o h) -> p s o h", s=8, o=4)
                cmpmask = tab("cmpmask", cmpmask_d, [128, 8, 4, 128], BF16, pat="p (s o q) -> p s o q", s=8, o=4)
                forced = tab("forced", forced_d, [128, 8, 128], pat="p (s j) -> p s j", s=8)
                notown = tab("notown", notown_d, [128, 8, 128], pat="p (s j) -> p s j", s=8)
                ctab = tab("ctab", ctab_d, [128, 4, 128], BF16, pat="p (a j) -> p a j", a=4)
                i4 = tab("i4", i4_d, [128, 512], BF16)
                tri = tab("tri", tri_d, [128, 2, 128], BF16, pat="p (a q) -> p a q", a=2)
                QT_ring = k.ring("QT", 2, [128, 8, 128], BF16)
                sz_ring = k.ring("sz", 2, [128, 1024], BF16)
                KwH_ring = k.ring("KwH", 2, [128, 2, 512], BF16)
                VwH_ring = k.ring("VwH", 2, [128, 4, 2, 130], BF16)
                ET_items = []
                for i_ in range(4):
                    base = k.sb("ET%d" % i_, [128, 512], BF16)
                    toks = [Tok() for _ in range(4)]
                    ET_items.append((V(base.ap, toks), [V(base.ap[:, r * 128:(r + 1) * 128], [toks[r]]) for r in range(4)]))
                ET_ring = Ring(ET_items)
                ETc_ring = k.ring("ETc", 2, [128, 4, 512], BF16)
                oacc_ring = k.ring("oacc", 2, [128, 1024], F32)
                mexp_ring = k.ring("mexp", 2, [128, 1024], BF16)
                sc_ring = k.ring("sc", 4, [128, 128], F32)
                m8_ring = k.ring("m8", 2, [128, 16], F32)
                rdc_ring = k.ring("rdc", 2, [128, 4], F32)
                mix_ring = k.ring("mix", 2, [128, 1024], BF16)
                mixa_ring = k.ring("mixa", 2, [128, 8, 128], BF16)
                OB = self.ps.items[0:4]
                STr = Ring(self.ps.items[4:6])
                MS = Ring(self.ps.items[6:8])
                offs = [sum(8 * s2 + 8 for s2 in range(s)) for s in range(8)]
                nS = 8 if self.stages >= 4 else 1
                for s in range(nS):
                    QT = QT_ring.next()
                    k.dma(QT, qT_scr[s])
                    sz = sz_ring.next()
                    k.dma(sz, sz_scr[s * 128:(s + 1) * 128, :])
                    KwH = KwH_ring.next()
                    k.dma(KwH, kwT_scr[s])
                    VwH = VwH_ring.next()
                    k.dma(VwH, vw_scr[s])
                    oacc = oacc_ring.next()
                    for g in range(2):
                        Qg = QT[:, g * 4:(g + 1) * 4, :].re("p h t -> p (h t)")
                        ETc0 = ETc_ring.next()
                        ctoks = [Tok() for _ in range(4)]
                        ETc = V(ETc0.ap, ctoks + ETc0.toks)
                        jobs = []

                        def mk_c(nt, g=g, Qg=Qg, ETc0=ETc0, ctoks=ctoks):
                            box = {}

                            def s0():
                                box["st"] = STr.next()
                                k.mm(box["st"], KcT[:, g, nt * 128:(nt + 1) * 128], Qg)

                            def s1():
                                ev = V(ETc0.ap[:, nt, :], [ctoks[nt]])
                                for r in range(4):
                                    h = g * 4 + r
                                    k.act(ev[:, r * 128:(r + 1) * 128], box["st"][:, r * 128:(r + 1) * 128], AF.Exp,
                                          bias=cmpbias[:, s, nt, h:h + 1])
                                e3 = ev.re("p (r q) -> p r q", r=4)
                                k.tt("pool", e3, e3, cmpmask[:, s, nt, :].us(1).bc([128, 4, 128]), ALU.mult)
                            return [s0, s1]
                        for nt in range(4):
                            jobs.append(mk_c(nt))
                        run_pipeline(jobs)
                        for r in range(4):
                            for nt in range(4):
                                k.mm(OB[r][:, 0:129], ETc[:, nt, r * 128:(r + 1) * 128], Vc[:, nt, g, 0:129],
                                     start=(nt == 0), stop=(nt == 3))
                        imp = MS.next()
                        for r in range(4):
                            for nt in range(4):
                                k.mm(imp[:, r * 128:(r + 1) * 128], ETc[:, nt, r * 128:(r + 1) * 128], ctab[:, nt, :],
                                     start=(nt == 0), stop=(nt == 3))
                        rdc = rdc_ring.next()
                        self.finish_branch(OB, oacc, s, g, 0, rd_keep=rdc)
                        sc = sc_ring.next()
                        sc2 = sc_ring.next()
                        m8 = m8_ring.next()
                        k.ts("dve", sc, imp[:, 0:128], rdc[:, 0:1], None, ALU.mult)
                        for r in range(1, 4):
                            k.stt("dve", sc, imp[:, r * 128:(r + 1) * 128], rdc[:, r:r + 1], sc, ALU.mult, ALU.add)
                        k.tt("dve", sc, sc, forced[:, s, :], ALU.add)
                        k.max8(m8[:, 0:8], sc)
                        k.mrep(sc2, m8[:, 0:8], sc, -3.0e38)
                        k.max8(m8[:, 8:16], sc2)
                        k.ts("dve", sc2, sc, m8[:, 15:16], None, ALU.is_ge)
                        k.tt("dve", sc2, sc2, notown[:, s, :], ALU.mult)
                        k.ts("dve", sc2, sc2, 30000.0, -30000.0, ALU.mult, ALU.add)
                        nT = 8 * s + 8
                        cur = {}

                        def mk_tile(lhsK, t_mask, biasfn, trimask, Vt, start, stop, g=g, Qg=Qg, cur=cur):
                            box = {}

                            def s0():
                                if t_mask is not None and t_mask % 8 == 0:
                                    cur["mexp"] = mexp_ring.next()
                                    c8 = t_mask // 8
                                    k.cp("pool", cur["mexp"].re("p (j e) -> p j e", e=64),
                                         sc2[:, c8 * 16:(c8 + 1) * 16].us(2).bc([128, 16, 64]))
                                st_ = STr.next()
                                box["st"] = st_
                                k.mm(st_, lhsK, Qg, start=True, stop=(t_mask is None))
                                if t_mask is not None:
                                    k.mm(st_, cur["mexp"][:, (t_mask % 8) * 128:(t_mask % 8 + 1) * 128], i4, start=False, stop=True)

                            def s1():
                                whole, parts = ET_ring.next()
                                box["ET"] = parts
                                for r in range(4):
                                    k.act(parts[r], box["st"][:, r * 128:(r + 1) * 128], AF.Exp, bias=biasfn(g * 4 + r))
                                if trimask is not None:
                                    e3 = whole.re("p (r q) -> p r q", r=4)
                                    k.tt("pool", e3, e3, tri[:, trimask, :].us(1).bc([128, 4, 128]), ALU.mult)

                            def s2():
                                for r in range(4):
                                    k.mm(OB[r][:, 0:129], box["ET"][r], Vt, start=start, stop=stop)
                            return [s0, s1, s2]

                        jobs = []
                        for t in range(nT):
                            jobs.append(mk_tile(KsT[:, g, t * 128:(t + 1) * 128], t,
                                                (lambda h, t=t: selbias[:, offs[s] + t, h:h + 1]), None,
                                                Vs[:, t, g, 0:129], (t == 0), False))
                        jobs.append(mk_tile(ksT_own[:, s, g, :], None, (lambda h: dbias[:, h:h + 1]), 0,
                                            Vs_own[:, s, g, 0:129], False, True))
                        run_pipeline(jobs)
                        self.finish_branch(OB, oacc, s, g, 1)
                        jobs = []
                        for o in range(5):
                            lhs = KwH[:, g, o * 128:(o + 1) * 128] if o < 4 else KwT_own[:, s, g, :]
                            Vt = VwH[:, o, g, 0:129] if o < 4 else Vw_own[:, s, g, 0:129]
                            jobs.append(mk_tile(lhs, None, (lambda h, o=o: winbias[:, s, o, h:h + 1]),
                                                (1 if o == 0 else (0 if o == 4 else None)), Vt, (o == 0), (o == 4)))
                        run_pipeline(jobs)
                        self.finish_branch(OB, oacc, s, g, 2)
                    mix = mix_ring.next()
                    k.tt("dve", mix, oacc, sz, ALU.mult)
                    mixa = mixa_ring.next()
                    for half in range(2):
                        pt = MS.next().cast(BF16)
                        for e4 in range(4):
                            e_ = half * 4 + e4
                            k.tr(pt[:, e4 * 128:(e4 + 1) * 128], mix[:, e_ * 128:(e_ + 1) * 128], self.ident)
                        k.cp("act", mixa[:, half * 4:(half + 1) * 4, :], pt[:, 0:512].re("p (a t) -> p a t", a=4))
                    k.dma(mixT[1024:2048, s * 128:(s + 1) * 128].re("(a p) t -> p a t", p=128), mixa, q="act")

        if self.stages >= 6:
            self.build3()

        with k.phase():
            self.wb_ring = k.ring("wb", 2, [128, 16, 512], BF16)
            self.wstg_ring = k.ring("wstg", 2, [128, 4, 512], F32)
            mixsb = k.sb("mixsb", [128, 16, NOWN], BF16)
            for e in range(16):
                k.dma(mixsb[:, e, :], mixT[e * 128:(e + 1) * 128, :])
            xr_ring = k.ring("xr", 3, [128, 512], F32)
            yo_ring = k.ring("yo", 3, [128, 512], F32)
            Wo_next = self.wload(0, 512, scale_g=False, src=w_out)
            for ct in range(4):
                Wo = Wo_next
                if ct + 1 < 4:
                    Wo_next = self.wload((ct + 1) * 512, 512, scale_g=False, src=w_out)
                for u in range(9):
                    R = 128 if u < 8 else 32
                    col = u * 128
                    pb = self.ps.next()
                    for e in range(16):
                        k.mm(pb[:R, :], mixsb[:, e, col:col + R], Wo[:, e, :], start=(e == 0), stop=(e == 15))
                    xr = xr_ring.next()
                    src = xB[u, 32:160, ct * 512:(ct + 1) * 512] if u < 8 else xS[:, ct * 512:(ct + 1) * 512]
                    k.dma(xr[:R], src)
                    yo = yo_ring.next()
                    k.tt("dve", yo[:R], pb[:R, :], xr[:R], ALU.add)
                    dst = o_y[u][:, ct * 512:(ct + 1) * 512] if u < 8 else o_y2[:, ct * 512:(ct + 1) * 512]
                    k.dma(dst, yo[:R], q="act")


    def build3(self):
        k = self.k
        inp = self.inp
        P = self.pers
        dr = self.dr
        mixT, sz_scr = dr["mixT"], dr["sz_scr"]
        q2T, ks2T, kw2T, V2s, V2w, gates2 = (P[n] for n in ("q2T", "ks2T", "kw2T", "V2s", "V2w", "gates2"))
        pt_d = inp("ptab", [4, 128], I32)
        caches = [inp(n, [NPHYS * 8, 4096]) for n in ("cache_k_cmp", "cache_v_cmp", "cache_k_slc", "cache_v_slc")]
        skw_d = inp("skw", [4, 512, 256])
        svw_d = inp("svw", [4, 512, 256])
        o_kwin, o_vwin = dr["o_kwin"], dr["o_vwin"]
        w1 = [self.din["w_cmp_k1"], self.din["w_cmp_v1"]]
        w2 = [self.din["w_cmp_k2"], self.din["w_cmp_v2"]]
        pe = [self.din["pe_cmp_k"], self.din["pe_cmp_v"]]
        marr_d = inp("marr", [128, 8])
        c2bias_d = inp("c2bias", [128, 64])
        c2tab_d = inp("c2tab", [128, 8 * 256], BF16)
        forced2_d = inp("forced2", [8, 256])
        abias_d = inp("abias", [128, 128 * 8])
        swbias_d = inp("swbias", [128, 32])
        swmask_d = inp("swmask", [128, 32], BF16)
        tbias_d = inp("tbias", [8, 8])
        tmask_d = inp("tmask", [8, 8], BF16)
        i48_d = inp("i48", [8, 32], BF16)
        selr_d = inp("selr", [32, 32])
        selsum_d = inp("selsum", [32, 8])

        with k.phase():
            self.st_ring = k.ring("st", 6, [128, 16], F32)
            self.etmp = k.ring("etmp", 4, [128, 512], F32)

            def tab(name, d, shape, dt=F32, pat=None, **kw):
                t_ = k.sb(name, shape, dt)
                k.dma(t_, d.re(pat, **kw) if pat else d)
                return t_
            marr = tab("marr", marr_d, [128, 8])
            c2bias = tab("c2bias", c2bias_d, [128, 8, 8], pat="p (m h) -> p m h", m=8)
            c2tab = tab("c2tab", c2tab_d, [128, 8, 256], BF16, pat="p (m j) -> p m j", m=8)
            forced2 = tab("forced2", forced2_d, [8, 256])
            abias = tab("abias", abias_d, [128, 128, 8], pat="p (r h) -> p r h", h=8)
            swbias = tab("swbias", swbias_d, [128, 4, 8], pat="p (w h) -> p w h", h=8)
            swmask = tab("swmask", swmask_d, [128, 4, 8], BF16, pat="p (w i) -> p w i", i=8)
            tbias = tab("tbias", tbias_d, [8, 8])
            tmask = tab("tmask", tmask_d, [8, 8], BF16)
            i48 = tab("i48", i48_d, [8, 32], BF16)
            selr = tab("selr", selr_d, [32, 4, 8], pat="p (r i) -> p r i", r=4)
            selsum = tab("selsum", selsum_d, [32, 8])
            X_ring = k.ring("X", 3, [128, 16, 2, 128], F32)
            W1b = [k.sb("W1k", [128, 32, 128], BF16), k.sb("W1v", [128, 32, 128], BF16)]
            W2b = [k.sb("W2k", [128, 128], BF16), k.sb("W2v", [128, 128], BF16)]
            peT = k.sb("peT", [128, 2, 32], BF16)
            biasv = k.sb("biasv", [128, 2], F32)
            gkc = k.sb("gkc", [128, 1], F32)
            ones_m = k.sb("ones_m", [128, 128], BF16)
            k.memset("dve", ones_m, 1.0 / 128)
            for kv in range(2):
                stgW = X_ring.next().re("p a g d -> p (a g) d")
                k.dma(stgW, w1[kv].re("c d e -> d c e"))
                k.cp("pool", W1b[kv], stgW)
                stg2 = self.etmp.next()
                k.dma(stg2[:, 0:128], w2[kv])
                k.cp("pool", W2b[kv], stg2[:, 0:128])
                pes = self.etmp.next()
                k.dma(pes[0:32, 0:128], pe[kv])
                pt = self.ps.next()
                k.tr(pt[:, 0:32], pes[0:32, 0:128], self.identf[:32, :32])
                k.cp("dve", peT[:, kv, :], pt[:, 0:32])
                pbias = self.ps.next()
                for c in range(32):
                    k.mm(pbias[:, 0:2], W1b[kv][:, c, :], peT[:, kv, c:c + 1].bc([128, 2]), start=(c == 0), stop=(c == 31))
                k.cp("dve", biasv[:, kv:kv + 1], pbias[:, 0:1])
            pt = self.ps.next()
            k.tr(pt[:, 0:128], self.gvec[:, 3, :], self.identf)
            k.cp("dve", gkc, pt[:, 0:1])
            pti = k.sb("pti", [128, 4], I32)
            for bl in range(4):
                k.dma(pti[:, bl:bl + 1], pt_d[bl].re("(p o) -> p o", o=1))
            ptf = k.sb("ptf", [128, 4], F32)
            k.cp("dve", ptf, pti)
            k.ts("dve", ptf, ptf, 8.0, None, ALU.mult)
            idxf = k.sb("idxf", [128, 4, 8], F32)
            idx8 = k.sb("idx8", [128, 4, 8], I32)
            for bl in range(4):
                k.ts("dve", idxf[:, bl, :], marr, ptf[:, bl:bl + 1], None, ALU.add)
            k.cp("dve", idx8, idxf)

            XT_ring = k.ring("XT", 2, [128, 16, 2, 128], BF16)
            Vb_ring = k.ring("Vb", 2, [128, 16, 2, 130], BF16)
            for v_ in Vb_ring.items:
                k.memset("dve", v_.re("p a g c -> p (a g) c")[:, :, 128:129], 1.0)
                k.memset("dve", v_.re("p a g c -> p (a g) c")[:, :, 129:130], 0.0)
            Fsb = k.sb("Fsb", [128, 2, 2, 8, 128], F32)
            hid_ring = k.ring("hid", 2, [128, 1024], BF16)
            sq_ring = k.ring("sq", 2, [128, 512], BF16)
            Kc2T = k.sb("Kc2T", [128, 2, 1024], BF16)
            Vc2 = k.sb("Vc2", [128, 8, 2, 130], BF16)
            k.memset("dve", Vc2.re("p a g c -> p (a g) c")[:, :, 128:129], 1.0)
            k.memset("dve", Vc2.re("p a g c -> p (a g) c")[:, :, 129:130], 0.0)
            stsb_ring = k.ring("stsb", 2, [128, 512], F32)
            ET_ring = k.ring("ET2", 2, [128, 512], BF16)
            o2_ring = k.ring("o2acc", 1, [8, 1024], F32)
            on_ring = k.ring("onorm", 2, [32, 128], F32)
            impn = k.sb("impn", [32, 256], F32)
            sc_ring = k.ring("sc2", 3, [8, 264], F32)
            m8 = k.sb("m8b", [8, 16], F32)
            Mb2h = k.sb("Mb2h", [8, 2, 2, 128], BF16)
            skw = k.sb("skw", [128, 4, 2, 128], F32)
            svw = k.sb("svw", [128, 4, 2, 128], F32)
            Vwb = k.sb("Vwb", [128, 4, 2, 130], BF16)
            k.memset("dve", Vwb.re("p a g c -> p (a g) c")[:, :, 128:129], 1.0)
            k.memset("dve", Vwb.re("p a g c -> p (a g) c")[:, :, 129:130], 0.0)
            KwT2 = k.sb("KwT2", [128, 2, 4, 128], BF16)
            sz2_ring = k.ring("sz2", 1, [8, 1024], BF16)
            mix2_ring = k.ring("mix2", 1, [8, 1024], BF16)
            mix2a_ring = k.ring("mix2a", 1, [128, 8, 8], BF16)
            OC = self.ps.items[0]
            OG = self.ps.items[1:3]
            STr = Ring(self.ps.items[3:5])
            MS = Ring(self.ps.items[5:8])

            def q2g(bl, g):
                return q2T[:, bl, g * 4:(g + 1) * 4, :].re("p h i -> p (h i)")

            def finish2(Og, o2acc, bl, g, b, keep=None):
                st = self.st_ring.next()
                k.ts("dve", st[:32, 0:1], Og[:32, 128:129], 1e-37, None, ALU.max)
                k.recip(st[:32, 1:2], st[:32, 0:1])
                if keep is not None:
                    k.cp("dve", keep, st[:32, 1:2])
                on = on_ring.next()
                k.ts("dve", on, Og[:32, 0:128], st[:32, 1:2], None, ALU.mult)
                pr = MS.next()
                for r in range(4):
                    k.mm(pr[:8, r * 128:(r + 1) * 128], selr[:, r, :], on)
                for r in range(4):
                    h = g * 4 + r
                    oh = o2acc[:, h * 128:(h + 1) * 128]
                    gt = gates2[:8, bl, h * 3 + b:h * 3 + b + 1]
                    if b == 0:
                        k.ts("dve", oh, pr[:8, r * 128:(r + 1) * 128], gt, None, ALU.mult)
                    else:
                        k.stt("dve", oh, pr[:8, r * 128:(r + 1) * 128], gt, oh, ALU.mult, ALU.add)

            def transposes_to(Xv, dstT, evac_i, gs=(0, 1)):
                n = 0
                for g in gs:
                    for r4 in range(4):
                        pb = MS.next()
                        for a in range(4):
                            k.tr(pb[:, a * 128:(a + 1) * 128], Xv[:, r4 * 4 + a, g, :], self.identf)
                        k.cp("act" if (n + evac_i) % 2 == 0 else "dve", dstT[:, r4 * 4:(r4 + 1) * 4, g, :],
                             pb.re("p (a q) -> p a q", a=4))
                        n += 1

            for bl in range(4):
                def post_cmp(kv):
                    for g in range(2):
                        h = hid_ring.next()
                        for half in range(2):
                            t_ = self.etmp.next()
                            if half == 0:
                                k.tt("dve", t_.re("p (m q) -> p m q", m=4), Fsb[:, 0, g, 0:4, :], Fsb[:, 1, g, 1:5, :], ALU.add)
                            else:
                                k.tt("dve", t_[:, 0:384].re("p (m q) -> p m q", m=3), Fsb[:, 0, g, 4:7, :], Fsb[:, 1, g, 5:8, :], ALU.add)
                                k.tt("dve", t_[:, 384:511], Fsb[:, 0, g, 7, 0:127], Fsb[:, 1, g, 0, 1:128], ALU.add)
                                k.memset("dve", t_[:, 511:512], 0.0)
                            self.silu_from("dve", h[:, half * 512:(half + 1) * 512], t_, 128, 512, bias=biasv[:, kv:kv + 1])
                        k.memset("dve", h[:, 1023:1024], 0.0)
                        if kv == 0:
                            for half in range(2):
                                hs = h[:, half * 512:(half + 1) * 512]
                                pk = MS.next()
                                k.mm(pk, W2b[0], hs)
                                sq = sq_ring.next()
                                k.act(sq, pk, AF.Square)
                                pss = MS.next()
                                k.mm(pss, ones_m, sq)
                                e = self.etmp.next()
                                k.act(e, pss, AF.Ln, bias=self.epsc[:, 0:1])
                                k.act(e, e, AF.Exp, scale=-0.5)
                                k.tt("dve", e, pk, e, ALU.mult)
                                k.ts("dve", Kc2T[:, g, half * 512:(half + 1) * 512], e, gkc[:, 0:1], None, ALU.mult)
                        else:
                            for half in range(2):
                                pv = MS.next()
                                for a in range(4):
                                    m = half * 4 + a
                                    k.mm(pv[:, a * 128:(a + 1) * 128], h[:, m * 128:(m + 1) * 128], W2b[1])
                                k.cp("dve", Vc2[:, half * 4:(half + 1) * 4, g, 0:128], pv.re("p (a d) -> p a d", a=4))

                def mk_cmp(kv, m, bl=bl):
                    box = {}

                    def sG():
                        box["X"] = X_ring.next()
                        k.gather(box["X"].re("p a g d -> p (a g d)"), caches[kv], idx8[:, bl, m:m + 1])

                    def s0():
                        box["XT"] = XT_ring.next()
                        transposes_to(box["X"], box["XT"], m)

                    def s1():
                        XT = box["XT"]
                        pf = STr.next()
                        for ab in range(2):
                            for g in range(2):
                                for c in range(16):
                                    k.mm(pf[:, (ab * 2 + g) * 128:(ab * 2 + g + 1) * 128], W1b[kv][:, ab * 16 + c, :],
                                         XT[:, c, g, :], start=(c == 0), stop=(c == 15))
                        k.cp("dve" if m % 2 == 0 else "act", Fsb[:, :, :, m, :], pf.re("p (a g q) -> p a g q", a=2, g=2))
                        if m == 7:
                            post_cmp(kv)
                    return [sG, s0, s1]
                run_pipeline([mk_cmp(kv, m) for kv in range(2) for m in range(8)])
                o2acc = o2_ring.next()
                k.dma(skw.re("p w g d -> p w (g d)"), skw_d[bl].re("(w p) c -> p w c", p=128))
                k.dma(svw.re("p w g d -> p w (g d)"), svw_d[bl].re("(w p) c -> p w c", p=128))
                for (src, dst) in ((skw, o_kwin), (svw, o_vwin)):
                    k.dma(dst[bl, 0:120, :], src[8:128, 0, :, :].re("p g d -> p (g d)"))
                    k.dma(dst[bl, 120:504, :].re("(w p) c -> p w c", p=128), src[:, 1:4, :, :].re("p w g d -> p w (g d)"))
                for g in range(2):
                    pb = MS.next()
                    for w in range(4):
                        k.tr(pb[:, w * 128:(w + 1) * 128], skw[:, w, g, :], self.identf)
                    k.cp("act", KwT2[:, g, :, :], pb.re("p (w q) -> p w q", w=4))
                k.cp("pool", Vwb[:, :, :, 0:128], svw)
                for g in range(2):
                    qg = q2g(bl, g)
                    st_ = STr.next()
                    for m in range(8):
                        k.mm(st_[:, m * 32:(m + 1) * 32], Kc2T[:, g, m * 128:(m + 1) * 128], qg)
                    sb_ = stsb_ring.next()
                    k.tt("dve", sb_[:, 0:256].re("p (m r i) -> p m r i", m=8, r=4), st_[:, 0:256].re("p (m r i) -> p m r i", m=8, r=4),
                         c2bias[:, :, g * 4:(g + 1) * 4].us(3).bc([128, 8, 4, 8]), ALU.add)
                    ETc = ET_ring.next()
                    k.act(ETc[:, 0:256], sb_[:, 0:256], AF.Exp)
                    for m in range(8):
                        k.mm(OC[:32, 0:129], ETc[:, m * 32:(m + 1) * 32], Vc2[:, m, g, 0:129], start=(m == 0), stop=(m == 7))
                    pim = MS.next()
                    for m in range(8):
                        k.mm(pim[:32, 0:256], ETc[:, m * 32:(m + 1) * 32], c2tab[:, m, :], start=(m == 0), stop=(m == 7))
                    rdk = self.st_ring.next()
                    finish2(OC, o2acc, bl, g, 0, keep=rdk[:32, 15:16])
                    k.ts("dve", impn, pim[:32, 0:256], rdk[:32, 15:16], None, ALU.mult)
                    pi2 = MS.next()
                    k.mm(pi2[:8, 0:256], selsum, impn)
                    sc = sc_ring.next()
                    sc2 = sc_ring.next()
                    k.tt("dve", sc[:, 0:256], pi2[:8, 0:256], forced2, ALU.add)
                    k.max8(m8[:, 0:8], sc[:, 0:256])
                    k.mrep(sc2[:, 0:256], m8[:, 0:8], sc[:, 0:256], -3.0e38)
                    k.max8(m8[:, 8:16], sc2[:, 0:256])
                    k.ts("dve", sc2[:, 0:256], sc[:, 0:256], m8[:, 14:15], None, ALU.is_ge)
                    k.ts("dve", sc2[:, 0:256], sc2[:, 0:256], 30000.0, -30000.0, ALU.mult, ALU.add)
                    k.cp("dve", Mb2h[:, g, :, :], sc2[:, 0:256].re("i (p h) -> i h p", h=2))
                shared = {}

                def mk_sel(m, g, bl=bl, shared=shared):
                    box = {}
                    qg = q2g(bl, g)
                    hb = m // 4

                    def sG():
                        if g == 0:
                            Xk = X_ring.next()
                            k.gather(Xk.re("p a g d -> p (a g d)"), caches[2], idx8[:, bl, m:m + 1])
                            Xv = X_ring.next()
                            k.gather(Xv.re("p a g d -> p (a g d)"), caches[3], idx8[:, bl, m:m + 1])
                            shared[m] = [Xk, Xv, None, None]

                    def s0():
                        if g == 0:
                            Xk, Xv = shared[m][0], shared[m][1]
                            Vb = Vb_ring.next()
                            k.cp("act", Vb[:, :, :, 0:128], Xv)
                            KT = XT_ring.next()
                            transposes_to(Xk, KT, m)
                            shared[m][2] = Vb
                            shared[m][3] = KT
                        KT = shared[m][3]
                        st_ = STr.next()
                        box["st"] = st_
                        for row in range(16):
                            k.mm(st_[:, row * 32:(row + 1) * 32], KT[:, row, g, :], qg, start=True, stop=False)
                            k.mm(st_[:, row * 32:(row + 1) * 32], Mb2h[:, g, hb, :], i48, start=False, stop=True)

                    def s1():
                        sb_ = stsb_ring.next()
                        k.tt("dve", sb_.re("p (a r i) -> p a r i", a=16, r=4), box["st"].re("p (a r i) -> p a r i", a=16, r=4),
                             abias[:, m * 16:(m + 1) * 16, g * 4:(g + 1) * 4].us(3).bc([128, 16, 4, 8]), ALU.add)
                        box["ET"] = ET_ring.next()
                        k.act(box["ET"], sb_, AF.Exp)

                    def s2():
                        Vb = shared[m][2]
                        for row in range(16):
                            k.mm(OG[g][:32, 0:129], box["ET"][:, row * 32:(row + 1) * 32], Vb[:, row, g, 0:129],
                                 start=(m == 0 and row == 0), stop=False)
                    return [sG, s0, s1, s2]
                run_pipeline([mk_sel(m, g) for m in range(8) for g in range(2)])
                for g in range(2):
                    qg = q2g(bl, g)
                    st_ = STr.next()
                    k.mm(st_[:8, 0:32], ks2T[:, bl, g, :], qg)
                    sb_ = stsb_ring.next()
                    k.tt("dve", sb_[:8, 0:32].re("p (r i) -> p r i", r=4), st_[:8, 0:32].re("p (r i) -> p r i", r=4),
                         tbias[:, g * 4:(g + 1) * 4].us(2).bc([8, 4, 8]), ALU.add)
                    ET = ET_ring.next()
                    k.act(ET[:8, 0:32], sb_[:8, 0:32], AF.Exp)
                    k.tt("dve", ET[:8, 0:32].re("p (r i) -> p r i", r=4), ET[:8, 0:32].re("p (r i) -> p r i", r=4),
                         tmask.us(1).bc([8, 4, 8]), ALU.mult)
                    k.mm(OG[g][:32, 0:129], ET[:8, 0:32], V2s[:8, bl, g, 0:129], start=False, stop=True)
                    finish2(OG[g], o2acc, bl, g, 1)
                    st_ = STr.next()
                    for w in range(4):
                        k.mm(st_[:, w * 32:(w + 1) * 32], KwT2[:, g, w, :], qg)
                    sb_ = stsb_ring.next()
                    k.tt("dve", sb_[:, 0:128].re("p (w r i) -> p w r i", w=4, r=4), st_[:, 0:128].re("p (w r i) -> p w r i", w=4, r=4),
                         swbias[:, :, g * 4:(g + 1) * 4].us(3).bc([128, 4, 4, 8]), ALU.add)
                    ET = ET_ring.next()
                    k.act(ET[:, 0:128], sb_[:, 0:128], AF.Exp)
                    k.tt("dve", ET[:, 0:128].re("p (w r i) -> p w r i", w=4, r=4), ET[:, 0:128].re("p (w r i) -> p w r i", w=4, r=4),
                         swmask.us(2).bc([128, 4, 4, 8]), ALU.mult)
                    for w in range(4):
                        k.mm(OC[:32, 0:129], ET[:, w * 32:(w + 1) * 32], Vwb[:, w, g, 0:129], start=(w == 0), stop=False)
                    st_ = STr.next()
                    k.mm(st_[:8, 0:32], kw2T[:, bl, g, :], qg)
                    sb_ = stsb_ring.next()
                    k.tt("dve", sb_[:8, 0:32].re("p (r i) -> p r i", r=4), st_[:8, 0:32].re("p (r i) -> p r i", r=4),
                         tbias[:, g * 4:(g + 1) * 4].us(2).bc([8, 4, 8]), ALU.add)
                    ET = ET_ring.next()
                    k.act(ET[:8, 0:32], sb_[:8, 0:32], AF.Exp)
                    k.tt("dve", ET[:8, 0:32].re("p (r i) -> p r i", r=4), ET[:8, 0:32].re("p (r i) -> p r i", r=4),
                         tmask.us(1).bc([8, 4, 8]), ALU.mult)
                    k.mm(OC[:32, 0:129], ET[:8, 0:32], V2w[:8, bl, g, 0:129], start=False, stop=True)
                    finish2(OC, o2acc, bl, g, 2)
                sz2 = sz2_ring.next()
                k.dma(sz2, sz_scr[1024 + bl * 8:1024 + bl * 8 + 8, :])
                mix2 = mix2_ring.next()
                k.tt("dve", mix2, o2acc, sz2, ALU.mult)
                pt = MS.next().cast(BF16)
                for e_ in range(8):
                    k.tr(pt[:, e_ * 8:(e_ + 1) * 8], mix2[:8, e_ * 128:(e_ + 1) * 128], self.ident[:8, :8])
                mix2a = mix2a_ring.next()
                k.cp("act", mix2a, pt[:, 0:64].re("p (a t) -> p a t", a=8))
                k.dma(mixT[1024:2048, 1024 + bl * 8:1024 + bl * 8 + 8].re("(a p) t -> p a t", p=128), mix2a)


def static_tables():
    f32 = np.float32
    sl = np.array(SLOPES, f32)
    kk = np.arange(128)
    dbias = (sl[None, :] * (kk[:, None] - 127)).astype(f32)
    n = np.arange(512)
    j = np.arange(128)
    o = n[:, None] - 4 * j[None, :]
    C = np.where((o == -1) | (o == 3), 0.5, np.where((o >= 0) & (o <= 2), 1.0, 0.0))
    C[511:, :] = 0.0
    ctab = C.reshape(4, 128, 128).transpose(1, 0, 2).reshape(128, 512)
    i4 = np.tile(np.eye(128, dtype=f32), (1, 4))
    tri = np.stack([(kk[:, None] <= kk[None, :]), (kk[:, None] > kk[None, :])], axis=1).astype(f32).reshape(128, 256)
    return {"dbias": dbias, "ctab": ctab.astype(NPBF), "i4": i4.astype(NPBF), "tri": tri.astype(NPBF)}


def core_tables(c):
    f32 = np.float32
    sl = np.array(SLOPES, f32)
    kk = np.arange(128)
    selb = np.zeros((128, 288, 8), f32)
    winb = np.zeros((128, 8, 5, 8), f32)
    cmpb = np.zeros((128, 8, 4, 8), f32)
    cmpm = np.zeros((128, 8, 4, 128), f32)
    forced = np.zeros((128, 8, 128), f32)
    notown = np.zeros((128, 8, 128), f32)
    off = 0
    q = np.arange(128)
    j = np.arange(128)
    for s in range(8):
        qb = c + 8 * s
        for t in range(8 * s + 8):
            if t < qb:
                selb[:, off + t, :] = sl[None, :] * (kk[:, None] - 127 - 128 * (qb - t))
        off += 8 * s + 8
        for o in range(5):
            if o == 4 or qb - 4 + o >= 0:
                winb[:, s, o, :] = sl[None, :] * (kk[:, None] - 127 - 128 * (4 - o))
            else:
                winb[:, s, o, :] = NEG
        for nt in range(4):
            n = nt * 128 + kk
            end = 16 * n + 31
            d = end - (128 * qb + 127)
            v = sl[None, :] * d[:, None]
            bad = (n >= 511) | (d > 0)
            v[bad, :] = NEG
            cmpb[:, s, nt, :] = v
            cmpm[:, s, nt, :] = ((end[:, None] <= 128 * qb + q[None, :]) & (n[:, None] < 511)).astype(f32)
        cur = (128 * qb + q) // 64
        fm = (j[None, :] == 0) | (j[None, :] == cur[:, None]) | (j[None, :] == cur[:, None] - 1)
        inv = j[None, :] > cur[:, None]
        forced[:, s, :] = np.where(fm, 1e30, np.where(inv, -1e30, 0.0))
        notown[:, s, :] = (j[None, :] < 2 * qb).astype(f32)
    return {"selbias": selb.reshape(128, -1), "winbias": winb.reshape(128, -1), "cmpbias": cmpb.reshape(128, -1),
            "cmpmask": cmpm.reshape(128, -1).astype(NPBF), "forced": forced.reshape(128, -1),
            "notown": notown.reshape(128, -1)}


def sample_tables():
    f32 = np.float32
    sl = np.array(SLOPES, f32)
    p = np.arange(128)
    ref = PAST + 7
    marr = np.tile(np.arange(8, dtype=f32)[None, :], (128, 1))
    m = np.arange(8)
    n = 8 * p[:, None] + m[None, :]
    end = 16 * n + 31
    c2b = (sl[None, None, :] * (end[:, :, None] - ref)).astype(f32)
    c2b[n >= 1023] = NEG
    j = np.arange(256)
    o = n[:, :, None] - 4 * j[None, None, :]
    C = np.where((o == -1) | (o == 3), 0.5, np.where((o >= 0) & (o <= 2), 1.0, 0.0))
    C[n >= 1023] = 0.0
    forced2 = np.zeros((8, 256), f32)
    forced2[:, 0] = 1e30
    forced2[:, 255] = 1e30
    row = np.arange(128)
    ab = (sl[None, None, :] * ((128 * p[:, None] + row[None, :])[:, :, None] - ref)).astype(f32)
    w = np.arange(4)
    wpos = w[None, :] * 128 + p[:, None]
    swb = (sl[None, None, :] * (wpos[:, :, None] - 519)).astype(f32)
    i = np.arange(8)
    swm = (wpos[:, :, None] > i[None, None, :]).astype(f32)
    tb = (sl[None, :] * (i[:, None] - 7)).astype(f32)
    tm = (i[:, None] <= i[None, :]).astype(f32)
    i48 = np.tile(np.eye(8, dtype=f32), (1, 4))
    selr = np.eye(32, dtype=f32)
    selsum = np.tile(np.eye(8, dtype=f32), (4, 1))
    return {"marr": marr, "c2bias": c2b.reshape(128, 64), "c2tab": C.reshape(128, -1).astype(NPBF), "forced2": forced2,
            "abias": ab.reshape(128, -1), "swbias": swb.reshape(128, 32), "swmask": swm.reshape(128, 32).astype(NPBF),
            "tbias": tb, "tmask": tm.astype(NPBF), "i48": i48.astype(NPBF), "selr": selr, "selsum": selsum}


_PROG = {}


def get_prog(stages):
    if stages not in _PROG:
        _PROG[stages] = Prog(stages)
    return _PROG[stages]


STAGES = 6


def kernel(**inputs):
    f32 = np.float32
    xp = np.asarray(inputs["x_prompt"], f32)[0]
    xsm = np.asarray(inputs["x_sample"], f32)
    prog = get_prog(STAGES)
    ident = np.eye(128, dtype=f32)
    gcol = np.ascontiguousarray(np.asarray(inputs["g_norm"], f32).reshape(16, 128).T)
    cvec = np.concatenate([np.asarray(inputs["w_dw"], f32), np.asarray(inputs["b_dw"], f32)[None],
                           np.asarray(inputs["ln_g"], f32)[None], np.asarray(inputs["ln_b"], f32)[None],
                           np.asarray(inputs["b_pw2"], f32)[None]], axis=0)
    gvec = np.stack([np.asarray(inputs[n], f32) for n in ("g_q", "g_k_slc", "g_k_win", "g_k_cmp")], axis=0)
    xpad = np.concatenate([np.zeros((512, D), f32), xp], axis=0)
    in_maps = []
    statics = static_tables()
    for c in range(NCORES):
        xB = np.stack([xpad[512 + (c + 8 * s) * 128 - 32: 512 + (c + 8 * s) * 128 + 128] for s in range(8)], axis=0)
        m = {
            "xB": np.ascontiguousarray(xB),
            "xS": np.ascontiguousarray(xsm[4 * c:4 * c + 4].reshape(32, D)),
            "w_in": np.asarray(inputs["w_in"], f32),
            "gcol": gcol, "cvec": np.ascontiguousarray(cvec), "gvec": np.ascontiguousarray(np.broadcast_to(gvec.reshape(1, 512), (128, 512))),
            "w_pw2": np.asarray(inputs["w_pw2"], f32),
            "ident": ident.astype(NPBF), "identf": ident,
            "sconv": np.ascontiguousarray(np.asarray(inputs["state_conv"], f32)[4 * c:4 * c + 4].reshape(120, 1024)),
        }
        if STAGES >= 3:
            m["xA"] = xp
            m["xH"] = np.ascontiguousarray(np.stack([np.stack([xpad[512 + (c + 8 * s - 4 + o) * 128: 512 + (c + 8 * s - 3 + o) * 128]
                                                     for o in range(4)], axis=0) for s in range(8)], axis=0))
            for nm in ("w_out", "w_cmp_k1", "w_cmp_v1", "w_cmp_k2", "w_cmp_v2", "pe_cmp_k", "pe_cmp_v"):
                m[nm] = np.asarray(inputs[nm], f32)
            m.update(statics)
            m.update(core_tables(c))
        if STAGES >= 6:
            m["ptab"] = np.ascontiguousarray(np.asarray(inputs["page_table"], np.int32)[4 * c:4 * c + 4])
            for nm in ("cache_k_cmp", "cache_v_cmp", "cache_k_slc", "cache_v_slc"):
                m[nm] = np.asarray(inputs[nm], f32).reshape(NPHYS * 8, 4096)
            m["skw"] = np.ascontiguousarray(np.asarray(inputs["state_k_win"], f32)[4 * c:4 * c + 4].reshape(4, 512, 256))
            m["svw"] = np.ascontiguousarray(np.asarray(inputs["state_v_win"], f32)[4 * c:4 * c + 4].reshape(4, 512, 256))
            m.update(sample_tables())
        in_maps.append({kk: m[kk] for kk in prog.din})
    res = run_bass_kernel_spmd(prog.k.nc, in_maps, core_ids=list(range(NCORES)))
    R = res.results

    def gat(name):
        return [np.asarray(R[c][name]) for c in range(NCORES)]

    def prow(name, width):
        out = np.zeros((T, width), f32)
        arr = gat(name)
        for c in range(NCORES):
            for s in range(8):
                qb = c + 8 * s
                out[qb * 128:(qb + 1) * 128] = arr[c][s]
        return out

    def srow(name, width):
        return np.concatenate([a.reshape(4, 8, width) for a in gat(name)], axis=0)

    y_prompt = prow("o_y", D)[None]
    y_sample = srow("o_y2", D)
    kc = prow("o_kc", 256).reshape(1, T, 2, 128); vc = prow("o_vc", 256).reshape(1, T, 2, 128)
    ks = prow("o_ks", 256).reshape(1, T, 2, 128); vs = prow("o_vs", 256).reshape(1, T, 2, 128)
    kw = prow("o_kw", 256).reshape(1, T, 2, 128); vw = prow("o_vw", 256).reshape(1, T, 2, 128)
    conv_p = np.asarray(R[7]["o_convp"])[None]
    kc2 = srow("o_kc2", 256).reshape(32, 8, 2, 128); vc2 = srow("o_vc2", 256).reshape(32, 8, 2, 128)
    ks2 = srow("o_ks2", 256).reshape(32, 8, 2, 128); vs2 = srow("o_vs2", 256).reshape(32, 8, 2, 128)
    conv_s = np.concatenate([a.reshape(4, 30, 1024) for a in gat("o_convs")], axis=0)
    kwin2 = np.concatenate(gat("o_kwin"), axis=0).reshape(32, 512, 2, 128)
    vwin2 = np.concatenate(gat("o_vwin"), axis=0).reshape(32, 512, 2, 128)
    return (y_prompt, y_sample, kc, vc, ks, vs, kw[:, T - 512:], vw[:, T - 512:], conv_p,
            kc2, vc2, ks2, vs2, kwin2, vwin2, conv_s)
```

```python
import contextlib
import numpy as np
import ml_dtypes
import concourse.bass as bass
import concourse.mybir as mybir
from concourse.bass_utils import run_bass_kernel_spmd

F32 = mybir.dt.float32
BF16 = mybir.dt.bfloat16
I32 = mybir.dt.int32
AF = mybir.ActivationFunctionType
ALU = mybir.AluOpType
AX = mybir.AxisListType
NPBF = ml_dtypes.bfloat16

NCORES = 8
D = 2048
T = 8192
HD = 128
NQB = 64
NSLOT = 8
DIN = 6680
EPS = 1e-6
SCALE = HD ** -0.5
PAST = 16384
NPAGE = 128
NPHYS = 5120
SLOPES = [2.0 ** (-(h + 1)) for h in range(8)]
NEG = -30000.0
C_UA, C_UB, C_ZC, C_Q, C_KC, C_KS, C_KW, C_GT, C_ZN = 0, 1024, 2048, 3072, 4096, 4608, 5120, 5632, 5656
TOKC = 8 * 160 + 32
NOWN = 1024 + 32
NS_DMA = 8


class Tok:
    __slots__ = ("w", "r")

    def __init__(self):
        self.w = None
        self.r = {}


class V:
    def __init__(self, ap, toks):
        self.ap = ap
        self.toks = toks

    def __getitem__(self, i):
        return V(self.ap[i], self.toks)

    def re(self, pat, **kw):
        return V(self.ap.rearrange(pat, **kw), self.toks)

    def bc(self, shape):
        return V(self.ap.broadcast_to(shape), self.toks)

    def us(self, dim):
        return V(self.ap.unsqueeze(dim), self.toks)

    def cast(self, dt):
        return V(self.ap.bitcast(dt), self.toks)

    def tk(self, tok):
        return V(self.ap, [tok])


class Ring:
    def __init__(self, items):
        self.items = items
        self.i = 0

    def next(self):
        it = self.items[self.i % len(self.items)]
        self.i += 1
        return it


def run_pipeline(jobs):
    if not jobs:
        return
    ns = max(len(j) for j in jobs)
    for step in range(len(jobs) + ns - 1):
        for st in range(ns):
            t = step - st
            if 0 <= t < len(jobs) and st < len(jobs[t]):
                jobs[t][st]()

ENG = ("pe", "act", "dve", "pool", "sp")


class Rec:
    def __init__(self):
        self.nc = bass.Bass("TRN2", target_bir_lowering=False)
        self.gst = contextlib.ExitStack()
        self.ops = {e: [] for e in ENG}
        self.cnt = {e: 0 for e in ENG}
        self.seen = {e: {} for e in ENG}
        self.dman = {e: 0 for e in ENG}
        self.sems = {}
        self.slot_last = {}
        for e in ("pe", "act", "dve", "pool"):
            self.sems[e] = self.gst.enter_context(self.nc.semaphore("c_" + e))
        for q in ("sp", "pool", "act"):
            for i in range(NS_DMA):
                nm = "d_%s_%d" % (q, i)
                self.sems[nm] = self.gst.enter_context(self.nc.semaphore(nm))
        self.uid = 0
        self.stack = self.gst

    def sb(self, name, shape, dt):
        self.uid += 1
        t = self.stack.enter_context(self.nc.sbuf_tensor("%s_%d" % (name, self.uid), list(shape), dt))
        return V(t[tuple(slice(None) for _ in shape)], [Tok()])

    def ring(self, name, n, shape, dt):
        return Ring([self.sb(name + str(i), shape, dt) for i in range(n)])

    def psum(self, name):
        t = self.gst.enter_context(self.nc.psum_tensor(name, [128, 512], F32))
        return V(t[:, :], [Tok()])

    def dram(self, name, shape, dt, kind):
        t = self.nc.dram_tensor(name, list(shape), dt, kind=kind)
        return V(t.ap(), [Tok()])

    @contextlib.contextmanager
    def phase(self):
        old = self.stack
        with contextlib.ExitStack() as st:
            self.stack = st
            yield
            self.flush()
        self.stack = old

    def _rec(self, eng, fn, R, W, dma=False):
        waits = {}

        def need(dep):
            if dep is None:
                return
            k, v = dep
            if eng == "pe" and k == "pe":
                return
            if waits.get(k, 0) < v:
                waits[k] = v

        for t in R:
            need(t.w)
        for t in W:
            need(t.w)
            for k, v in t.r.items():
                need((k, v))
        if dma:
            n = self.dman[eng]
            self.dman[eng] += 1
            slot = "d_%s_%d" % (eng, n % NS_DMA)
            rnd = n // NS_DMA
            if rnd > 0:
                need((slot, 16 * rnd))
            dep = (slot, 16 * (rnd + 1))
            inc = (slot, 16)
            self.slot_last[slot] = dep[1]
        else:
            self.cnt[eng] += 1
            dep = (eng, self.cnt[eng])
            inc = (eng, 1)
        wl = []
        seen = self.seen[eng]
        for k, v in waits.items():
            if seen.get(k, 0) < v:
                seen[k] = v
                wl.append((k, v))
        self.ops[eng].append((fn, wl, inc))
        for t in R:
            if t.r.get(dep[0], 0) < dep[1]:
                t.r[dep[0]] = dep[1]
        for t in W:
            t.w = dep
            t.r = {}

    def flush(self, final=False):
        if final:
            wl = [(k, v) for k, v in self.slot_last.items()]
            self.ops["sp"].append((None, wl, None))
        ops_all = self.ops
        self.ops = {e: [] for e in ENG}
        sems = self.sems
        with self.nc.Block() as blk:
            for eng, deco in (("pe", blk.tensor), ("act", blk.scalar), ("dve", blk.vector),
                              ("pool", blk.gpsimd), ("sp", blk.sync)):
                ops = ops_all[eng]

                def body(e, ops=ops):
                    for fn, wl, inc in ops:
                        for k, v in wl:
                            e.wait_ge(sems[k], v)
                        if fn is not None:
                            fn(e).then_inc(sems[inc[0]], inc[1])

                deco(body)

    @staticmethod
    def _s(x):
        return x.ap if isinstance(x, V) else x

    @staticmethod
    def _t(*xs):
        out = []
        for x in xs:
            if isinstance(x, V):
                out += x.toks
        return out

    def mm(self, out, lhsT, rhs, start=True, stop=True):
        self._rec("pe", lambda e: e.matmul(out=out.ap, lhsT=lhsT.ap, rhs=rhs.ap, start=start, stop=stop),
                  self._t(lhsT, rhs), out.toks)

    def tr(self, out, in_, ident):
        self._rec("pe", lambda e: e.transpose(out=out.ap, in_=in_.ap, identity=ident.ap),
                  self._t(in_, ident), out.toks)

    def act(self, out, in_, func, bias=None, scale=None, accum=None):
        kw = {}
        if bias is not None:
            kw["bias"] = self._s(bias)
        if scale is not None:
            kw["scale"] = self._s(scale)
        if accum is not None:
            kw["accum_out"] = accum.ap
        self._rec("act", lambda e: e.activation(out=out.ap, in_=in_.ap, func=func, **kw),
                  self._t(in_, bias, scale), self._t(out, accum))

    def ts(self, eng, out, in0, s1, s2, op0, op1=None, accum=None):
        kw = {}
        if op1 is not None:
            kw["op1"] = op1
        if accum is not None:
            kw["accum_out"] = accum.ap
        a1, a2 = self._s(s1), self._s(s2)
        self._rec(eng, lambda e: e.tensor_scalar(out=out.ap, in0=in0.ap, scalar1=a1, scalar2=a2, op0=op0, **kw),
                  self._t(in0, s1, s2), self._t(out, accum))

    def tt(self, eng, out, in0, in1, op):
        self._rec(eng, lambda e: e.tensor_tensor(out=out.ap, in0=in0.ap, in1=in1.ap, op=op),
                  self._t(in0, in1), out.toks)

    def stt(self, eng, out, in0, sc, in1, op0, op1):
        a = self._s(sc)
        self._rec(eng, lambda e: e.scalar_tensor_tensor(out=out.ap, in0=in0.ap, scalar=a, in1=in1.ap, op0=op0, op1=op1),
                  self._t(in0, sc, in1), out.toks)

    def cp(self, eng, out, in_):
        if eng == "act":
            self._rec("act", lambda e: e.copy(out=out.ap, in_=in_.ap), in_.toks, out.toks)
        else:
            self._rec(eng, lambda e: e.tensor_copy(out=out.ap, in_=in_.ap), in_.toks, out.toks)

    def red(self, out, in_, op=ALU.add, axis=AX.X):
        self._rec("dve", lambda e: e.tensor_reduce(out=out.ap, in_=in_.ap, axis=axis, op=op), in_.toks, out.toks)

    def recip(self, out, in_):
        self._rec("dve", lambda e: e.reciprocal(out=out.ap, in_=in_.ap), in_.toks, out.toks)

    def memset(self, eng, out, val):
        self._rec(eng, lambda e: e.memset(out.ap, val), [], out.toks)

    def max8(self, out, in_):
        self._rec("dve", lambda e: e.max(out=out.ap, in_=in_.ap), in_.toks, out.toks)

    def mrep(self, out, rep, vals, imm):
        self._rec("dve", lambda e: e.match_replace(out=out.ap, in_to_replace=rep.ap, in_values=vals.ap, imm_value=imm),
                  self._t(rep, vals), out.toks)

    def dma(self, out, in_, q="sp"):
        self._rec(q, lambda e: e.dma_start(out=out.ap, in_=in_.ap), in_.toks, out.toks, dma=True)

    def gather(self, out, in_, idx):
        self._rec("pool", lambda e: e.indirect_dma_start(
            out=out.ap, out_offset=None, in_=in_.ap,
            in_offset=bass.IndirectOffsetOnAxis(ap=idx.ap, axis=0)),
            self._t(in_, idx), out.toks, dma=True)


class Prog:
    def __init__(self, stages):
        self.stages = stages
        k = self.k = Rec()
        self.din = {}
        self.dout = {}
        self.ps = Ring([k.psum("ps%d" % i) for i in range(8)])
        self.build()

    def inp(self, name, shape, dt=F32):
        v = self.k.dram(name, shape, dt, "ExternalInput")
        self.din[name] = v
        return v

    def outp(self, name, shape, dt=F32):
        v = self.k.dram(name, shape, dt, "ExternalOutput")
        self.dout[name] = v
        return v

    def scr(self, name, shape, dt):
        return self.k.dram(name, shape, dt, "Internal")

    def rstd_from_ss(self, st, R, n, inv_n):
        k = self.k
        k.act(st[:R, n:2 * n], st[:R, 0:n], AF.Ln, scale=inv_n, bias=self.epsc[:R, 0:1])
        k.act(st[:R, 2 * n:3 * n], st[:R, n:2 * n], AF.Exp, scale=-0.5)
        return st[:R, 2 * n:3 * n]

    def silu_from(self, eng, out, src, R, n, bias=None):
        k = self.k
        e = self.etmp.next()
        if bias is None:
            k.act(e[:R, :n], src, AF.Exp, scale=-1.0)
            k.ts("dve", e[:R, :n], e[:R, :n], 1.0, None, ALU.add)
            k.recip(e[:R, :n], e[:R, :n])
            k.tt("dve", out, src, e[:R, :n], ALU.mult)
        else:
            z = self.etmp.next()
            k.ts("dve", z[:R, :n], src, bias, None, ALU.add)
            k.act(e[:R, :n], z[:R, :n], AF.Exp, scale=-1.0)
            k.ts("dve", e[:R, :n], e[:R, :n], 1.0, None, ALU.add)
            k.recip(e[:R, :n], e[:R, :n])
            k.tt("dve", out, z[:R, :n], e[:R, :n], ALU.mult)

    def front(self, src, R, dst):
        xt = self.front_load(src, R)
        self.front_rest(xt, R, dst)

    def front_load(self, src, R):
        xt = self.xt_ring.next()
        self.k.dma(xt[:R], src)
        return xt

    def front_rest(self, xt, R, dst):
        self.front_T(self.front_norm(xt, R), R, dst)

    def front_norm(self, xt, R):
        k = self.k
        xs = self.xs_ring.next()
        st = self.st_ring.next()
        k.act(xs[:R], xt[:R], AF.Square, accum=st[:R, 0:1])
        rs = self.rstd_from_ss(st, R, 1, 1.0 / D)
        k.ts("dve", xs[:R], xt[:R], rs, None, ALU.mult)
        return xs

    def front_T(self, xs, R, dst):
        k = self.k
        for half in range(2):
            pb = self.ps.next().cast(BF16)
            for j in range(8):
                kc = half * 8 + j
                k.tr(pb[:, j * R:(j + 1) * R], xs[:R, kc * 128:(kc + 1) * 128], self.ident[:R, :R])
            k.cp("act" if half == 0 else "dve", dst[:, half * 8:(half + 1) * 8, :],
                 pb[:, 0:8 * R].re("p (a b) -> p a b", a=8))

    def wload(self, c0, wd, scale_g=True, src=None, nkc=16):
        k = self.k
        wb = self.wb_ring.next()
        src = self.din["w_in"] if src is None else src
        for q4 in range(nkc // 4):
            stg = self.wstg_ring.next()
            k.dma(stg[:, :, :wd], src[q4 * 512:(q4 + 1) * 512, c0:c0 + wd].re("(a p) c -> p a c", p=128),
                  q="sp")
            for a in range(4):
                kc = q4 * 4 + a
                if scale_g:
                    k.ts("pool", wb[:, kc, :wd], stg[:, a, :wd], self.gcol[:, kc:kc + 1], 1.0, ALU.mult, ALU.mult)
                else:
                    k.cp("pool", wb[:, kc, :wd], stg[:, a, :wd])
        return wb

    def normed(self, pb, R, nh, gidx, out_f32=None, out_bf=None, mul=None):
        k = self.k
        sq = self.etmp.next()
        st = self.st_ring.next()
        k.act(sq[:R, :nh * 128], pb, AF.Square)
        k.red(st[:R, 0:nh], sq[:R, :nh * 128].re("p (h d) -> p h d", h=nh))
        rs = self.rstd_from_ss(st, R, nh, 1.0 / HD)
        tmp = self.etmp.next()
        t3 = tmp[:R, :nh * 128].re("p (h d) -> p h d", h=nh)
        k.tt("dve", t3, pb.re("p (h d) -> p h d", h=nh), rs.us(2).bc([R, nh, 128]), ALU.mult)
        g = self.gvec[:R, gidx:gidx + 1, :].bc([R, nh, 128])
        if out_f32 is not None:
            k.tt("dve", out_f32.re("p (h d) -> p h d", h=nh), t3, g, ALU.mult)
            if out_bf is not None:
                k.cp("dve", out_bf, out_f32)
        else:
            k.tt("dve", out_bf.re("p (h d) -> p h d", h=nh), t3, g, ALU.mult)

    def build(self):
        k = self.k
        inp, outp = self.inp, self.outp
        xB = inp("xB", [8, 160, D])
        xS = inp("xS", [32, D])
        w_in = inp("w_in", [D, DIN])
        gcol_d = inp("gcol", [128, 16])
        cvec_d = inp("cvec", [35, 1024])
        gvec_d = inp("gvec", [128, 512])
        w_pw2 = inp("w_pw2", [1024, 1024])
        ident_d = inp("ident", [128, 128], BF16)
        identf_d = inp("identf", [128, 128])
        sconv_d = inp("sconv", [120, 1024])

        o_kc = outp("o_kc", [8, 128, 256]); o_vc = outp("o_vc", [8, 128, 256])
        o_ks = outp("o_ks", [8, 128, 256]); o_vs = outp("o_vs", [8, 128, 256])
        o_kw = outp("o_kw", [8, 128, 256]); o_vw = outp("o_vw", [8, 128, 256])
        o_convp = outp("o_convp", [30, 1024])
        o_kc2 = outp("o_kc2", [32, 256]); o_vc2 = outp("o_vc2", [32, 256])
        o_ks2 = outp("o_ks2", [32, 256]); o_vs2 = outp("o_vs2", [32, 256])
        o_convs = outp("o_convs", [120, 1024])
        o_kwin = outp("o_kwin", [4, 512, 256]); o_vwin = outp("o_vwin", [4, 512, 256])
        o_y = outp("o_y", [8, 128, D])
        o_y2 = outp("o_y2", [32, D])

        mixT = self.scr("mixT", [D, NOWN], BF16)
        qT_scr = self.scr("qT_scr", [8, 128, 8, 128], BF16)
        sz_scr = self.scr("sz_scr", [NOWN, 1024], BF16)

        self.ident = k.sb("ident", [128, 128], BF16)
        self.identf = k.sb("identf", [128, 128], F32)
        self.gcol = k.sb("gcol", [128, 16], F32)
        self.gvec = k.sb("gvec", [128, 4, 128], F32)
        self.cvT = k.sb("cvT", [128, 8, 35], F32)
        self.epsc = k.sb("epsc", [128, 1], F32)
        ksT_own = k.sb("ksT_own", [128, 8, 2, 128], BF16)
        Vs_own = k.sb("Vs_own", [128, 8, 2, 130], BF16)
        KwT_own = k.sb("KwT_own", [128, 8, 2, 128], BF16)
        Vw_own = k.sb("Vw_own", [128, 8, 2, 130], BF16)
        gates = k.sb("gates", [128, 8, 24], F32)
        q2T = k.sb("q2T", [128, 4, 8, 8], BF16)
        ks2T = k.sb("ks2T", [128, 4, 2, 8], BF16)
        kw2T = k.sb("kw2T", [128, 4, 2, 8], BF16)
        V2s = k.sb("V2s", [8, 4, 2, 130], BF16)
        V2w = k.sb("V2w", [8, 4, 2, 130], BF16)
        gates2 = k.sb("gates2", [8, 4, 24], F32)

        self.pers = dict(ksT_own=ksT_own, Vs_own=Vs_own, KwT_own=KwT_own, Vw_own=Vw_own, gates=gates, q2T=q2T,
                         ks2T=ks2T, kw2T=kw2T, V2s=V2s, V2w=V2w, gates2=gates2)
        self.dr = dict(mixT=mixT, qT_scr=qT_scr, sz_scr=sz_scr, xB=xB, xS=xS, o_y=o_y, o_y2=o_y2, o_kwin=o_kwin, o_vwin=o_vwin)
        with k.phase():
            k.dma(self.ident, ident_d)
            k.dma(self.identf, identf_d)
            k.dma(self.gcol, gcol_d)
            k.dma(self.gvec, gvec_d.re("p (a d) -> p a d", a=4))
            k.memset("dve", self.epsc, EPS)
            k.ts("dve", self.gvec[:, 0, :], self.gvec[:, 0, :], SCALE, None, ALU.mult)
            cv = k.sb("cv", [35, 1024], F32)
            k.dma(cv, cvec_d)
            for j in range(8):
                pb = self.ps.next()
                k.tr(pb[:, 0:35], cv[:, j * 128:(j + 1) * 128], self.identf[:35, :35])
                k.cp("dve", self.cvT[:, j, :], pb[:, 0:35])
            for t in (Vs_own, Vw_own, V2s, V2w):
                R = 8 if t in (V2s, V2w) else 128
                k.memset("dve", t[:R].re("p a g c -> p (a g) c")[:, :, 128:129], 1.0)
                k.memset("dve", t[:R].re("p a g c -> p (a g) c")[:, :, 129:130], 0.0)

        if self.stages < 1:
            k.flush(final=True)
            return

        with contextlib.ExitStack() as bst:
            old = k.stack
            k.stack = bst
            xTB = k.sb("xTB", [128, 16, TOKC], BF16)
            yT = k.sb("yT", [128, 8, NOWN], BF16)
            k.stack = old
            with k.phase():
                self.xt_ring = k.ring("xt", 2, [128, D], F32)
                self.xs_ring = k.ring("xs", 2, [128, D], BF16)
                self.st_ring = k.ring("st", 4, [128, 16], F32)
                self.wb_ring = k.ring("wb", 2, [128, 16, 512], BF16)
                self.wstg_ring = k.ring("wstg", 2, [128, 4, 512], F32)
                self.etmp = k.ring("etmp", 3, [128, 512], F32)
                uT_ring = k.ring("uT", 2, [128, 1280 + 4 * 38], F32)
                acc_ring = k.ring("acc", 2, [128, NOWN], F32)
                ptmp = k.sb("ptmp", [128, NOWN], F32)
                scT = k.sb("scT", [128, 8, 120], F32)
                uP = k.sb("uP", [128, 8, 30], F32)
                uS = k.sb("uS", [128, 8, 4, 30], F32)
                sc_in = k.sb("sc_in", [120, 1024], F32)
                k.dma(sc_in, sconv_d)
                for j in range(8):
                    pb = self.ps.next()
                    k.tr(pb[:, 0:120], sc_in[:, j * 128:(j + 1) * 128], self.identf[:120, :120])
                    k.cp("dve", scT[:, j, :], pb[:, 0:120])
                for s in range(8):
                    self.front(xB[s, 32:160, :], 128, xTB[:, :, s * 160 + 32:s * 160 + 160])
                    self.front(xB[s, 0:32, :], 32, xTB[:, :, s * 160:s * 160 + 32])
                self.front(xS[:, :], 32, xTB[:, :, 1280:1312])
                GROUPS = [(0, 480), (480, 480), (960, 352)]
                for half in range(2):
                    wa = self.wload(C_UA + 512 * half, 512)
                    wbk = self.wload(C_UB + 512 * half, 512)
                    for sub in range(4):
                        j = half * 4 + sub
                        uT = uT_ring.next()
                        for (c0, n) in GROUPS:
                            pa = self.ps.next()
                            pbb = self.ps.next()
                            for kc in range(16):
                                k.mm(pa[:, :n], wa[:, kc, sub * 128:(sub + 1) * 128], xTB[:, kc, c0:c0 + n],
                                     start=(kc == 0), stop=(kc == 15))
                            for kc in range(16):
                                k.mm(pbb[:, :n], wbk[:, kc, sub * 128:(sub + 1) * 128], xTB[:, kc, c0:c0 + n],
                                     start=(kc == 0), stop=(kc == 15))
                            e = self.etmp.next()
                            k.act(e[:, :n], pbb[:, :n], AF.Exp, scale=-1.0)
                            k.ts("dve", e[:, :n], e[:, :n], 1.0, None, ALU.add)
                            k.recip(e[:, :n], e[:, :n])
                            if c0 < 960:
                                k.tt("dve", uT[:, c0:c0 + n], pa[:, :n], e[:, :n], ALU.mult)
                            else:
                                k.tt("dve", uT[:, 960:1280], pa[:, :320], e[:, :320], ALU.mult)
                                k.tt("dve", uT[:, 1280:1432].re("p (b t) -> p b t", t=38)[:, :, 30:38],
                                     pa[:, 320:352].re("p (b t) -> p b t", t=8),
                                     e[:, 320:352].re("p (b t) -> p b t", t=8), ALU.mult)
                        eng = "pool" if (j % 4 == 3) else "dve"
                        k.cp(eng, uT[:, 1280:1432].re("p (b t) -> p b t", t=38)[:, :, 0:30],
                             scT[:, j, :].re("p (b t) -> p b t", t=30))
                        k.cp(eng, uP[:, j, :], uT[:, 7 * 160 + 130:7 * 160 + 160])
                        k.cp(eng, uS[:, j, :, :], uT[:, 1280:1432].re("p (b t) -> p b t", t=38)[:, :, 8:38])
                        acc = acc_ring.next()
                        a_own = acc[:, 0:1024].re("p (s t) -> p s t", t=128)
                        a_smp = acc[:, 1024:1056].re("p (b t) -> p b t", t=8)
                        u_own = uT[:, 0:1280].re("p (s t) -> p s t", t=160)
                        u_smp = uT[:, 1280:1432].re("p (b t) -> p b t", t=38)
                        wd = self.cvT[:, j, :]
                        k.ts(eng, a_own, u_own[:, :, 2:130], wd[:, 0:1], wd[:, 31:32], ALU.mult, ALU.add)
                        k.ts(eng, a_smp, u_smp[:, :, 0:8], wd[:, 0:1], wd[:, 31:32], ALU.mult, ALU.add)
                        for kk in range(1, 31):
                            if eng == "dve":
                                k.stt(eng, a_own, u_own[:, :, 2 + kk:130 + kk], wd[:, kk:kk + 1], a_own, ALU.mult, ALU.add)
                                k.stt(eng, a_smp, u_smp[:, :, kk:kk + 8], wd[:, kk:kk + 1], a_smp, ALU.mult, ALU.add)
                            else:
                                tp = ptmp[:, 0:1024].re("p (s t) -> p s t", t=128)
                                tq = ptmp[:, 1024:1056].re("p (b t) -> p b t", t=8)
                                k.ts(eng, tp, u_own[:, :, 2 + kk:130 + kk], wd[:, kk:kk + 1], 1.0, ALU.mult, ALU.mult)
                                k.tt(eng, a_own, a_own, tp, ALU.add)
                                k.ts(eng, tq, u_smp[:, :, kk:kk + 8], wd[:, kk:kk + 1], 1.0, ALU.mult, ALU.mult)
                                k.tt(eng, a_smp, a_smp, tq, ALU.add)
                        k.cp(eng, yT[:, j, :], acc)
                cps = k.sb("cps", [128, 1024], F32)
                for j in range(8):
                    pb = self.ps.next()
                    k.tr(pb[0:30, 0:128], uP[:, j, :], self.identf)
                    k.cp("dve", cps[0:30, j * 128:(j + 1) * 128], pb[0:30, 0:128])
                k.dma(o_convp, cps[0:30, :])
                cps2 = k.sb("cps2", [128, 1024], F32)
                for j in range(8):
                    pb = self.ps.next()
                    k.tr(pb[0:120, 0:128], uS[:, j, :, :].re("p b t -> p (b t)"), self.identf)
                    k.cp("dve", cps2[0:120, j * 128:(j + 1) * 128], pb[0:120, 0:128])
                k.dma(o_convs, cps2[0:120, :])

            if self.stages < 2:
                k.flush(final=True)
                return
            with k.phase():
                self.st_ring = k.ring("st", 4, [128, 16], F32)
                self.wb_ring = k.ring("wb", 2, [128, 16, 512], BF16)
                self.wstg_ring = k.ring("wstg", 2, [128, 4, 512], F32)
                self.etmp = k.ring("etmp", 4, [128, 512], F32)
                ones_b = k.sb("ones_b", [128, 128], BF16)
                k.memset("dve", ones_b, 1.0 / 1024)
                mean = k.sb("mean", [128, NOWN], F32)
                rstd = k.sb("rstd", [128, NOWN], F32)
                ysq_ring = k.ring("ysq", 2, [128, 512], BF16)
                LG = [(0, 512), (512, 512), (1024, 32)]
                for (c0, n) in LG:
                    pm = self.ps.next()
                    pq = self.ps.next()
                    for j in range(8):
                        k.mm(pm[:, :n], ones_b, yT[:, j, c0:c0 + n], start=(j == 0), stop=(j == 7))
                    for j in range(8):
                        ysq = ysq_ring.next()
                        k.tt("pool", ysq[:, :n], yT[:, j, c0:c0 + n], yT[:, j, c0:c0 + n], ALU.mult)
                        k.mm(pq[:, :n], ones_b, ysq[:, :n], start=(j == 0), stop=(j == 7))
                    k.cp("act", mean[:, c0:c0 + n], pm[:, :n])
                    e = self.etmp.next()
                    k.tt("dve", e[:, :n], mean[:, c0:c0 + n], mean[:, c0:c0 + n], ALU.mult)
                    k.tt("dve", e[:, :n], pq[:, :n], e[:, :n], ALU.subtract)
                    k.act(e[:, :n], e[:, :n], AF.Ln, bias=self.epsc[:, 0:1])
                    k.act(rstd[:, c0:c0 + n], e[:, :n], AF.Exp, scale=-0.5)
                for j in range(8):
                    for (c0, n) in LG:
                        t = self.etmp.next()
                        k.tt("dve", t[:, :n], yT[:, j, c0:c0 + n], mean[:, c0:c0 + n], ALU.subtract)
                        k.tt("dve", t[:, :n], t[:, :n], rstd[:, c0:c0 + n], ALU.mult)
                        k.ts("dve", t[:, :n], t[:, :n], self.cvT[:, j, 32:33], self.cvT[:, j, 33:34], ALU.mult, ALU.add)
                        self.silu_from("dve", yT[:, j, c0:c0 + n], t[:, :n], 128, n)
                Wp = k.sb("Wp", [128, 8, 1024], BF16)
                for hh in range(2):
                    for q4 in range(2):
                        stg = self.wstg_ring.next()
                        k.dma(stg, w_pw2[q4 * 512:(q4 + 1) * 512, hh * 512:(hh + 1) * 512].re("(a p) c -> p a c", p=128))
                        k.cp("pool", Wp[:, q4 * 4:(q4 + 1) * 4, hh * 512:(hh + 1) * 512], stg)
                szc_ring = k.ring("szc", 2, [128, NOWN], BF16)
                mixc_ring = k.ring("mixc", 2, [128, 512], BF16)
                GROUPS = [(0, 480), (480, 480), (960, 352)]
                for half in range(2):
                    wz = self.wload(C_ZC + 512 * half, 512)
                    for sub in range(4):
                        e_ = half * 4 + sub
                        szc = szc_ring.next()
                        for gi, (c0, n) in enumerate(GROUPS):
                            pz = self.ps.next()
                            for kc in range(16):
                                k.mm(pz[:, :n], wz[:, kc, sub * 128:(sub + 1) * 128], xTB[:, kc, c0:c0 + n],
                                     start=(kc == 0), stop=(kc == 15))
                            ns = 3 if gi < 2 else 2
                            zt = self.etmp.next()
                            self.silu_from("dve", zt[:, :n], pz[:, :n], 128, n)
                            k.cp("pool", szc[:, gi * 384:gi * 384 + ns * 128].re("p (s t) -> p s t", t=128),
                                 zt[:, :ns * 160].re("p (s t) -> p s t", t=160)[:, :, 32:160])
                            if gi == 2:
                                k.cp("pool", szc[:, 1024:1056], zt[:, 320:352])
                        for (c0, n) in LG:
                            po = self.ps.next()
                            for j in range(8):
                                k.mm(po[:, :n], Wp[:, j, e_ * 128:(e_ + 1) * 128], yT[:, j, c0:c0 + n],
                                     start=(j == 0), stop=(j == 7))
                            mc = mixc_ring.next()
                            k.stt("dve", mc[:, :n], po[:, :n], self.cvT[:, e_, 34:35], szc[:, c0:c0 + n], ALU.add, ALU.mult)
                            k.dma(mixT[e_ * 128:(e_ + 1) * 128, c0:c0 + n], mc[:, :n], q="pool")

                stg_ring = k.ring("ostg", 3, [128, 512], F32)
                bfo_ring = k.ring("bfo", 3, [128, 512], BF16)
                units = [(s, s * 160 + 32, 128) for s in range(8)] + [(8 + b, 1280 + 8 * b, 8) for b in range(4)]

                tm_list = []

                def tm_block(c0, wd, handler):
                    tm_list.append((c0, wd, handler))

                def tm_run(wb, wd, handler):
                    for (u, col, R) in units:
                        pb = self.ps.next()
                        for kc in range(16):
                            k.mm(pb[:R, :wd], xTB[:, kc, col:col + R], wb[:, kc, :wd], start=(kc == 0), stop=(kc == 15))
                        handler(u, R, pb)

                def to_T(src_bf, R, nh, dst):
                    pt = self.ps.next().cast(BF16)
                    for h in range(nh):
                        k.tr(pt[:, h * R:(h + 1) * R], src_bf[:R, h * 128:(h + 1) * 128], self.ident[:R, :R])
                    k.cp("act", dst, pt[:, 0:nh * R].re("p (h t) -> p h t", h=nh))

                for g in range(2):
                    def h_q(u, R, pb, g=g):
                        bf = bfo_ring.next()
                        self.normed(pb[:R, :512], R, 4, 0, out_bf=bf[:R, :])
                        if u < 8:
                            qsb = bfo_ring.next()
                            to_T(bf, R, 4, qsb[:, :].re("p (h t) -> p h t", h=4))
                            k.dma(qT_scr[u, :, g * 4:(g + 1) * 4, :], qsb[:, :].re("p (h t) -> p h t", h=4), q="act")
                        else:
                            to_T(bf, R, 4, q2T[:, u - 8, g * 4:(g + 1) * 4, :])
                    tm_block(C_Q + 512 * g, 512, h_q)

                def h_kcvc(u, R, pb):
                    stg = stg_ring.next()
                    k.cp("act", stg[:R, :], pb[:R, :512])
                    if u < 8:
                        k.dma(o_kc[u], stg[:, 0:256], q="act"); k.dma(o_vc[u], stg[:, 256:512], q="act")
                    else:
                        b = u - 8
                        k.dma(o_kc2[b * 8:(b + 1) * 8, :], stg[:8, 0:256]); k.dma(o_vc2[b * 8:(b + 1) * 8, :], stg[:8, 256:512])
                tm_block(C_KC, 512, h_kcvc)

                def mk_kv(gidx, oK, oV, oK2, oV2, KT_own, V_own, K2T, V2):
                    def h(u, R, pb):
                        stg = stg_ring.next()
                        bf = bfo_ring.next()
                        self.normed(pb[:R, 0:256], R, 2, gidx, out_f32=stg[:R, 0:256], out_bf=bf[:R, 0:256])
                        k.cp("act", stg[:R, 256:512], pb[:R, 256:512])
                        if u < 8:
                            k.dma(oK[u], stg[:, 0:256], q="pool"); k.dma(oV[u], stg[:, 256:512], q="act")
                            to_T(bf, R, 2, KT_own[:, u, :, :])
                            k.cp("pool", V_own[:, u, :, 0:128], stg[:, 256:512].re("p (g d) -> p g d", g=2))
                        else:
                            b = u - 8
                            if oK2 is not None:
                                k.dma(oK2[b * 8:(b + 1) * 8, :], stg[:8, 0:256]); k.dma(oV2[b * 8:(b + 1) * 8, :], stg[:8, 256:512])
                            else:
                                self.win_new_rows(b, stg)
                            to_T(bf, R, 2, K2T[:, b, :, :])
                            k.cp("pool", V2[:8, b, :, 0:128], stg[:8, 256:512].re("p (g d) -> p g d", g=2))
                    return h
                tm_block(C_KS, 512, mk_kv(1, o_ks, o_vs, o_ks2, o_vs2, ksT_own, Vs_own, ks2T, V2s))
                def _wnr(b, stg):
                    k.dma(o_kwin[b, 504:512, :], stg[:8, 0:256])
                    k.dma(o_vwin[b, 504:512, :], stg[:8, 256:512])
                self.win_new_rows = _wnr
                tm_block(C_KW, 512, mk_kv(2, o_kw, o_vw, None, None, KwT_own, Vw_own, kw2T, V2w))

                def h_gt(u, R, pb):
                    e = self.etmp.next()
                    k.act(e[:R, :24], pb[:R, :24], AF.Exp, scale=-1.0)
                    k.ts("dve", e[:R, :24], e[:R, :24], 1.0, None, ALU.add)
                    if u < 8:
                        k.recip(gates[:, u, :], e[:, :24])
                    else:
                        k.recip(gates2[:8, u - 8, :], e[:8, :24])
                tm_block(C_GT, 24, h_gt)

                for half in range(2):
                    def h_zn(u, R, pb, half=half):
                        bf = bfo_ring.next()
                        self.silu_from("dve", bf[:R, :], pb[:R, :512], R, 512)
                        r0 = u * 128 if u < 8 else 1024 + (u - 8) * 8
                        k.dma(sz_scr[r0:r0 + R, half * 512:(half + 1) * 512], bf[:R, :], q="pool")
                    tm_block(C_ZN + 512 * half, 512, h_zn)
                wb_next = self.wload(tm_list[0][0], tm_list[0][1])
                for i_, (c0_, wd_, hd_) in enumerate(tm_list):
                    wb_cur = wb_next
                    if i_ + 1 < len(tm_list):
                        wb_next = self.wload(tm_list[i_ + 1][0], tm_list[i_ + 1][1])
                    tm_run(wb_cur, wd_, hd_)

        if self.stages >= 3:
            self.build2()
        k.flush(final=True)


    def finish_branch(self, OB, oacc, s, g, b, rd_keep=None):
        k = self.k
        gates = self.pers["gates"]
        st = self.st_ring.next()
        for r in range(4):
            h = g * 4 + r
            k.ts("dve", st[:, r:r + 1], OB[r][:, 128:129], 1e-37, None, ALU.max)
        k.recip(st[:, 4:8], st[:, 0:4])
        if rd_keep is not None:
            k.cp("dve", rd_keep, st[:, 4:8])
        for r in range(4):
            h = g * 4 + r
            k.tt("dve", st[:, 8 + r:9 + r], st[:, 4 + r:5 + r], gates[:, s, h * 3 + b:h * 3 + b + 1], ALU.mult)
            oh = oacc[:, h * 128:(h + 1) * 128]
            if b == 0:
                k.ts("dve", oh, OB[r][:, 0:128], st[:, 8 + r:9 + r], None, ALU.mult)
            else:
                k.stt("dve", oh, OB[r][:, 0:128], st[:, 8 + r:9 + r], oh, ALU.mult, ALU.add)

    def build2(self):
        k = self.k
        inp = self.inp
        P = self.pers
        dr = self.dr
        mixT, qT_scr, sz_scr, xB, xS, o_y, o_y2 = (dr[n] for n in ("mixT", "qT_scr", "sz_scr", "xB", "xS", "o_y", "o_y2"))
        ksT_own, Vs_own, KwT_own, Vw_own = P["ksT_own"], P["Vs_own"], P["KwT_own"], P["Vw_own"]
        xA = inp("xA", [T, D])
        xH = inp("xH", [8, 4, 128, D])
        w_out = inp("w_out", [D, D])
        w1 = [inp("w_cmp_k1", [32, 128, 128]), inp("w_cmp_v1", [32, 128, 128])]
        w2 = [inp("w_cmp_k2", [128, 128]), inp("w_cmp_v2", [128, 128])]
        pe = [inp("pe_cmp_k", [32, 128]), inp("pe_cmp_v", [32, 128])]
        selbias_d = inp("selbias", [128, 288 * 8])
        dbias_d = inp("dbias", [128, 8])
        winbias_d = inp("winbias", [128, 8 * 5 * 8])
        cmpbias_d = inp("cmpbias", [128, 8 * 4 * 8])
        cmpmask_d = inp("cmpmask", [128, 8 * 4 * 128], BF16)
        forced_d = inp("forced", [128, 8 * 128])
        notown_d = inp("notown", [128, 8 * 128])
        ctab_d = inp("ctab", [128, 4 * 128], BF16)
        i4_d = inp("i4", [128, 512], BF16)
        tri_d = inp("tri", [128, 2 * 128], BF16)
        kcT_scr = self.scr("kcT_scr", [4, 128, T], BF16)
        kwT_scr = self.scr("kwT_scr", [8, 128, 2, 512], BF16)
        vw_scr = self.scr("vw_scr", [8, 128, 4, 2, 130], BF16)

        with contextlib.ExitStack() as cst:
            old = k.stack
            k.stack = cst
            KsT = k.sb("KsT", [128, 2, T], BF16)
            Vs = k.sb("Vs", [128, 64, 2, 130], BF16)
            KcT = k.sb("KcT", [128, 2, 512], BF16)
            Vc = k.sb("Vc", [128, 4, 2, 130], BF16)
            k.stack = old
            with k.phase():
                self.xt_ring = k.ring("xt", 3, [128, D], F32)
                self.xs_ring = k.ring("xs", 3, [128, D], BF16)
                self.st_ring = k.ring("st", 8, [128, 16], F32)
                self.wb_ring = k.ring("wb", 3, [128, 16, 512], BF16)
                self.wstg_ring = k.ring("wstg", 1, [128, 4, 512], F32)
                self.etmp = k.ring("etmp", 2, [128, 512], F32)
                for t_ in (Vs, Vc):
                    k.memset("dve", t_.re("p a g c -> p (a g) c")[:, :, 128:129], 1.0)
                    k.memset("dve", t_.re("p a g c -> p (a g) c")[:, :, 129:130], 0.0)
                W0 = self.wload(C_KC, 512)
                W1 = self.wload(C_KS, 512)
                Ww = self.wload(C_KW, 512)
                xT_ring = k.ring("xT", 2, [128, 16, 128], BF16)
                kcsb_ring = k.ring("kcsb", 2, [128, 4, 128], BF16)
                pab_ring = k.ring("pab", 2, [128, 512], BF16)
                bf_ring = k.ring("bfa", 3, [128, 256], BF16)
                kwsb_ring = k.ring("kwsb", 2, [128, 2, 128], BF16)
                vwsb_ring = k.ring("vwsb", 2, [128, 2, 130], BF16)
                for v_ in vwsb_ring.items:
                    k.memset("dve", v_[:, :, 128:129], 1.0)
                    k.memset("dve", v_[:, :, 129:130], 0.0)
                nA = 64 if self.stages >= 4 else 2
                jobs = []

                def mkA(t):
                    box = {}

                    def sL():
                        box["xt"] = self.front_load(xA[t * 128:(t + 1) * 128, :], 128)

                    def sF():
                        box["xs"] = self.front_norm(box["xt"], 128)

                    def s0():
                        box["xT"] = xT_ring.next()
                        self.front_T(box["xs"], 128, box["xT"])

                    def s1():
                        xT = box["xT"]
                        pa = self.ps.next()
                        for kc in range(16):
                            k.mm(pa, xT[:, kc, :], W0[:, kc, :], start=(kc == 0), stop=(kc == 15))
                        pab = pab_ring.next()
                        k.cp("act", pab, pa)
                        pb = self.ps.next()
                        for kc in range(16):
                            k.mm(pb, xT[:, kc, :], W1[:, kc, :], start=(kc == 0), stop=(kc == 15))
                        pt2 = self.ps.next().cast(BF16)
                        for cb in range(4):
                            k.tr(pt2[:, cb * 128:(cb + 1) * 128], pab[:, cb * 128:(cb + 1) * 128], self.ident)
                        kcsb = kcsb_ring.next()
                        k.cp("dve", kcsb, pt2[:, 0:512].re("p (a t) -> p a t", a=4))
                        k.dma(kcT_scr[:, :, t * 128:(t + 1) * 128].re("a p t -> p a t"), kcsb, q="pool")
                        box["bf"] = bf_ring.next()
                        self.normed(pb[:, 0:256], 128, 2, 1, out_bf=box["bf"])
                        k.cp("dve", Vs[:, t, :, 0:128], pb[:, 256:512].re("p (g d) -> p g d", g=2))

                    def s2():
                        bf = box["bf"]
                        pt = self.ps.next().cast(BF16)
                        for g in range(2):
                            k.tr(pt[:, g * 128:(g + 1) * 128], bf[:, g * 128:(g + 1) * 128], self.ident)
                        k.cp("act", KsT[:, :, t * 128:(t + 1) * 128], pt[:, 0:256].re("p (g t) -> p g t", g=2))
                    return [sL, sF, s0, s1, s2]

                def mkH(s_, o):
                    box = {}

                    def sL():
                        box["xt"] = self.front_load(xH[s_, o], 128)

                    def sF():
                        box["xs"] = self.front_norm(box["xt"], 128)

                    def s0():
                        box["xT"] = xT_ring.next()
                        self.front_T(box["xs"], 128, box["xT"])

                    def s1():
                        xT = box["xT"]
                        pb = self.ps.next()
                        for kc in range(16):
                            k.mm(pb, xT[:, kc, :], Ww[:, kc, :], start=(kc == 0), stop=(kc == 15))
                        box["bf"] = bf_ring.next()
                        self.normed(pb[:, 0:256], 128, 2, 2, out_bf=box["bf"])
                        vwsb = vwsb_ring.next()
                        k.cp("dve", vwsb[:, :, 0:128], pb[:, 256:512].re("p (g d) -> p g d", g=2))
                        k.dma(vw_scr[s_, :, o, :, :], vwsb, q="pool")

                    def s2():
                        bf = box["bf"]
                        pt = self.ps.next().cast(BF16)
                        for g in range(2):
                            k.tr(pt[:, g * 128:(g + 1) * 128], bf[:, g * 128:(g + 1) * 128], self.ident)
                        kwsb = kwsb_ring.next()
                        k.cp("act", kwsb, pt[:, 0:256].re("p (g t) -> p g t", g=2))
                        k.dma(kwT_scr[s_, :, :, o * 128:(o + 1) * 128], kwsb, q="act")
                    return [sL, sF, s0, s1, s2]
                for t in range(nA):
                    jobs.append(mkA(t))
                for s_ in range(8):
                    for o in range(4):
                        jobs.append(mkH(s_, o))
                run_pipeline(jobs)

            with k.phase():
                self.st_ring = k.ring("st", 4, [128, 16], F32)
                self.etmp = k.ring("etmp", 4, [128, 512], F32)
                stgW = k.sb("stgW", [128, 32, 128], F32)
                W1b = [k.sb("W1k", [128, 32, 128], BF16), k.sb("W1v", [128, 32, 128], BF16)]
                W2b = [k.sb("W2k", [128, 128], BF16), k.sb("W2v", [128, 128], BF16)]
                peT = k.sb("peT", [128, 2, 32], BF16)
                biasv = k.sb("biasv", [128, 2], F32)
                gkc = k.sb("gkc", [128, 1], F32)
                ones_m = k.sb("ones_m", [128, 128], BF16)
                k.memset("dve", ones_m, 1.0 / 128)
                for kv in range(2):
                    k.dma(stgW, w1[kv].re("c d e -> d c e"))
                    k.cp("pool", W1b[kv], stgW)
                    stg2 = self.etmp.next()
                    k.dma(stg2[:, 0:128], w2[kv])
                    k.cp("pool", W2b[kv], stg2[:, 0:128])
                    pes = self.etmp.next()
                    k.dma(pes[0:32, 0:128], pe[kv])
                    pt = self.ps.next()
                    k.tr(pt[:, 0:32], pes[0:32, 0:128], self.identf[:32, :32])
                    k.cp("dve", peT[:, kv, :], pt[:, 0:32])
                    pbias = self.ps.next()
                    for c in range(32):
                        k.mm(pbias[:, 0:2], W1b[kv][:, c, :], peT[:, kv, c:c + 1].bc([128, 2]), start=(c == 0), stop=(c == 31))
                    k.cp("dve", biasv[:, kv:kv + 1], pbias[:, 0:1])
                pt = self.ps.next()
                k.tr(pt[:, 0:128], self.gvec[:, 3, :], self.identf)
                k.cp("dve", gkc, pt[:, 0:1])
                self.cmp_w = (W1b, W2b, biasv, gkc, ones_m)
                kc_ring = k.ring("kcl", 2, [128, T], BF16)
                fb_sb = k.sb("fb_sb", [128, 512], F32)
                hid_ring = k.ring("hid", 2, [128, 512], BF16)
                sq_ring = k.ring("sq", 2, [128, 512], BF16)
                for kv in range(2):
                    for g in range(2):
                        kcl = kc_ring.next()
                        k.dma(kcl, kcT_scr[kv * 2 + g])
                        kview = kcl.re("p (n c) -> p n c", c=16)
                        pfa = self.ps.next()
                        pfb = self.ps.next()
                        for c in range(16):
                            k.mm(pfa, W1b[kv][:, c, :], kview[:, :, c], start=(c == 0), stop=(c == 15))
                        for c in range(16):
                            k.mm(pfb, W1b[kv][:, 16 + c, :], kview[:, :, c], start=(c == 0), stop=(c == 15))
                        k.cp("act", fb_sb, pfb)
                        t_ = self.etmp.next()
                        k.tt("dve", t_[:, 0:511], pfa[:, 0:511], fb_sb[:, 1:512], ALU.add)
                        h = hid_ring.next()
                        k.memset("dve", h[:, 511:512], 0.0)
                        self.silu_from("dve", h[:, 0:511], t_[:, 0:511], 128, 511, bias=biasv[:, kv:kv + 1])
                        if kv == 0:
                            pk = self.ps.next()
                            k.mm(pk, W2b[0], h)
                            sq = sq_ring.next()
                            k.act(sq, pk, AF.Square)
                            pss = self.ps.next()
                            k.mm(pss, ones_m, sq)
                            e = self.etmp.next()
                            k.act(e, pss, AF.Ln, bias=self.epsc[:, 0:1])
                            k.act(e, e, AF.Exp, scale=-0.5)
                            k.tt("dve", e, pk, e, ALU.mult)
                            k.ts("dve", KcT[:, g, :], e, gkc[:, 0:1], None, ALU.mult)
                        else:
                            pv = self.ps.next()
                            for nt in range(4):
                                k.mm(pv[:, nt * 128:(nt + 1) * 128], h[:, nt * 128:(nt + 1) * 128], W2b[1])
                            k.cp("dve", Vc[:, :, g, 0:128], pv.re("p (a d) -> p a d", a=4))

            with k.phase():
                self.st_ring = k.ring("st", 6, [128, 16], F32)
                self.etmp = k.ring("etmp", 3, [128, 512], F32)

                def tab(name, d, shape, dt=F32, pat=None, **kw):
                    t_ = k.sb(name, shape, dt)
                    k.dma(t_, d.re(pat, **kw) if pat else d)
                    return t_
                selbias = tab("selbias", selbias_d, [128, 288, 8], pat="p (a h) -> p a h", h=8)
                dbias = tab("dbias", dbias_d, [128, 8])
                winbias = tab("winbias", winbias_d, [128, 8, 5, 8], pat="p (s o h) -> p s o h", s=8, o=5)
                cmpbias = tab("cmpbias", cmpbias_d, [128, 8, 4, 8], pat="p (s o h) -> p s o h", s=8, o=4)
                cmpmask = tab("cmpmask", cmpmask_d, [128, 8, 4, 128], BF16, pat="p (s o q) -> p s o q", s=8, o=4)
                forced = tab("forced", forced_d, [128, 8, 128], pat="p (s j) -> p s j", s=8)
                notown = tab("notown", notown_d, [128, 8, 128], pat="p (s j) -> p s j", s=8)
                ctab = tab("ctab", ctab_d, [128, 4, 128], BF16, pat="p (a j) -> p a j", a=4)
                i4 = tab("i4", i4_d, [128, 512], BF16)
                tri = tab("tri", tri_d, [128, 2, 128], BF16, pat="p (a q) -> p a q", a=2)
                QT_ring = k.ring("QT", 2, [128, 8, 128], BF16)
                sz_ring = k.ring("sz", 2, [128, 1024], BF16)
                KwH_ring = k.ring("KwH", 2, [128, 2, 512], BF16)
                VwH_ring = k.ring("VwH", 2, [128, 4, 2, 130], BF16)
                ET_items = []
                for i_ in range(4):
                    base = k.sb("ET%d" % i_, [128, 512], BF16)
                    toks = [Tok() for _ in range(4)]
                    ET_items.append((V(base.ap, toks), [V(base.ap[:, r * 128:(r + 1) * 128], [toks[r]]) for r in range(4)]))
                ET_ring = Ring(ET_items)
                ETc_ring = k.ring("ETc", 2, [128, 4, 512], BF16)
                stsbC_ring = k.ring("stsbC", 3, [128, 512], F32)
                oacc_ring = k.ring("oacc", 2, [128, 1024], F32)
                mexp_ring = k.ring("mexp", 2, [128, 1024], BF16)
                sc_ring = k.ring("sc", 4, [128, 128], F32)
                m8_ring = k.ring("m8", 2, [128, 16], F32)
                rdc_ring = k.ring("rdc", 2, [128, 4], F32)
                mix_ring = k.ring("mix", 2, [128, 1024], BF16)
                mixa_ring = k.ring("mixa", 2, [128, 8, 128], BF16)
                OB = self.ps.items[0:4]
                STr = Ring(self.ps.items[4:6])
                MS = Ring(self.ps.items[6:8])
                offs = [sum(8 * s2 + 8 for s2 in range(s)) for s in range(8)]
                nS = 8 if self.stages >= 4 else 1
                for s in range(nS):
                    QT = QT_ring.next()
                    k.dma(QT, qT_scr[s])
                    sz = sz_ring.next()
                    k.dma(sz, sz_scr[s * 128:(s + 1) * 128, :])
                    KwH = KwH_ring.next()
                    k.dma(KwH, kwT_scr[s])
                    VwH = VwH_ring.next()
                    k.dma(VwH, vw_scr[s])
                    oacc = oacc_ring.next()
                    for g in range(2):
                        Qg = QT[:, g * 4:(g + 1) * 4, :].re("p h t -> p (h t)")
                        ETc0 = ETc_ring.next()
                        ctoks = [Tok() for _ in range(4)]
                        ETc = V(ETc0.ap, ctoks + ETc0.toks)
                        jobs = []

                        def mk_c(nt, g=g, Qg=Qg, ETc0=ETc0, ctoks=ctoks):
                            box = {}

                            def s0():
                                box["st"] = STr.next()
                                k.mm(box["st"], KcT[:, g, nt * 128:(nt + 1) * 128], Qg)

                            def s1():
                                ev = V(ETc0.ap[:, nt, :], [ctoks[nt]])
                                sb_ = stsbC_ring.next()
                                k.tt("dve", sb_.re("p (r q) -> p r q", r=4), box["st"].re("p (r q) -> p r q", r=4),
                                     cmpbias[:, s, nt, g * 4:(g + 1) * 4].us(2).bc([128, 4, 128]), ALU.add)
                                k.act(ev, sb_, AF.Exp)
                                e3 = ev.re("p (r q) -> p r q", r=4)
                                k.tt("pool", e3, e3, cmpmask[:, s, nt, :].us(1).bc([128, 4, 128]), ALU.mult)
                            return [s0, s1]
                        for nt in range(4):
                            jobs.append(mk_c(nt))
                        run_pipeline(jobs)
                        for r in range(4):
                            for nt in range(4):
                                k.mm(OB[r][:, 0:129], ETc[:, nt, r * 128:(r + 1) * 128], Vc[:, nt, g, 0:129],
                                     start=(nt == 0), stop=(nt == 3))
                        imp = MS.next()
                        for r in range(4):
                            for nt in range(4):
                                k.mm(imp[:, r * 128:(r + 1) * 128], ETc[:, nt, r * 128:(r + 1) * 128], ctab[:, nt, :],
                                     start=(nt == 0), stop=(nt == 3))
                        rdc = rdc_ring.next()
                        self.finish_branch(OB, oacc, s, g, 0, rd_keep=rdc)
                        sc = sc_ring.next()
                        sc2 = sc_ring.next()
                        m8 = m8_ring.next()
                        k.ts("dve", sc, imp[:, 0:128], rdc[:, 0:1], None, ALU.mult)
                        for r in range(1, 4):
                            k.stt("dve", sc, imp[:, r * 128:(r + 1) * 128], rdc[:, r:r + 1], sc, ALU.mult, ALU.add)
                        k.tt("dve", sc, sc, forced[:, s, :], ALU.add)
                        k.max8(m8[:, 0:8], sc)
                        k.mrep(sc2, m8[:, 0:8], sc, -3.0e38)
                        k.max8(m8[:, 8:16], sc2)
                        k.ts("dve", sc2, sc, m8[:, 15:16], None, ALU.is_ge)
                        k.tt("dve", sc2, sc2, notown[:, s, :], ALU.mult)
                        k.ts("dve", sc2, sc2, 30000.0, -30000.0, ALU.mult, ALU.add)
                        nT = 8 * s + 8
                        cur = {}

                        def mk_tile(lhsK, t_mask, biasfn, trimask, Vt, start, stop, g=g, Qg=Qg, cur=cur):
                            box = {}

                            def s0():
                                if t_mask is not None and t_mask % 8 == 0:
                                    cur["mexp"] = mexp_ring.next()
                                    c8 = t_mask // 8
                                    k.cp("pool", cur["mexp"].re("p (j e) -> p j e", e=64),
                                         sc2[:, c8 * 16:(c8 + 1) * 16].us(2).bc([128, 16, 64]))
                                st_ = STr.next()
                                box["st"] = st_
                                k.mm(st_, lhsK, Qg, start=True, stop=(t_mask is None))
                                if t_mask is not None:
                                    k.mm(st_, cur["mexp"][:, (t_mask % 8) * 128:(t_mask % 8 + 1) * 128], i4, start=False, stop=True)

                            def s1():
                                whole, parts = ET_ring.next()
                                box["ET"] = parts
                                sb_ = stsbC_ring.next()
                                k.tt("dve", sb_.re("p (r q) -> p r q", r=4), box["st"].re("p (r q) -> p r q", r=4),
                                     biasfn(g).us(2).bc([128, 4, 128]), ALU.add)
                                k.act(whole, sb_, AF.Exp)
                                if trimask is not None:
                                    e3 = whole.re("p (r q) -> p r q", r=4)
                                    k.tt("pool", e3, e3, tri[:, trimask, :].us(1).bc([128, 4, 128]), ALU.mult)

                            def s2():
                                for r in range(4):
                                    k.mm(OB[r][:, 0:129], box["ET"][r], Vt, start=start, stop=stop)
                            return [s0, s1, s2]

                        jobs = []
                        for t in range(nT):
                            jobs.append(mk_tile(KsT[:, g, t * 128:(t + 1) * 128], t,
                                                (lambda g_, t=t: selbias[:, offs[s] + t, g_ * 4:(g_ + 1) * 4]), None,
                                                Vs[:, t, g, 0:129], (t == 0), False))
                        jobs.append(mk_tile(ksT_own[:, s, g, :], None, (lambda g_: dbias[:, g_ * 4:(g_ + 1) * 4]), 0,
                                            Vs_own[:, s, g, 0:129], False, True))
                        run_pipeline(jobs)
                        self.finish_branch(OB, oacc, s, g, 1)
                        jobs = []
                        for o in range(5):
                            lhs = KwH[:, g, o * 128:(o + 1) * 128] if o < 4 else KwT_own[:, s, g, :]
                            Vt = VwH[:, o, g, 0:129] if o < 4 else Vw_own[:, s, g, 0:129]
                            jobs.append(mk_tile(lhs, None, (lambda g_, o=o: winbias[:, s, o, g_ * 4:(g_ + 1) * 4]),
                                                (1 if o == 0 else (0 if o == 4 else None)), Vt, (o == 0), (o == 4)))
                        run_pipeline(jobs)
                        self.finish_branch(OB, oacc, s, g, 2)
                    mix = mix_ring.next()
                    k.tt("dve", mix, oacc, sz, ALU.mult)
                    mixa = mixa_ring.next()
                    for half in range(2):
                        pt = MS.next().cast(BF16)
                        for e4 in range(4):
                            e_ = half * 4 + e4
                            k.tr(pt[:, e4 * 128:(e4 + 1) * 128], mix[:, e_ * 128:(e_ + 1) * 128], self.ident)
                        k.cp("act", mixa[:, half * 4:(half + 1) * 4, :], pt[:, 0:512].re("p (a t) -> p a t", a=4))
                    k.dma(mixT[1024:2048, s * 128:(s + 1) * 128].re("(a p) t -> p a t", p=128), mixa, q="act")

        if self.stages >= 6:
            self.build3()

        with k.phase():
            self.wb_ring = k.ring("wb", 2, [128, 16, 512], BF16)
            self.wstg_ring = k.ring("wstg", 2, [128, 4, 512], F32)
            mixsb = k.sb("mixsb", [128, 16, NOWN], BF16)
            for e in range(16):
                k.dma(mixsb[:, e, :], mixT[e * 128:(e + 1) * 128, :])
            xr_ring = k.ring("xr", 3, [128, 512], F32)
            yo_ring = k.ring("yo", 3, [128, 512], F32)
            Wo_next = self.wload(0, 512, scale_g=False, src=w_out)
            for ct in range(4):
                Wo = Wo_next
                if ct + 1 < 4:
                    Wo_next = self.wload((ct + 1) * 512, 512, scale_g=False, src=w_out)
                for u in range(9):
                    R = 128 if u < 8 else 32
                    col = u * 128
                    pb = self.ps.next()
                    for e in range(16):
                        k.mm(pb[:R, :], mixsb[:, e, col:col + R], Wo[:, e, :], start=(e == 0), stop=(e == 15))
                    xr = xr_ring.next()
                    src = xB[u, 32:160, ct * 512:(ct + 1) * 512] if u < 8 else xS[:, ct * 512:(ct + 1) * 512]
                    k.dma(xr[:R], src)
                    yo = yo_ring.next()
                    k.tt("dve", yo[:R], pb[:R, :], xr[:R], ALU.add)
                    dst = o_y[u][:, ct * 512:(ct + 1) * 512] if u < 8 else o_y2[:, ct * 512:(ct + 1) * 512]
                    k.dma(dst, yo[:R], q="act")


    def build3(self):
        k = self.k
        inp = self.inp
        P = self.pers
        dr = self.dr
        mixT, sz_scr = dr["mixT"], dr["sz_scr"]
        q2T, ks2T, kw2T, V2s, V2w, gates2 = (P[n] for n in ("q2T", "ks2T", "kw2T", "V2s", "V2w", "gates2"))
        pt_d = inp("ptab", [4, 128], I32)
        caches = [inp(n, [NPHYS * 8, 4096]) for n in ("cache_k_cmp", "cache_v_cmp", "cache_k_slc", "cache_v_slc")]
        skw_d = inp("skw", [4, 512, 256])
        svw_d = inp("svw", [4, 512, 256])
        o_kwin, o_vwin = dr["o_kwin"], dr["o_vwin"]
        w1 = [self.din["w_cmp_k1"], self.din["w_cmp_v1"]]
        w2 = [self.din["w_cmp_k2"], self.din["w_cmp_v2"]]
        pe = [self.din["pe_cmp_k"], self.din["pe_cmp_v"]]
        marr_d = inp("marr", [128, 8])
        c2bias_d = inp("c2bias", [128, 64])
        c2tab_d = inp("c2tab", [128, 8 * 256], BF16)
        forced2_d = inp("forced2", [8, 256])
        abias_d = inp("abias", [128, 128 * 8])
        swbias_d = inp("swbias", [128, 32])
        swmask_d = inp("swmask", [128, 32], BF16)
        tbias_d = inp("tbias", [8, 8])
        tmask_d = inp("tmask", [8, 8], BF16)
        i48_d = inp("i48", [8, 32], BF16)
        selr_d = inp("selr", [32, 32])
        selsum_d = inp("selsum", [32, 8])

        with k.phase():
            self.st_ring = k.ring("st", 6, [128, 16], F32)
            self.etmp = k.ring("etmp", 4, [128, 512], F32)

            def tab(name, d, shape, dt=F32, pat=None, **kw):
                t_ = k.sb(name, shape, dt)
                k.dma(t_, d.re(pat, **kw) if pat else d)
                return t_
            marr = tab("marr", marr_d, [128, 8])
            c2bias = tab("c2bias", c2bias_d, [128, 8, 8], pat="p (m h) -> p m h", m=8)
            c2tab = tab("c2tab", c2tab_d, [128, 8, 256], BF16, pat="p (m j) -> p m j", m=8)
            forced2 = tab("forced2", forced2_d, [8, 256])
            abias = tab("abias", abias_d, [128, 128, 8], pat="p (r h) -> p r h", h=8)
            swbias = tab("swbias", swbias_d, [128, 4, 8], pat="p (w h) -> p w h", h=8)
            swmask = tab("swmask", swmask_d, [128, 4, 8], BF16, pat="p (w i) -> p w i", i=8)
            tbias = tab("tbias", tbias_d, [8, 8])
            tmask = tab("tmask", tmask_d, [8, 8], BF16)
            i48 = tab("i48", i48_d, [8, 32], BF16)
            selr = tab("selr", selr_d, [32, 4, 8], pat="p (r i) -> p r i", r=4)
            selsum = tab("selsum", selsum_d, [32, 8])
            X_ring = k.ring("X", 3, [128, 16, 2, 128], F32)
            W1b = [k.sb("W1k", [128, 32, 128], BF16), k.sb("W1v", [128, 32, 128], BF16)]
            W2b = [k.sb("W2k", [128, 128], BF16), k.sb("W2v", [128, 128], BF16)]
            peT = k.sb("peT", [128, 2, 32], BF16)
            biasv = k.sb("biasv", [128, 2], F32)
            gkc = k.sb("gkc", [128, 1], F32)
            ones_m = k.sb("ones_m", [128, 128], BF16)
            k.memset("dve", ones_m, 1.0 / 128)
            for kv in range(2):
                stgW = X_ring.next().re("p a g d -> p (a g) d")
                k.dma(stgW, w1[kv].re("c d e -> d c e"))
                k.cp("pool", W1b[kv], stgW)
                stg2 = self.etmp.next()
                k.dma(stg2[:, 0:128], w2[kv])
                k.cp("pool", W2b[kv], stg2[:, 0:128])
                pes = self.etmp.next()
                k.dma(pes[0:32, 0:128], pe[kv])
                pt = self.ps.next()
                k.tr(pt[:, 0:32], pes[0:32, 0:128], self.identf[:32, :32])
                k.cp("dve", peT[:, kv, :], pt[:, 0:32])
                pbias = self.ps.next()
                for c in range(32):
                    k.mm(pbias[:, 0:2], W1b[kv][:, c, :], peT[:, kv, c:c + 1].bc([128, 2]), start=(c == 0), stop=(c == 31))
                k.cp("dve", biasv[:, kv:kv + 1], pbias[:, 0:1])
            pt = self.ps.next()
            k.tr(pt[:, 0:128], self.gvec[:, 3, :], self.identf)
            k.cp("dve", gkc, pt[:, 0:1])
            pti = k.sb("pti", [128, 4], I32)
            for bl in range(4):
                k.dma(pti[:, bl:bl + 1], pt_d[bl].re("(p o) -> p o", o=1))
            ptf = k.sb("ptf", [128, 4], F32)
            k.cp("dve", ptf, pti)
            k.ts("dve", ptf, ptf, 8.0, None, ALU.mult)
            idxf = k.sb("idxf", [128, 4, 8], F32)
            idx8 = k.sb("idx8", [128, 4, 8], I32)
            for bl in range(4):
                k.ts("dve", idxf[:, bl, :], marr, ptf[:, bl:bl + 1], None, ALU.add)
            k.cp("dve", idx8, idxf)

            XT_ring = k.ring("XT", 2, [128, 16, 2, 128], BF16)
            Vb_ring = k.ring("Vb", 2, [128, 16, 2, 130], BF16)
            for v_ in Vb_ring.items:
                k.memset("dve", v_.re("p a g c -> p (a g) c")[:, :, 128:129], 1.0)
                k.memset("dve", v_.re("p a g c -> p (a g) c")[:, :, 129:130], 0.0)
            Fsb = k.sb("Fsb", [128, 2, 2, 8, 128], F32)
            hid_ring = k.ring("hid", 2, [128, 1024], BF16)
            sq_ring = k.ring("sq", 2, [128, 512], BF16)
            Kc2T = k.sb("Kc2T", [128, 2, 1024], BF16)
            Vc2 = k.sb("Vc2", [128, 8, 2, 130], BF16)
            k.memset("dve", Vc2.re("p a g c -> p (a g) c")[:, :, 128:129], 1.0)
            k.memset("dve", Vc2.re("p a g c -> p (a g) c")[:, :, 129:130], 0.0)
            stsb_ring = k.ring("stsb", 2, [128, 512], F32)
            ET_ring = k.ring("ET2", 2, [128, 512], BF16)
            o2_ring = k.ring("o2acc", 1, [8, 1024], F32)
            on_ring = k.ring("onorm", 2, [32, 128], F32)
            impn = k.sb("impn", [32, 256], F32)
            sc_ring = k.ring("sc2", 3, [8, 264], F32)
            m8 = k.sb("m8b", [8, 16], F32)
            Mb2h = k.sb("Mb2h", [8, 2, 2, 128], BF16)
            skw = k.sb("skw", [128, 4, 2, 128], F32)
            svw = k.sb("svw", [128, 4, 2, 128], F32)
            Vwb = k.sb("Vwb", [128, 4, 2, 130], BF16)
            k.memset("dve", Vwb.re("p a g c -> p (a g) c")[:, :, 128:129], 1.0)
            k.memset("dve", Vwb.re("p a g c -> p (a g) c")[:, :, 129:130], 0.0)
            KwT2 = k.sb("KwT2", [128, 2, 4, 128], BF16)
            sz2_ring = k.ring("sz2", 1, [8, 1024], BF16)
            mix2_ring = k.ring("mix2", 1, [8, 1024], BF16)
            mix2a_ring = k.ring("mix2a", 1, [128, 8, 8], BF16)
            OC = self.ps.items[0]
            OG = self.ps.items[1:3]
            STr = Ring(self.ps.items[3:5])
            MS = Ring(self.ps.items[5:8])

            def q2g(bl, g):
                return q2T[:, bl, g * 4:(g + 1) * 4, :].re("p h i -> p (h i)")

            def finish2(Og, o2acc, bl, g, b, keep=None):
                st = self.st_ring.next()
                k.ts("dve", st[:32, 0:1], Og[:32, 128:129], 1e-37, None, ALU.max)
                k.recip(st[:32, 1:2], st[:32, 0:1])
                if keep is not None:
                    k.cp("dve", keep, st[:32, 1:2])
                on = on_ring.next()
                k.ts("dve", on, Og[:32, 0:128], st[:32, 1:2], None, ALU.mult)
                pr = MS.next()
                for r in range(4):
                    k.mm(pr[:8, r * 128:(r + 1) * 128], selr[:, r, :], on)
                for r in range(4):
                    h = g * 4 + r
                    oh = o2acc[:, h * 128:(h + 1) * 128]
                    gt = gates2[:8, bl, h * 3 + b:h * 3 + b + 1]
                    if b == 0:
                        k.ts("dve", oh, pr[:8, r * 128:(r + 1) * 128], gt, None, ALU.mult)
                    else:
                        k.stt("dve", oh, pr[:8, r * 128:(r + 1) * 128], gt, oh, ALU.mult, ALU.add)

            def transposes_to(Xv, dstT, evac_i, gs=(0, 1)):
                n = 0
                for g in gs:
                    for r4 in range(4):
                        pb = MS.next()
                        for a in range(4):
                            k.tr(pb[:, a * 128:(a + 1) * 128], Xv[:, r4 * 4 + a, g, :], self.identf)
                        k.cp("act" if (n + evac_i) % 2 == 0 else "dve", dstT[:, r4 * 4:(r4 + 1) * 4, g, :],
                             pb.re("p (a q) -> p a q", a=4))
                        n += 1

            for bl in range(4):
                def post_cmp(kv):
                    for g in range(2):
                        h = hid_ring.next()
                        for half in range(2):
                            t_ = self.etmp.next()
                            if half == 0:
                                k.tt("dve", t_.re("p (m q) -> p m q", m=4), Fsb[:, 0, g, 0:4, :], Fsb[:, 1, g, 1:5, :], ALU.add)
                            else:
                                k.tt("dve", t_[:, 0:384].re("p (m q) -> p m q", m=3), Fsb[:, 0, g, 4:7, :], Fsb[:, 1, g, 5:8, :], ALU.add)
                                k.tt("dve", t_[:, 384:511], Fsb[:, 0, g, 7, 0:127], Fsb[:, 1, g, 0, 1:128], ALU.add)
                                k.memset("dve", t_[:, 511:512], 0.0)
                            self.silu_from("dve", h[:, half * 512:(half + 1) * 512], t_, 128, 512, bias=biasv[:, kv:kv + 1])
                        k.memset("dve", h[:, 1023:1024], 0.0)
                        if kv == 0:
                            for half in range(2):
                                hs = h[:, half * 512:(half + 1) * 512]
                                pk = MS.next()
                                k.mm(pk, W2b[0], hs)
                                sq = sq_ring.next()
                                k.act(sq, pk, AF.Square)
                                pss = MS.next()
                                k.mm(pss, ones_m, sq)
                                e = self.etmp.next()
                                k.act(e, pss, AF.Ln, bias=self.epsc[:, 0:1])
                                k.act(e, e, AF.Exp, scale=-0.5)
                                k.tt("dve", e, pk, e, ALU.mult)
                                k.ts("dve", Kc2T[:, g, half * 512:(half + 1) * 512], e, gkc[:, 0:1], None, ALU.mult)
                        else:
                            for half in range(2):
                                pv = MS.next()
                                for a in range(4):
                                    m = half * 4 + a
                                    k.mm(pv[:, a * 128:(a + 1) * 128], h[:, m * 128:(m + 1) * 128], W2b[1])
                                k.cp("dve", Vc2[:, half * 4:(half + 1) * 4, g, 0:128], pv.re("p (a d) -> p a d", a=4))

                def mk_cmp(kv, m, bl=bl):
                    box = {}

                    def sG():
                        box["X"] = X_ring.next()
                        k.gather(box["X"].re("p a g d -> p (a g d)"), caches[kv], idx8[:, bl, m:m + 1])

                    def s0():
                        box["XT"] = XT_ring.next()
                        transposes_to(box["X"], box["XT"], m)

                    def s1():
                        XT = box["XT"]
                        pf = STr.next()
                        for ab in range(2):
                            for g in range(2):
                                for c in range(16):
                                    k.mm(pf[:, (ab * 2 + g) * 128:(ab * 2 + g + 1) * 128], W1b[kv][:, ab * 16 + c, :],
                                         XT[:, c, g, :], start=(c == 0), stop=(c == 15))
                        k.cp("dve" if m % 2 == 0 else "act", Fsb[:, :, :, m, :], pf.re("p (a g q) -> p a g q", a=2, g=2))
                        if m == 7:
                            post_cmp(kv)
                    return [sG, s0, s1]
                run_pipeline([mk_cmp(kv, m) for kv in range(2) for m in range(8)])
                o2acc = o2_ring.next()
                k.dma(skw.re("p w g d -> p w (g d)"), skw_d[bl].re("(w p) c -> p w c", p=128))
                k.dma(svw.re("p w g d -> p w (g d)"), svw_d[bl].re("(w p) c -> p w c", p=128))
                for (src, dst) in ((skw, o_kwin), (svw, o_vwin)):
                    k.dma(dst[bl, 0:120, :], src[8:128, 0, :, :].re("p g d -> p (g d)"))
                    k.dma(dst[bl, 120:504, :].re("(w p) c -> p w c", p=128), src[:, 1:4, :, :].re("p w g d -> p w (g d)"))
                for g in range(2):
                    pb = MS.next()
                    for w in range(4):
                        k.tr(pb[:, w * 128:(w + 1) * 128], skw[:, w, g, :], self.identf)
                    k.cp("act", KwT2[:, g, :, :], pb.re("p (w q) -> p w q", w=4))
                k.cp("pool", Vwb[:, :, :, 0:128], svw)
                for g in range(2):
                    qg = q2g(bl, g)
                    st_ = STr.next()
                    for m in range(8):
                        k.mm(st_[:, m * 32:(m + 1) * 32], Kc2T[:, g, m * 128:(m + 1) * 128], qg)
                    sb_ = stsb_ring.next()
                    k.tt("dve", sb_[:, 0:256].re("p (m r i) -> p m r i", m=8, r=4), st_[:, 0:256].re("p (m r i) -> p m r i", m=8, r=4),
                         c2bias[:, :, g * 4:(g + 1) * 4].us(3).bc([128, 8, 4, 8]), ALU.add)
                    ETc = ET_ring.next()
                    k.act(ETc[:, 0:256], sb_[:, 0:256], AF.Exp)
                    for m in range(8):
                        k.mm(OC[:32, 0:129], ETc[:, m * 32:(m + 1) * 32], Vc2[:, m, g, 0:129], start=(m == 0), stop=(m == 7))
                    pim = MS.next()
                    for m in range(8):
                        k.mm(pim[:32, 0:256], ETc[:, m * 32:(m + 1) * 32], c2tab[:, m, :], start=(m == 0), stop=(m == 7))
                    rdk = self.st_ring.next()
                    finish2(OC, o2acc, bl, g, 0, keep=rdk[:32, 15:16])
                    k.ts("dve", impn, pim[:32, 0:256], rdk[:32, 15:16], None, ALU.mult)
                    pi2 = MS.next()
                    k.mm(pi2[:8, 0:256], selsum, impn)
                    sc = sc_ring.next()
                    sc2 = sc_ring.next()
                    k.tt("dve", sc[:, 0:256], pi2[:8, 0:256], forced2, ALU.add)
                    k.max8(m8[:, 0:8], sc[:, 0:256])
                    k.mrep(sc2[:, 0:256], m8[:, 0:8], sc[:, 0:256], -3.0e38)
                    k.max8(m8[:, 8:16], sc2[:, 0:256])
                    k.ts("dve", sc2[:, 0:256], sc[:, 0:256], m8[:, 14:15], None, ALU.is_ge)
                    k.ts("dve", sc2[:, 0:256], sc2[:, 0:256], 30000.0, -30000.0, ALU.mult, ALU.add)
                    k.cp("dve", Mb2h[:, g, :, :], sc2[:, 0:256].re("i (p h) -> i h p", h=2))
                shared = {}

                def mk_sel(m, g, bl=bl, shared=shared):
                    box = {}
                    qg = q2g(bl, g)
                    hb = m // 4

                    def sG():
                        if g == 0:
                            Xk = X_ring.next()
                            k.gather(Xk.re("p a g d -> p (a g d)"), caches[2], idx8[:, bl, m:m + 1])
                            Xv = X_ring.next()
                            k.gather(Xv.re("p a g d -> p (a g d)"), caches[3], idx8[:, bl, m:m + 1])
                            shared[m] = [Xk, Xv, None, None]

                    def s0():
                        if g == 0:
                            Xk, Xv = shared[m][0], shared[m][1]
                            Vb = Vb_ring.next()
                            k.cp("act", Vb[:, :, :, 0:128], Xv)
                            KT = XT_ring.next()
                            transposes_to(Xk, KT, m)
                            shared[m][2] = Vb
                            shared[m][3] = KT
                        KT = shared[m][3]
                        st_ = STr.next()
                        box["st"] = st_
                        for row in range(16):
                            k.mm(st_[:, row * 32:(row + 1) * 32], KT[:, row, g, :], qg, start=True, stop=False)
                            k.mm(st_[:, row * 32:(row + 1) * 32], Mb2h[:, g, hb, :], i48, start=False, stop=True)

                    def s1():
                        sb_ = stsb_ring.next()
                        k.tt("dve", sb_.re("p (a r i) -> p a r i", a=16, r=4), box["st"].re("p (a r i) -> p a r i", a=16, r=4),
                             abias[:, m * 16:(m + 1) * 16, g * 4:(g + 1) * 4].us(3).bc([128, 16, 4, 8]), ALU.add)
                        box["ET"] = ET_ring.next()
                        k.act(box["ET"], sb_, AF.Exp)

                    def s2():
                        Vb = shared[m][2]
                        for row in range(16):
                            k.mm(OG[g][:32, 0:129], box["ET"][:, row * 32:(row + 1) * 32], Vb[:, row, g, 0:129],
                                 start=(m == 0 and row == 0), stop=False)
                    return [sG, s0, s1, s2]
                run_pipeline([mk_sel(m, g) for m in range(8) for g in range(2)])
                for g in range(2):
                    qg = q2g(bl, g)
                    st_ = STr.next()
                    k.mm(st_[:8, 0:32], ks2T[:, bl, g, :], qg)
                    sb_ = stsb_ring.next()
                    k.tt("dve", sb_[:8, 0:32].re("p (r i) -> p r i", r=4), st_[:8, 0:32].re("p (r i) -> p r i", r=4),
                         tbias[:, g * 4:(g + 1) * 4].us(2).bc([8, 4, 8]), ALU.add)
                    ET = ET_ring.next()
                    k.act(ET[:8, 0:32], sb_[:8, 0:32], AF.Exp)
                    k.tt("dve", ET[:8, 0:32].re("p (r i) -> p r i", r=4), ET[:8, 0:32].re("p (r i) -> p r i", r=4),
                         tmask.us(1).bc([8, 4, 8]), ALU.mult)
                    k.mm(OG[g][:32, 0:129], ET[:8, 0:32], V2s[:8, bl, g, 0:129], start=False, stop=True)
                    finish2(OG[g], o2acc, bl, g, 1)
                    st_ = STr.next()
                    for w in range(4):
                        k.mm(st_[:, w * 32:(w + 1) * 32], KwT2[:, g, w, :], qg)
                    sb_ = stsb_ring.next()
                    k.tt("dve", sb_[:, 0:128].re("p (w r i) -> p w r i", w=4, r=4), st_[:, 0:128].re("p (w r i) -> p w r i", w=4, r=4),
                         swbias[:, :, g * 4:(g + 1) * 4].us(3).bc([128, 4, 4, 8]), ALU.add)
                    ET = ET_ring.next()
                    k.act(ET[:, 0:128], sb_[:, 0:128], AF.Exp)
                    k.tt("dve", ET[:, 0:128].re("p (w r i) -> p w r i", w=4, r=4), ET[:, 0:128].re("p (w r i) -> p w r i", w=4, r=4),
                         swmask.us(2).bc([128, 4, 4, 8]), ALU.mult)
                    for w in range(4):
                        k.mm(OC[:32, 0:129], ET[:, w * 32:(w + 1) * 32], Vwb[:, w, g, 0:129], start=(w == 0), stop=False)
                    st_ = STr.next()
                    k.mm(st_[:8, 0:32], kw2T[:, bl, g, :], qg)
                    sb_ = stsb_ring.next()
                    k.tt("dve", sb_[:8, 0:32].re("p (r i) -> p r i", r=4), st_[:8, 0:32].re("p (r i) -> p r i", r=4),
                         tbias[:, g * 4:(g + 1) * 4].us(2).bc([8, 4, 8]), ALU.add)
                    ET = ET_ring.next()
                    k.act(ET[:8, 0:32], sb_[:8, 0:32], AF.Exp)
                    k.tt("dve", ET[:8, 0:32].re("p (r i) -> p r i", r=4), ET[:8, 0:32].re("p (r i) -> p r i", r=4),
                         tmask.us(1).bc([8, 4, 8]), ALU.mult)
                    k.mm(OC[:32, 0:129], ET[:8, 0:32], V2w[:8, bl, g, 0:129], start=False, stop=True)
                    finish2(OC, o2acc, bl, g, 2)
                sz2 = sz2_ring.next()
                k.dma(sz2, sz_scr[1024 + bl * 8:1024 + bl * 8 + 8, :])
                mix2 = mix2_ring.next()
                k.tt("dve", mix2, o2acc, sz2, ALU.mult)
                pt = MS.next().cast(BF16)
                for e_ in range(8):
                    k.tr(pt[:, e_ * 8:(e_ + 1) * 8], mix2[:8, e_ * 128:(e_ + 1) * 128], self.ident[:8, :8])
                mix2a = mix2a_ring.next()
                k.cp("act", mix2a, pt[:, 0:64].re("p (a t) -> p a t", a=8))
                k.dma(mixT[1024:2048, 1024 + bl * 8:1024 + bl * 8 + 8].re("(a p) t -> p a t", p=128), mix2a)


def static_tables():
    f32 = np.float32
    sl = np.array(SLOPES, f32)
    kk = np.arange(128)
    dbias = (sl[None, :] * (kk[:, None] - 127)).astype(f32)
    n = np.arange(512)
    j = np.arange(128)
    o = n[:, None] - 4 * j[None, :]
    C = np.where((o == -1) | (o == 3), 0.5, np.where((o >= 0) & (o <= 2), 1.0, 0.0))
    C[511:, :] = 0.0
    ctab = C.reshape(4, 128, 128).transpose(1, 0, 2).reshape(128, 512)
    i4 = np.tile(np.eye(128, dtype=f32), (1, 4))
    tri = np.stack([(kk[:, None] <= kk[None, :]), (kk[:, None] > kk[None, :])], axis=1).astype(f32).reshape(128, 256)
    return {"dbias": dbias, "ctab": ctab.astype(NPBF), "i4": i4.astype(NPBF), "tri": tri.astype(NPBF)}


def core_tables(c):
    f32 = np.float32
    sl = np.array(SLOPES, f32)
    kk = np.arange(128)
    selb = np.zeros((128, 288, 8), f32)
    winb = np.zeros((128, 8, 5, 8), f32)
    cmpb = np.zeros((128, 8, 4, 8), f32)
    cmpm = np.zeros((128, 8, 4, 128), f32)
    forced = np.zeros((128, 8, 128), f32)
    notown = np.zeros((128, 8, 128), f32)
    off = 0
    q = np.arange(128)
    j = np.arange(128)
    for s in range(8):
        qb = c + 8 * s
        for t in range(8 * s + 8):
            if t < qb:
                selb[:, off + t, :] = sl[None, :] * (kk[:, None] - 127 - 128 * (qb - t))
        off += 8 * s + 8
        for o in range(5):
            if o == 4 or qb - 4 + o >= 0:
                winb[:, s, o, :] = sl[None, :] * (kk[:, None] - 127 - 128 * (4 - o))
            else:
                winb[:, s, o, :] = NEG
        for nt in range(4):
            n = nt * 128 + kk
            end = 16 * n + 31
            d = end - (128 * qb + 127)
            v = sl[None, :] * d[:, None]
            bad = (n >= 511) | (d > 0)
            v[bad, :] = NEG
            cmpb[:, s, nt, :] = v
            cmpm[:, s, nt, :] = ((end[:, None] <= 128 * qb + q[None, :]) & (n[:, None] < 511)).astype(f32)
        cur = (128 * qb + q) // 64
        fm = (j[None, :] == 0) | (j[None, :] == cur[:, None]) | (j[None, :] == cur[:, None] - 1)
        inv = j[None, :] > cur[:, None]
        forced[:, s, :] = np.where(fm, 1e30, np.where(inv, -1e30, 0.0))
        notown[:, s, :] = (j[None, :] < 2 * qb).astype(f32)
    return {"selbias": selb.reshape(128, -1), "winbias": winb.reshape(128, -1), "cmpbias": cmpb.reshape(128, -1),
            "cmpmask": cmpm.reshape(128, -1).astype(NPBF), "forced": forced.reshape(128, -1),
            "notown": notown.reshape(128, -1)}


def sample_tables():
    f32 = np.float32
    sl = np.array(SLOPES, f32)
    p = np.arange(128)
    ref = PAST + 7
    marr = np.tile(np.arange(8, dtype=f32)[None, :], (128, 1))
    m = np.arange(8)
    n = 8 * p[:, None] + m[None, :]
    end = 16 * n + 31
    c2b = (sl[None, None, :] * (end[:, :, None] - ref)).astype(f32)
    c2b[n >= 1023] = NEG
    j = np.arange(256)
    o = n[:, :, None] - 4 * j[None, None, :]
    C = np.where((o == -1) | (o == 3), 0.5, np.where((o >= 0) & (o <= 2), 1.0, 0.0))
    C[n >= 1023] = 0.0
    forced2 = np.zeros((8, 256), f32)
    forced2[:, 0] = 1e30
    forced2[:, 255] = 1e30
    row = np.arange(128)
    ab = (sl[None, None, :] * ((128 * p[:, None] + row[None, :])[:, :, None] - ref)).astype(f32)
    w = np.arange(4)
    wpos = w[None, :] * 128 + p[:, None]
    swb = (sl[None, None, :] * (wpos[:, :, None] - 519)).astype(f32)
    i = np.arange(8)
    swm = (wpos[:, :, None] > i[None, None, :]).astype(f32)
    tb = (sl[None, :] * (i[:, None] - 7)).astype(f32)
    tm = (i[:, None] <= i[None, :]).astype(f32)
    i48 = np.tile(np.eye(8, dtype=f32), (1, 4))
    selr = np.eye(32, dtype=f32)
    selsum = np.tile(np.eye(8, dtype=f32), (4, 1))
    return {"marr": marr, "c2bias": c2b.reshape(128, 64), "c2tab": C.reshape(128, -1).astype(NPBF), "forced2": forced2,
            "abias": ab.reshape(128, -1), "swbias": swb.reshape(128, 32), "swmask": swm.reshape(128, 32).astype(NPBF),
            "tbias": tb, "tmask": tm.astype(NPBF), "i48": i48.astype(NPBF), "selr": selr, "selsum": selsum}


_PROG = {}


def get_prog(stages):
    if stages not in _PROG:
        _PROG[stages] = Prog(stages)
    return _PROG[stages]


STAGES = 6


def kernel(**inputs):
    f32 = np.float32
    xp = np.asarray(inputs["x_prompt"], f32)[0]
    xsm = np.asarray(inputs["x_sample"], f32)
    prog = get_prog(STAGES)
    ident = np.eye(128, dtype=f32)
    gcol = np.ascontiguousarray(np.asarray(inputs["g_norm"], f32).reshape(16, 128).T)
    cvec = np.concatenate([np.asarray(inputs["w_dw"], f32), np.asarray(inputs["b_dw"], f32)[None],
                           np.asarray(inputs["ln_g"], f32)[None], np.asarray(inputs["ln_b"], f32)[None],
                           np.asarray(inputs["b_pw2"], f32)[None]], axis=0)
    gvec = np.stack([np.asarray(inputs[n], f32) for n in ("g_q", "g_k_slc", "g_k_win", "g_k_cmp")], axis=0)
    xpad = np.concatenate([np.zeros((512, D), f32), xp], axis=0)
    in_maps = []
    statics = static_tables()
    for c in range(NCORES):
        xB = np.stack([xpad[512 + (c + 8 * s) * 128 - 32: 512 + (c + 8 * s) * 128 + 128] for s in range(8)], axis=0)
        m = {
            "xB": np.ascontiguousarray(xB),
            "xS": np.ascontiguousarray(xsm[4 * c:4 * c + 4].reshape(32, D)),
            "w_in": np.asarray(inputs["w_in"], f32),
            "gcol": gcol, "cvec": np.ascontiguousarray(cvec), "gvec": np.ascontiguousarray(np.broadcast_to(gvec.reshape(1, 512), (128, 512))),
            "w_pw2": np.asarray(inputs["w_pw2"], f32),
            "ident": ident.astype(NPBF), "identf": ident,
            "sconv": np.ascontiguousarray(np.asarray(inputs["state_conv"], f32)[4 * c:4 * c + 4].reshape(120, 1024)),
        }
        if STAGES >= 3:
            m["xA"] = xp
            m["xH"] = np.ascontiguousarray(np.stack([np.stack([xpad[512 + (c + 8 * s - 4 + o) * 128: 512 + (c + 8 * s - 3 + o) * 128]
                                                     for o in range(4)], axis=0) for s in range(8)], axis=0))
            for nm in ("w_out", "w_cmp_k1", "w_cmp_v1", "w_cmp_k2", "w_cmp_v2", "pe_cmp_k", "pe_cmp_v"):
                m[nm] = np.asarray(inputs[nm], f32)
            m.update(statics)
            m.update(core_tables(c))
        if STAGES >= 6:
            m["ptab"] = np.ascontiguousarray(np.asarray(inputs["page_table"], np.int32)[4 * c:4 * c + 4])
            for nm in ("cache_k_cmp", "cache_v_cmp", "cache_k_slc", "cache_v_slc"):
                m[nm] = np.asarray(inputs[nm], f32).reshape(NPHYS * 8, 4096)
            m["skw"] = np.ascontiguousarray(np.asarray(inputs["state_k_win"], f32)[4 * c:4 * c + 4].reshape(4, 512, 256))
            m["svw"] = np.ascontiguousarray(np.asarray(inputs["state_v_win"], f32)[4 * c:4 * c + 4].reshape(4, 512, 256))
            m.update(sample_tables())
        in_maps.append({kk: m[kk] for kk in prog.din})
    res = run_bass_kernel_spmd(prog.k.nc, in_maps, core_ids=list(range(NCORES)))
    R = res.results

    def gat(name):
        return [np.asarray(R[c][name]) for c in range(NCORES)]

    def prow(name, width):
        out = np.zeros((T, width), f32)
        arr = gat(name)
        for c in range(NCORES):
            for s in range(8):
                qb = c + 8 * s
                out[qb * 128:(qb + 1) * 128] = arr[c][s]
        return out

    def srow(name, width):
        return np.concatenate([a.reshape(4, 8, width) for a in gat(name)], axis=0)

    y_prompt = prow("o_y", D)[None]
    y_sample = srow("o_y2", D)
    kc = prow("o_kc", 256).reshape(1, T, 2, 128); vc = prow("o_vc", 256).reshape(1, T, 2, 128)
    ks = prow("o_ks", 256).reshape(1, T, 2, 128); vs = prow("o_vs", 256).reshape(1, T, 2, 128)
    kw = prow("o_kw", 256).reshape(1, T, 2, 128); vw = prow("o_vw", 256).reshape(1, T, 2, 128)
    conv_p = np.asarray(R[7]["o_convp"])[None]
    kc2 = srow("o_kc2", 256).reshape(32, 8, 2, 128); vc2 = srow("o_vc2", 256).reshape(32, 8, 2, 128)
    ks2 = srow("o_ks2", 256).reshape(32, 8, 2, 128); vs2 = srow("o_vs2", 256).reshape(32, 8, 2, 128)
    conv_s = np.concatenate([a.reshape(4, 30, 1024) for a in gat("o_convs")], axis=0)
    kwin2 = np.concatenate(gat("o_kwin"), axis=0).reshape(32, 512, 2, 128)
    vwin2 = np.concatenate(gat("o_vwin"), axis=0).reshape(32, 512, 2, 128)
    return (y_prompt, y_sample, kc, vc, ks, vs, kw[:, T - 512:], vw[:, T - 512:], conv_p,
            kc2, vc2, ks2, vs2, kwin2, vwin2, conv_s)
```

```python
import contextlib
import numpy as np
import ml_dtypes
import concourse.bass as bass
import concourse.mybir as mybir
from concourse.bass_utils import run_bass_kernel_spmd

F32 = mybir.dt.float32
BF16 = mybir.dt.bfloat16
I32 = mybir.dt.int32
AF = mybir.ActivationFunctionType
ALU = mybir.AluOpType
AX = mybir.AxisListType
NPBF = ml_dtypes.bfloat16

NCORES = 8
D = 2048
T = 8192
HD = 128
NQB = 64
NSLOT = 8
DIN = 6680
EPS = 1e-6
SCALE = HD ** -0.5
PAST = 16384
NPAGE = 128
NPHYS = 5120
SLOPES = [2.0 ** (-(h + 1)) for h in range(8)]
NEG = -30000.0
C_UA, C_UB, C_ZC, C_Q, C_KC, C_KS, C_KW, C_GT, C_ZN = 0, 1024, 2048, 3072, 4096, 4608, 5120, 5632, 5656
TOKC = 8 * 160 + 32
NOWN = 1024 + 32
NS_DMA = 8


class Tok:
    __slots__ = ("w", "r")

    def __init__(self):
        self.w = None
        self.r = {}


class V:
    def __init__(self, ap, toks):
        self.ap = ap
        self.toks = toks

    def __getitem__(self, i):
        return V(self.ap[i], self.toks)

    def re(self, pat, **kw):
        return V(self.ap.rearrange(pat, **kw), self.toks)

    def bc(self, shape):
        return V(self.ap.broadcast_to(shape), self.toks)

    def us(self, dim):
        return V(self.ap.unsqueeze(dim), self.toks)

    def cast(self, dt):
        return V(self.ap.bitcast(dt), self.toks)

    def tk(self, tok):
        return V(self.ap, [tok])


class Ring:
    def __init__(self, items):
        self.items = items
        self.i = 0

    def next(self):
        it = self.items[self.i % len(self.items)]
        self.i += 1
        return it


def run_pipeline(jobs):
    if not jobs:
        return
    ns = max(len(j) for j in jobs)
    for step in range(len(jobs) + ns - 1):
        for st in range(ns):
            t = step - st
            if 0 <= t < len(jobs) and st < len(jobs[t]):
                jobs[t][st]()

ENG = ("pe", "act", "dve", "pool", "sp")


class Rec:
    def __init__(self):
        self.nc = bass.Bass("TRN2", target_bir_lowering=False)
        self.gst = contextlib.ExitStack()
        self.ops = {e: [] for e in ENG}
        self.cnt = {e: 0 for e in ENG}
        self.seen = {e: {} for e in ENG}
        self.dman = {e: 0 for e in ENG}
        self.sems = {}
        self.slot_last = {}
        for e in ("pe", "act", "dve", "pool"):
            self.sems[e] = self.gst.enter_context(self.nc.semaphore("c_" + e))
        for q in ("sp", "pool", "act"):
            for i in range(NS_DMA):
                nm = "d_%s_%d" % (q, i)
                self.sems[nm] = self.gst.enter_context(self.nc.semaphore(nm))
        self.uid = 0
        self.stack = self.gst

    def sb(self, name, shape, dt):
        self.uid += 1
        t = self.stack.enter_context(self.nc.sbuf_tensor("%s_%d" % (name, self.uid), list(shape), dt))
        return V(t[tuple(slice(None) for _ in shape)], [Tok()])

    def ring(self, name, n, shape, dt):
        return Ring([self.sb(name + str(i), shape, dt) for i in range(n)])

    def psum(self, name):
        t = self.gst.enter_context(self.nc.psum_tensor(name, [128, 512], F32))
        return V(t[:, :], [Tok()])

    def dram(self, name, shape, dt, kind):
        t = self.nc.dram_tensor(name, list(shape), dt, kind=kind)
        return V(t.ap(), [Tok()])

    @contextlib.contextmanager
    def phase(self):
        old = self.stack
        with contextlib.ExitStack() as st:
            self.stack = st
            yield
            self.flush()
        self.stack = old

    def _rec(self, eng, fn, R, W, dma=False):
        waits = {}

        def need(dep):
            if dep is None:
                return
            k, v = dep
            if eng == "pe" and k == "pe":
                return
            if waits.get(k, 0) < v:
                waits[k] = v

        for t in R:
            need(t.w)
        for t in W:
            need(t.w)
            for k, v in t.r.items():
                need((k, v))
        if dma:
            n = self.dman[eng]
            self.dman[eng] += 1
            slot = "d_%s_%d" % (eng, n % NS_DMA)
            rnd = n // NS_DMA
            if rnd > 0:
                need((slot, 16 * rnd))
            dep = (slot, 16 * (rnd + 1))
            inc = (slot, 16)
            self.slot_last[slot] = dep[1]
        else:
            self.cnt[eng] += 1
            dep = (eng, self.cnt[eng])
            inc = (eng, 1)
        wl = []
        seen = self.seen[eng]
        for k, v in waits.items():
            if seen.get(k, 0) < v:
                seen[k] = v
                wl.append((k, v))
        self.ops[eng].append((fn, wl, inc))
        for t in R:
            if t.r.get(dep[0], 0) < dep[1]:
                t.r[dep[0]] = dep[1]
        for t in W:
            t.w = dep
            t.r = {}

    def flush(self, final=False):
        if final:
            wl = [(k, v) for k, v in self.slot_last.items()]
            self.ops["sp"].append((None, wl, None))
        ops_all = self.ops
        self.ops = {e: [] for e in ENG}
        sems = self.sems
        with self.nc.Block() as blk:
            for eng, deco in (("pe", blk.tensor), ("act", blk.scalar), ("dve", blk.vector),
                              ("pool", blk.gpsimd), ("sp", blk.sync)):
                ops = ops_all[eng]

                def body(e, ops=ops):
                    for fn, wl, inc in ops:
                        for k, v in wl:
                            e.wait_ge(sems[k], v)
                        if fn is not None:
                            fn(e).then_inc(sems[inc[0]], inc[1])

                deco(body)

    @staticmethod
    def _s(x):
        return x.ap if isinstance(x, V) else x

    @staticmethod
    def _t(*xs):
        out = []
        for x in xs:
            if isinstance(x, V):
                out += x.toks
        return out

    def mm(self, out, lhsT, rhs, start=True, stop=True):
        self._rec("pe", lambda e: e.matmul(out=out.ap, lhsT=lhsT.ap, rhs=rhs.ap, start=start, stop=stop),
                  self._t(lhsT, rhs), out.toks)

    def tr(self, out, in_, ident):
        self._rec("pe", lambda e: e.transpose(out=out.ap, in_=in_.ap, identity=ident.ap),
                  self._t(in_, ident), out.toks)

    def act(self, out, in_, func, bias=None, scale=None, accum=None):
        kw = {}
        if bias is not None:
            kw["bias"] = self._s(bias)
        if scale is not None:
            kw["scale"] = self._s(scale)
        if accum is not None:
            kw["accum_out"] = accum.ap
        self._rec("act", lambda e: e.activation(out=out.ap, in_=in_.ap, func=func, **kw),
                  self._t(in_, bias, scale), self._t(out, accum))

    def ts(self, eng, out, in0, s1, s2, op0, op1=None, accum=None):
        kw = {}
        if op1 is not None:
            kw["op1"] = op1
        if accum is not None:
            kw["accum_out"] = accum.ap
        a1, a2 = self._s(s1), self._s(s2)
        self._rec(eng, lambda e: e.tensor_scalar(out=out.ap, in0=in0.ap, scalar1=a1, scalar2=a2, op0=op0, **kw),
                  self._t(in0, s1, s2), self._t(out, accum))

    def tt(self, eng, out, in0, in1, op):
        self._rec(eng, lambda e: e.tensor_tensor(out=out.ap, in0=in0.ap, in1=in1.ap, op=op),
                  self._t(in0, in1), out.toks)

    def stt(self, eng, out, in0, sc, in1, op0, op1):
        a = self._s(sc)
        self._rec(eng, lambda e: e.scalar_tensor_tensor(out=out.ap, in0=in0.ap, scalar=a, in1=in1.ap, op0=op0, op1=op1),
                  self._t(in0, sc, in1), out.toks)

    def cp(self, eng, out, in_):
        if eng == "act":
            self._rec("act", lambda e: e.copy(out=out.ap, in_=in_.ap), in_.toks, out.toks)
        else:
            self._rec(eng, lambda e: e.tensor_copy(out=out.ap, in_=in_.ap), in_.toks, out.toks)

    def red(self, out, in_, op=ALU.add, axis=AX.X):
        self._rec("dve", lambda e: e.tensor_reduce(out=out.ap, in_=in_.ap, axis=axis, op=op), in_.toks, out.toks)

    def recip(self, out, in_):
        self._rec("dve", lambda e: e.reciprocal(out=out.ap, in_=in_.ap), in_.toks, out.toks)

    def memset(self, eng, out, val):
        self._rec(eng, lambda e: e.memset(out.ap, val), [], out.toks)

    def max8(self, out, in_):
        self._rec("dve", lambda e: e.max(out=out.ap, in_=in_.ap), in_.toks, out.toks)

    def mrep(self, out, rep, vals, imm):
        self._rec("dve", lambda e: e.match_replace(out=out.ap, in_to_replace=rep.ap, in_values=vals.ap, imm_value=imm),
                  self._t(rep, vals), out.toks)

    def dma(self, out, in_, q="sp"):
        self._rec(q, lambda e: e.dma_start(out=out.ap, in_=in_.ap), in_.toks, out.toks, dma=True)

    def gather(self, out, in_, idx):
        self._rec("pool", lambda e: e.indirect_dma_start(
            out=out.ap, out_offset=None, in_=in_.ap,
            in_offset=bass.IndirectOffsetOnAxis(ap=idx.ap, axis=0)),
            self._t(in_, idx), out.toks, dma=True)


class Prog:
    def __init__(self, stages):
        self.stages = stages
        k = self.k = Rec()
        self.din = {}
        self.dout = {}
        self.ps = Ring([k.psum("ps%d" % i) for i in range(8)])
        self.build()

    def inp(self, name, shape, dt=F32):
        v = self.k.dram(name, shape, dt, "ExternalInput")
        self.din[name] = v
        return v

    def outp(self, name, shape, dt=F32):
        v = self.k.dram(name, shape, dt, "ExternalOutput")
        self.dout[name] = v
        return v

    def scr(self, name, shape, dt):
        return self.k.dram(name, shape, dt, "Internal")

    def rstd_from_ss(self, st, R, n, inv_n):
        k = self.k
        k.act(st[:R, n:2 * n], st[:R, 0:n], AF.Ln, scale=inv_n, bias=self.epsc[:R, 0:1])
        k.act(st[:R, 2 * n:3 * n], st[:R, n:2 * n], AF.Exp, scale=-0.5)
        return st[:R, 2 * n:3 * n]

    def silu_from(self, eng, out, src, R, n, bias=None):
        k = self.k
        e = self.etmp.next()
        if bias is None:
            k.act(e[:R, :n], src, AF.Exp, scale=-1.0)
            k.ts("dve", e[:R, :n], e[:R, :n], 1.0, None, ALU.add)
            k.recip(e[:R, :n], e[:R, :n])
            k.tt("dve", out, src, e[:R, :n], ALU.mult)
        else:
            z = self.etmp.next()
            k.ts("dve", z[:R, :n], src, bias, None, ALU.add)
            k.act(e[:R, :n], z[:R, :n], AF.Exp, scale=-1.0)
            k.ts("dve", e[:R, :n], e[:R, :n], 1.0, None, ALU.add)
            k.recip(e[:R, :n], e[:R, :n])
            k.tt("dve", out, z[:R, :n], e[:R, :n], ALU.mult)

    def front(self, src, R, dst):
        xt = self.front_load(src, R)
        self.front_rest(xt, R, dst)

    def front_load(self, src, R):
        xt = self.xt_ring.next()
        self.k.dma(xt[:R], src)
        return xt

    def front_rest(self, xt, R, dst):
        self.front_T(self.front_norm(xt, R), R, dst)

    def front_norm(self, xt, R):
        k = self.k
        xs = self.xs_ring.next()
        st = self.st_ring.next()
        k.act(xs[:R], xt[:R], AF.Square, accum=st[:R, 0:1])
        rs = self.rstd_from_ss(st, R, 1, 1.0 / D)
        k.ts("dve", xs[:R], xt[:R], rs, None, ALU.mult)
        return xs

    def front_T(self, xs, R, dst):
        k = self.k
        for half in range(2):
            pb = self.ps.next().cast(BF16)
            for j in range(8):
                kc = half * 8 + j
                k.tr(pb[:, j * R:(j + 1) * R], xs[:R, kc * 128:(kc + 1) * 128], self.ident[:R, :R])
            k.cp("act" if half == 0 else "dve", dst[:, half * 8:(half + 1) * 8, :],
                 pb[:, 0:8 * R].re("p (a b) -> p a b", a=8))

    def wload(self, c0, wd, scale_g=True, src=None, nkc=16):
        k = self.k
        wb = self.wb_ring.next()
        src = self.din["w_in"] if src is None else src
        for q4 in range(nkc // 4):
            stg = self.wstg_ring.next()
            k.dma(stg[:, :, :wd], src[q4 * 512:(q4 + 1) * 512, c0:c0 + wd].re("(a p) c -> p a c", p=128),
                  q="sp")
            for a in range(4):
                kc = q4 * 4 + a
                if scale_g:
                    k.ts("pool", wb[:, kc, :wd], stg[:, a, :wd], self.gcol[:, kc:kc + 1], 1.0, ALU.mult, ALU.mult)
                else:
                    k.cp("pool", wb[:, kc, :wd], stg[:, a, :wd])
        return wb

    def normed(self, pb, R, nh, gidx, out_f32=None, out_bf=None, mul=None):
        k = self.k
        sq = self.etmp.next()
        st = self.st_ring.next()
        k.act(sq[:R, :nh * 128], pb, AF.Square)
        k.red(st[:R, 0:nh], sq[:R, :nh * 128].re("p (h d) -> p h d", h=nh))
        rs = self.rstd_from_ss(st, R, nh, 1.0 / HD)
        tmp = self.etmp.next()
        t3 = tmp[:R, :nh * 128].re("p (h d) -> p h d", h=nh)
        k.tt("dve", t3, pb.re("p (h d) -> p h d", h=nh), rs.us(2).bc([R, nh, 128]), ALU.mult)
        g = self.gvec[:R, gidx:gidx + 1, :].bc([R, nh, 128])
        if out_f32 is not None:
            k.tt("dve", out_f32.re("p (h d) -> p h d", h=nh), t3, g, ALU.mult)
            if out_bf is not None:
                k.cp("dve", out_bf, out_f32)
        else:
            k.tt("dve", out_bf.re("p (h d) -> p h d", h=nh), t3, g, ALU.mult)

    def build(self):
        k = self.k
        inp, outp = self.inp, self.outp
        xB = inp("xB", [8, 160, D])
        xS = inp("xS", [32, D])
        w_in = inp("w_in", [D, DIN])
        gcol_d = inp("gcol", [128, 16])
        cvec_d = inp("cvec", [35, 1024])
        gvec_d = inp("gvec", [128, 512])
        w_pw2 = inp("w_pw2", [1024, 1024])
        ident_d = inp("ident", [128, 128], BF16)
        identf_d = inp("identf", [128, 128])
        sconv_d = inp("sconv", [120, 1024])

        o_kc = outp("o_kc", [8, 128, 256]); o_vc = outp("o_vc", [8, 128, 256])
        o_ks = outp("o_ks", [8, 128, 256]); o_vs = outp("o_vs", [8, 128, 256])
        o_kw = outp("o_kw", [8, 128, 256]); o_vw = outp("o_vw", [8, 128, 256])
        o_convp = outp("o_convp", [30, 1024])
        o_kc2 = outp("o_kc2", [32, 256]); o_vc2 = outp("o_vc2", [32, 256])
        o_ks2 = outp("o_ks2", [32, 256]); o_vs2 = outp("o_vs2", [32, 256])
        o_convs = outp("o_convs", [120, 1024])
        o_kwin = outp("o_kwin", [4, 512, 256]); o_vwin = outp("o_vwin", [4, 512, 256])
        o_y = outp("o_y", [8, 128, D])
        o_y2 = outp("o_y2", [32, D])

        mixT = self.scr("mixT", [D, NOWN], BF16)
        qT_scr = self.scr("qT_scr", [8, 128, 8, 128], BF16)
        sz_scr = self.scr("sz_scr", [NOWN, 1024], BF16)

        self.ident = k.sb("ident", [128, 128], BF16)
        self.identf = k.sb("identf", [128, 128], F32)
        self.gcol = k.sb("gcol", [128, 16], F32)
        self.gvec = k.sb("gvec", [128, 4, 128], F32)
        self.cvT = k.sb("cvT", [128, 8, 35], F32)
        self.epsc = k.sb("epsc", [128, 1], F32)
        ksT_own = k.sb("ksT_own", [128, 8, 2, 128], BF16)
        Vs_own = k.sb("Vs_own", [128, 8, 2, 130], BF16)
        KwT_own = k.sb("KwT_own", [128, 8, 2, 128], BF16)
        Vw_own = k.sb("Vw_own", [128, 8, 2, 130], BF16)
        gates = k.sb("gates", [128, 8, 24], F32)
        q2T = k.sb("q2T", [128, 4, 8, 8], BF16)
        ks2T = k.sb("ks2T", [128, 4, 2, 8], BF16)
        kw2T = k.sb("kw2T", [128, 4, 2, 8], BF16)
        V2s = k.sb("V2s", [8, 4, 2, 130], BF16)
        V2w = k.sb("V2w", [8, 4, 2, 130], BF16)
        gates2 = k.sb("gates2", [8, 4, 24], F32)

        self.pers = dict(ksT_own=ksT_own, Vs_own=Vs_own, KwT_own=KwT_own, Vw_own=Vw_own, gates=gates, q2T=q2T,
                         ks2T=ks2T, kw2T=kw2T, V2s=V2s, V2w=V2w, gates2=gates2)
        self.dr = dict(mixT=mixT, qT_scr=qT_scr, sz_scr=sz_scr, xB=xB, xS=xS, o_y=o_y, o_y2=o_y2, o_kwin=o_kwin, o_vwin=o_vwin)
        with k.phase():
            k.dma(self.ident, ident_d)
            k.dma(self.identf, identf_d)
            k.dma(self.gcol, gcol_d)
            k.dma(self.gvec, gvec_d.re("p (a d) -> p a d", a=4))
            k.memset("dve", self.epsc, EPS)
            k.ts("dve", self.gvec[:, 0, :], self.gvec[:, 0, :], SCALE, None, ALU.mult)
            cv = k.sb("cv", [35, 1024], F32)
            k.dma(cv, cvec_d)
            for j in range(8):
                pb = self.ps.next()
                k.tr(pb[:, 0:35], cv[:, j * 128:(j + 1) * 128], self.identf[:35, :35])
                k.cp("dve", self.cvT[:, j, :], pb[:, 0:35])
            for t in (Vs_own, Vw_own, V2s, V2w):
                R = 8 if t in (V2s, V2w) else 128
                k.memset("dve", t[:R].re("p a g c -> p (a g) c")[:, :, 128:129], 1.0)
                k.memset("dve", t[:R].re("p a g c -> p (a g) c")[:, :, 129:130], 0.0)

        if self.stages < 1:
            k.flush(final=True)
            return

        with contextlib.ExitStack() as bst:
            old = k.stack
            k.stack = bst
            xTB = k.sb("xTB", [128, 16, TOKC], BF16)
            yT = k.sb("yT", [128, 8, NOWN], BF16)
            k.stack = old
            with k.phase():
                self.xt_ring = k.ring("xt", 2, [128, D], F32)
                self.xs_ring = k.ring("xs", 2, [128, D], BF16)
                self.st_ring = k.ring("st", 4, [128, 16], F32)
                self.wb_ring = k.ring("wb", 2, [128, 16, 512], BF16)
                self.wstg_ring = k.ring("wstg", 2, [128, 4, 512], F32)
                self.etmp = k.ring("etmp", 3, [128, 512], F32)
                uT_ring = k.ring("uT", 2, [128, 1280 + 4 * 38], F32)
                acc_ring = k.ring("acc", 2, [128, NOWN], F32)
                ptmp = k.sb("ptmp", [128, NOWN], F32)
                scT = k.sb("scT", [128, 8, 120], F32)
                uP = k.sb("uP", [128, 8, 30], F32)
                uS = k.sb("uS", [128, 8, 4, 30], F32)
                sc_in = k.sb("sc_in", [120, 1024], F32)
                k.dma(sc_in, sconv_d)
                for j in range(8):
                    pb = self.ps.next()
                    k.tr(pb[:, 0:120], sc_in[:, j * 128:(j + 1) * 128], self.identf[:120, :120])
                    k.cp("dve", scT[:, j, :], pb[:, 0:120])
                for s in range(8):
                    self.front(xB[s, 32:160, :], 128, xTB[:, :, s * 160 + 32:s * 160 + 160])
                    self.front(xB[s, 0:32, :], 32, xTB[:, :, s * 160:s * 160 + 32])
                self.front(xS[:, :], 32, xTB[:, :, 1280:1312])
                GROUPS = [(0, 480), (480, 480), (960, 352)]
                for half in range(2):
                    wa = self.wload(C_UA + 512 * half, 512)
                    wbk = self.wload(C_UB + 512 * half, 512)
                    for sub in range(4):
                        j = half * 4 + sub
                        uT = uT_ring.next()
                        for (c0, n) in GROUPS:
                            pa = self.ps.next()
                            pbb = self.ps.next()
                            for kc in range(16):
                                k.mm(pa[:, :n], wa[:, kc, sub * 128:(sub + 1) * 128], xTB[:, kc, c0:c0 + n],
                                     start=(kc == 0), stop=(kc == 15))
                            for kc in range(16):
                                k.mm(pbb[:, :n], wbk[:, kc, sub * 128:(sub + 1) * 128], xTB[:, kc, c0:c0 + n],
                                     start=(kc == 0), stop=(kc == 15))
                            e = self.etmp.next()
                            k.act(e[:, :n], pbb[:, :n], AF.Exp, scale=-1.0)
                            k.ts("dve", e[:, :n], e[:, :n], 1.0, None, ALU.add)
                            k.recip(e[:, :n], e[:, :n])
                            if c0 < 960:
                                k.tt("dve", uT[:, c0:c0 + n], pa[:, :n], e[:, :n], ALU.mult)
                            else:
                                k.tt("dve", uT[:, 960:1280], pa[:, :320], e[:, :320], ALU.mult)
                                k.tt("dve", uT[:, 1280:1432].re("p (b t) -> p b t", t=38)[:, :, 30:38],
                                     pa[:, 320:352].re("p (b t) -> p b t", t=8),
                                     e[:, 320:352].re("p (b t) -> p b t", t=8), ALU.mult)
                        eng = "pool" if (j % 4 == 3) else "dve"
                        k.cp(eng, uT[:, 1280:1432].re("p (b t) -> p b t", t=38)[:, :, 0:30],
                             scT[:, j, :].re("p (b t) -> p b t", t=30))
                        k.cp(eng, uP[:, j, :], uT[:, 7 * 160 + 130:7 * 160 + 160])
                        k.cp(eng, uS[:, j, :, :], uT[:, 1280:1432].re("p (b t) -> p b t", t=38)[:, :, 8:38])
                        acc = acc_ring.next()
                        a_own = acc[:, 0:1024].re("p (s t) -> p s t", t=128)
                        a_smp = acc[:, 1024:1056].re("p (b t) -> p b t", t=8)
                        u_own = uT[:, 0:1280].re("p (s t) -> p s t", t=160)
                        u_smp = uT[:, 1280:1432].re("p (b t) -> p b t", t=38)
                        wd = self.cvT[:, j, :]
                        k.ts(eng, a_own, u_own[:, :, 2:130], wd[:, 0:1], wd[:, 31:32], ALU.mult, ALU.add)
                        k.ts(eng, a_smp, u_smp[:, :, 0:8], wd[:, 0:1], wd[:, 31:32], ALU.mult, ALU.add)
                        for kk in range(1, 31):
                            if eng == "dve":
                                k.stt(eng, a_own, u_own[:, :, 2 + kk:130 + kk], wd[:, kk:kk + 1], a_own, ALU.mult, ALU.add)
                                k.stt(eng, a_smp, u_smp[:, :, kk:kk + 8], wd[:, kk:kk + 1], a_smp, ALU.mult, ALU.add)
                            else:
                                tp = ptmp[:, 0:1024].re("p (s t) -> p s t", t=128)
                                tq = ptmp[:, 1024:1056].re("p (b t) -> p b t", t=8)
                                k.ts(eng, tp, u_own[:, :, 2 + kk:130 + kk], wd[:, kk:kk + 1], 1.0, ALU.mult, ALU.mult)
                                k.tt(eng, a_own, a_own, tp, ALU.add)
                                k.ts(eng, tq, u_smp[:, :, kk:kk + 8], wd[:, kk:kk + 1], 1.0, ALU.mult, ALU.mult)
                                k.tt(eng, a_smp, a_smp, tq, ALU.add)
                        k.cp(eng, yT[:, j, :], acc)
                cps = k.sb("cps", [128, 1024], F32)
                for j in range(8):
                    pb = self.ps.next()
                    k.tr(pb[0:30, 0:128], uP[:, j, :], self.identf)
                    k.cp("dve", cps[0:30, j * 128:(j + 1) * 128], pb[0:30, 0:128])
                k.dma(o_convp, cps[0:30, :])
                cps2 = k.sb("cps2", [128, 1024], F32)
                for j in range(8):
                    pb = self.ps.next()
                    k.tr(pb[0:120, 0:128], uS[:, j, :, :].re("p b t -> p (b t)"), self.identf)
                    k.cp("dve", cps2[0:120, j * 128:(j + 1) * 128], pb[0:120, 0:128])
                k.dma(o_convs, cps2[0:120, :])

            if self.stages < 2:
                k.flush(final=True)
                return
            with k.phase():
                self.st_ring = k.ring("st", 4, [128, 16], F32)
                self.wb_ring = k.ring("wb", 2, [128, 16, 512], BF16)
                self.wstg_ring = k.ring("wstg", 2, [128, 4, 512], F32)
                self.etmp = k.ring("etmp", 4, [128, 512], F32)
                ones_b = k.sb("ones_b", [128, 128], BF16)
                k.memset("dve", ones_b, 1.0 / 1024)
                mean = k.sb("mean", [128, NOWN], F32)
                rstd = k.sb("rstd", [128, NOWN], F32)
                ysq_ring = k.ring("ysq", 2, [128, 512], BF16)
                LG = [(0, 512), (512, 512), (1024, 32)]
                for (c0, n) in LG:
                    pm = self.ps.next()
                    pq = self.ps.next()
                    for j in range(8):
                        k.mm(pm[:, :n], ones_b, yT[:, j, c0:c0 + n], start=(j == 0), stop=(j == 7))
                    for j in range(8):
                        ysq = ysq_ring.next()
                        k.tt("pool", ysq[:, :n], yT[:, j, c0:c0 + n], yT[:, j, c0:c0 + n], ALU.mult)
                        k.mm(pq[:, :n], ones_b, ysq[:, :n], start=(j == 0), stop=(j == 7))
                    k.cp("act", mean[:, c0:c0 + n], pm[:, :n])
                    e = self.etmp.next()
                    k.tt("dve", e[:, :n], mean[:, c0:c0 + n], mean[:, c0:c0 + n], ALU.mult)
                    k.tt("dve", e[:, :n], pq[:, :n], e[:, :n], ALU.subtract)
                    k.act(e[:, :n], e[:, :n], AF.Ln, bias=self.epsc[:, 0:1])
                    k.act(rstd[:, c0:c0 + n], e[:, :n], AF.Exp, scale=-0.5)
                for j in range(8):
                    for (c0, n) in LG:
                        t = self.etmp.next()
                        k.tt("dve", t[:, :n], yT[:, j, c0:c0 + n], mean[:, c0:c0 + n], ALU.subtract)
                        k.tt("dve", t[:, :n], t[:, :n], rstd[:, c0:c0 + n], ALU.mult)
                        k.ts("dve", t[:, :n], t[:, :n], self.cvT[:, j, 32:33], self.cvT[:, j, 33:34], ALU.mult, ALU.add)
                        self.silu_from("dve", yT[:, j, c0:c0 + n], t[:, :n], 128, n)
                Wp = k.sb("Wp", [128, 8, 1024], BF16)
                for hh in range(2):
                    for q4 in range(2):
                        stg = self.wstg_ring.next()
                        k.dma(stg, w_pw2[q4 * 512:(q4 + 1) * 512, hh * 512:(hh + 1) * 512].re("(a p) c -> p a c", p=128))
                        k.cp("pool", Wp[:, q4 * 4:(q4 + 1) * 4, hh * 512:(hh + 1) * 512], stg)
                szc_ring = k.ring("szc", 2, [128, NOWN], BF16)
                mixc_ring = k.ring("mixc", 2, [128, 512], BF16)
                GROUPS = [(0, 480), (480, 480), (960, 352)]
                for half in range(2):
                    wz = self.wload(C_ZC + 512 * half, 512)
                    for sub in range(4):
                        e_ = half * 4 + sub
                        szc = szc_ring.next()
                        for gi, (c0, n) in enumerate(GROUPS):
                            pz = self.ps.next()
                            for kc in range(16):
                                k.mm(pz[:, :n], wz[:, kc, sub * 128:(sub + 1) * 128], xTB[:, kc, c0:c0 + n],
                                     start=(kc == 0), stop=(kc == 15))
                            ns = 3 if gi < 2 else 2
                            zt = self.etmp.next()
                            self.silu_from("dve", zt[:, :n], pz[:, :n], 128, n)
                            k.cp("pool", szc[:, gi * 384:gi * 384 + ns * 128].re("p (s t) -> p s t", t=128),
                                 zt[:, :ns * 160].re("p (s t) -> p s t", t=160)[:, :, 32:160])
                            if gi == 2:
                                k.cp("pool", szc[:, 1024:1056], zt[:, 320:352])
                        for (c0, n) in LG:
                            po = self.ps.next()
                            for j in range(8):
                                k.mm(po[:, :n], Wp[:, j, e_ * 128:(e_ + 1) * 128], yT[:, j, c0:c0 + n],
                                     start=(j == 0), stop=(j == 7))
                            mc = mixc_ring.next()
                            k.stt("dve", mc[:, :n], po[:, :n], self.cvT[:, e_, 34:35], szc[:, c0:c0 + n], ALU.add, ALU.mult)
                            k.dma(mixT[e_ * 128:(e_ + 1) * 128, c0:c0 + n], mc[:, :n], q="pool")

                stg_ring = k.ring("ostg", 3, [128, 512], F32)
                bfo_ring = k.ring("bfo", 3, [128, 512], BF16)
                units = [(s, s * 160 + 32, 128) for s in range(8)] + [(8 + b, 1280 + 8 * b, 8) for b in range(4)]

                tm_list = []

                def tm_block(c0, wd, handler):
                    tm_list.append((c0, wd, handler))

                def tm_run(wb, wd, handler):
                    for (u, col, R) in units:
                        pb = self.ps.next()
                        for kc in range(16):
                            k.mm(pb[:R, :wd], xTB[:, kc, col:col + R], wb[:, kc, :wd], start=(kc == 0), stop=(kc == 15))
                        handler(u, R, pb)

                def to_T(src_bf, R, nh, dst):
                    pt = self.ps.next().cast(BF16)
                    for h in range(nh):
                        k.tr(pt[:, h * R:(h + 1) * R], src_bf[:R, h * 128:(h + 1) * 128], self.ident[:R, :R])
                    k.cp("act", dst, pt[:, 0:nh * R].re("p (h t) -> p h t", h=nh))

                for g in range(2):
                    def h_q(u, R, pb, g=g):
                        bf = bfo_ring.next()
                        self.normed(pb[:R, :512], R, 4, 0, out_bf=bf[:R, :])
                        if u < 8:
                            qsb = bfo_ring.next()
                            to_T(bf, R, 4, qsb[:, :].re("p (h t) -> p h t", h=4))
                            k.dma(qT_scr[u, :, g * 4:(g + 1) * 4, :], qsb[:, :].re("p (h t) -> p h t", h=4), q="act")
                        else:
                            to_T(bf, R, 4, q2T[:, u - 8, g * 4:(g + 1) * 4, :])
                    tm_block(C_Q + 512 * g, 512, h_q)

                def h_kcvc(u, R, pb):
                    stg = stg_ring.next()
                    k.cp("act", stg[:R, :], pb[:R, :512])
                    if u < 8:
                        k.dma(o_kc[u], stg[:, 0:256], q="act"); k.dma(o_vc[u], stg[:, 256:512], q="act")
                    else:
                        b = u - 8
                        k.dma(o_kc2[b * 8:(b + 1) * 8, :], stg[:8, 0:256]); k.dma(o_vc2[b * 8:(b + 1) * 8, :], stg[:8, 256:512])
                tm_block(C_KC, 512, h_kcvc)

                def mk_kv(gidx, oK, oV, oK2, oV2, KT_own, V_own, K2T, V2):
                    def h(u, R, pb):
                        stg = stg_ring.next()
                        bf = bfo_ring.next()
                        self.normed(pb[:R, 0:256], R, 2, gidx, out_f32=stg[:R, 0:256], out_bf=bf[:R, 0:256])
                        k.cp("act", stg[:R, 256:512], pb[:R, 256:512])
                        if u < 8:
                            k.dma(oK[u], stg[:, 0:256], q="pool"); k.dma(oV[u], stg[:, 256:512], q="act")
                            to_T(bf, R, 2, KT_own[:, u, :, :])
                            k.cp("pool", V_own[:, u, :, 0:128], stg[:, 256:512].re("p (g d) -> p g d", g=2))
                        else:
                            b = u - 8
                            if oK2 is not None:
                                k.dma(oK2[b * 8:(b + 1) * 8, :], stg[:8, 0:256]); k.dma(oV2[b * 8:(b + 1) * 8, :], stg[:8, 256:512])
                            else:
                                self.win_new_rows(b, stg)
                            to_T(bf, R, 2, K2T[:, b, :, :])
                            k.cp("pool", V2[:8, b, :, 0:128], stg[:8, 256:512].re("p (g d) -> p g d", g=2))
                    return h
                tm_block(C_KS, 512, mk_kv(1, o_ks, o_vs, o_ks2, o_vs2, ksT_own, Vs_own, ks2T, V2s))
                def _wnr(b, stg):
                    k.dma(o_kwin[b, 504:512, :], stg[:8, 0:256])
                    k.dma(o_vwin[b, 504:512, :], stg[:8, 256:512])
                self.win_new_rows = _wnr
                tm_block(C_KW, 512, mk_kv(2, o_kw, o_vw, None, None, KwT_own, Vw_own, kw2T, V2w))

                def h_gt(u, R, pb):
                    e = self.etmp.next()
                    k.act(e[:R, :24], pb[:R, :24], AF.Exp, scale=-1.0)
                    k.ts("dve", e[:R, :24], e[:R, :24], 1.0, None, ALU.add)
                    if u < 8:
                        k.recip(gates[:, u, :], e[:, :24])
                    else:
                        k.recip(gates2[:8, u - 8, :], e[:8, :24])
                tm_block(C_GT, 24, h_gt)

                for half in range(2):
                    def h_zn(u, R, pb, half=half):
                        bf = bfo_ring.next()
                        self.silu_from("dve", bf[:R, :], pb[:R, :512], R, 512)
                        r0 = u * 128 if u < 8 else 1024 + (u - 8) * 8
                        k.dma(sz_scr[r0:r0 + R, half * 512:(half + 1) * 512], bf[:R, :], q="pool")
                    tm_block(C_ZN + 512 * half, 512, h_zn)
                wb_next = self.wload(tm_list[0][0], tm_list[0][1])
                for i_, (c0_, wd_, hd_) in enumerate(tm_list):
                    wb_cur = wb_next
                    if i_ + 1 < len(tm_list):
                        wb_next = self.wload(tm_list[i_ + 1][0], tm_list[i_ + 1][1])
                    tm_run(wb_cur, wd_, hd_)

        if self.stages >= 3:
            self.build2()
        k.flush(final=True)


    def finish_branch(self, OB, oacc, s, g, b, rd_keep=None):
        k = self.k
        gates = self.pers["gates"]
        st = self.st_ring.next()
        for r in range(4):
            h = g * 4 + r
            k.ts("dve", st[:, r:r + 1], OB[r][:, 128:129], 1e-37, None, ALU.max)
        k.recip(st[:, 4:8], st[:, 0:4])
        if rd_keep is not None:
            k.cp("dve", rd_keep, st[:, 4:8])
        for r in range(4):
            h = g * 4 + r
            k.tt("dve", st[:, 8 + r:9 + r], st[:, 4 + r:5 + r], gates[:, s, h * 3 + b:h * 3 + b + 1], ALU.mult)
            oh = oacc[:, h * 128:(h + 1) * 128]
            if b == 0:
                k.ts("dve", oh, OB[r][:, 0:128], st[:, 8 + r:9 + r], None, ALU.mult)
            else:
                k.stt("dve", oh, OB[r][:, 0:128], st[:, 8 + r:9 + r], oh, ALU.mult, ALU.add)

    def build2(self):
        k = self.k
        inp = self.inp
        P = self.pers
        dr = self.dr
        mixT, qT_scr, sz_scr, xB, xS, o_y, o_y2 = (dr[n] for n in ("mixT", "qT_scr", "sz_scr", "xB", "xS", "o_y", "o_y2"))
        ksT_own, Vs_own, KwT_own, Vw_own = P["ksT_own"], P["Vs_own"], P["KwT_own"], P["Vw_own"]
        xA = inp("xA", [T, D])
        xH = inp("xH", [8, 4, 128, D])
        w_out = inp("w_out", [D, D])
        w1 = [inp("w_cmp_k1", [32, 128, 128]), inp("w_cmp_v1", [32, 128, 128])]
        w2 = [inp("w_cmp_k2", [128, 128]), inp("w_cmp_v2", [128, 128])]
        pe = [inp("pe_cmp_k", [32, 128]), inp("pe_cmp_v", [32, 128])]
        selbias_d = inp("selbias", [128, 288 * 8])
        dbias_d = inp("dbias", [128, 8])
        winbias_d = inp("winbias", [128, 8 * 5 * 8])
        cmpbias_d = inp("cmpbias", [128, 8 * 4 * 8])
        cmpmask_d = inp("cmpmask", [128, 8 * 4 * 128], BF16)
        forced_d = inp("forced", [128, 8 * 128])
        notown_d = inp("notown", [128, 8 * 128])
        ctab_d = inp("ctab", [128, 4 * 128], BF16)
        i4_d = inp("i4", [128, 512], BF16)
        tri_d = inp("tri", [128, 2 * 128], BF16)
        kcT_scr = self.scr("kcT_scr", [4, 128, T], BF16)
        kwT_scr = self.scr("kwT_scr", [8, 128, 2, 512], BF16)
        vw_scr = self.scr("vw_scr", [8, 128, 4, 2, 130], BF16)

        with contextlib.ExitStack() as cst:
            old = k.stack
            k.stack = cst
            KsT = k.sb("KsT", [128, 2, T], BF16)
            Vs = k.sb("Vs", [128, 64, 2, 130], BF16)
            KcT = k.sb("KcT", [128, 2, 512], BF16)
            Vc = k.sb("Vc", [128, 4, 2, 130], BF16)
            k.stack = old
            with k.phase():
                self.xt_ring = k.ring("xt", 3, [128, D], F32)
                self.xs_ring = k.ring("xs", 3, [128, D], BF16)
                self.st_ring = k.ring("st", 8, [128, 16], F32)
                self.wb_ring = k.ring("wb", 3, [128, 16, 512], BF16)
                self.wstg_ring = k.ring("wstg", 1, [128, 4, 512], F32)
                self.etmp = k.ring("etmp", 2, [128, 512], F32)
                for t_ in (Vs, Vc):
                    k.memset("dve", t_.re("p a g c -> p (a g) c")[:, :, 128:129], 1.0)
                    k.memset("dve", t_.re("p a g c -> p (a g) c")[:, :, 129:130], 0.0)
                W0 = self.wload(C_KC, 512)
                W1 = self.wload(C_KS, 512)
                Ww = self.wload(C_KW, 512)
                xT_ring = k.ring("xT", 2, [128, 16, 128], BF16)
                kcsb_ring = k.ring("kcsb", 2, [128, 4, 128], BF16)
                pab_ring = k.ring("pab", 2, [128, 512], BF16)
                bf_ring = k.ring("bfa", 3, [128, 256], BF16)
                kwsb_ring = k.ring("kwsb", 2, [128, 2, 128], BF16)
                vwsb_ring = k.ring("vwsb", 2, [128, 2, 130], BF16)
                for v_ in vwsb_ring.items:
                    k.memset("dve", v_[:, :, 128:129], 1.0)
                    k.memset("dve", v_[:, :, 129:130], 0.0)
                nA = 64 if self.stages >= 4 else 2
                jobs = []

                def mkA(t):
                    box = {}

                    def sL():
                        box["xt"] = self.front_load(xA[t * 128:(t + 1) * 128, :], 128)

                    def sF():
                        box["xs"] = self.front_norm(box["xt"], 128)

                    def s0():
                        box["xT"] = xT_ring.next()
                        self.front_T(box["xs"], 128, box["xT"])

                    def s1():
                        xT = box["xT"]
                        pa = self.ps.next()
                        for kc in range(16):
                            k.mm(pa, xT[:, kc, :], W0[:, kc, :], start=(kc == 0), stop=(kc == 15))
                        pab = pab_ring.next()
                        k.cp("act", pab, pa)
                        pb = self.ps.next()
                        for kc in range(16):
                            k.mm(pb, xT[:, kc, :], W1[:, kc, :], start=(kc == 0), stop=(kc == 15))
                        pt2 = self.ps.next().cast(BF16)
                        for cb in range(4):
                            k.tr(pt2[:, cb * 128:(cb + 1) * 128], pab[:, cb * 128:(cb + 1) * 128], self.ident)
                        kcsb = kcsb_ring.next()
                        k.cp("dve", kcsb, pt2[:, 0:512].re("p (a t) -> p a t", a=4))
                        k.dma(kcT_scr[:, :, t * 128:(t + 1) * 128].re("a p t -> p a t"), kcsb, q="pool")
                        box["bf"] = bf_ring.next()
                        self.normed(pb[:, 0:256], 128, 2, 1, out_bf=box["bf"])
                        k.cp("dve", Vs[:, t, :, 0:128], pb[:, 256:512].re("p (g d) -> p g d", g=2))

                    def s2():
                        bf = box["bf"]
                        pt = self.ps.next().cast(BF16)
                        for g in range(2):
                            k.tr(pt[:, g * 128:(g + 1) * 128], bf[:, g * 128:(g + 1) * 128], self.ident)
                        k.cp("act", KsT[:, :, t * 128:(t + 1) * 128], pt[:, 0:256].re("p (g t) -> p g t", g=2))
                    return [sL, sF, s0, s1, s2]

                def mkH(s_, o):
                    box = {}

                    def sL():
                        box["xt"] = self.front_load(xH[s_, o], 128)

                    def sF():
                        box["xs"] = self.front_norm(box["xt"], 128)

                    def s0():
                        box["xT"] = xT_ring.next()
                        self.front_T(box["xs"], 128, box["xT"])

                    def s1():
                        xT = box["xT"]
                        pb = self.ps.next()
                        for kc in range(16):
                            k.mm(pb, xT[:, kc, :], Ww[:, kc, :], start=(kc == 0), stop=(kc == 15))
                        box["bf"] = bf_ring.next()
                        self.normed(pb[:, 0:256], 128, 2, 2, out_bf=box["bf"])
                        vwsb = vwsb_ring.next()
                        k.cp("dve", vwsb[:, :, 0:128], pb[:, 256:512].re("p (g d) -> p g d", g=2))
                        k.dma(vw_scr[s_, :, o, :, :], vwsb, q="pool")

                    def s2():
                        bf = box["bf"]
                        pt = self.ps.next().cast(BF16)
                        for g in range(2):
                            k.tr(pt[:, g * 128:(g + 1) * 128], bf[:, g * 128:(g + 1) * 128], self.ident)
                        kwsb = kwsb_ring.next()
                        k.cp("act", kwsb, pt[:, 0:256].re("p (g t) -> p g t", g=2))
                        k.dma(kwT_scr[s_, :, :, o * 128:(o + 1) * 128], kwsb, q="act")
                    return [sL, sF, s0, s1, s2]
                for t in range(nA):
                    jobs.append(mkA(t))
                for s_ in range(8):
                    for o in range(4):
                        jobs.append(mkH(s_, o))
                run_pipeline(jobs)

            with k.phase():
                self.st_ring = k.ring("st", 4, [128, 16], F32)
                self.etmp = k.ring("etmp", 4, [128, 512], F32)
                stgW = k.sb("stgW", [128, 32, 128], F32)
                W1b = [k.sb("W1k", [128, 32, 128], BF16), k.sb("W1v", [128, 32, 128], BF16)]
                W2b = [k.sb("W2k", [128, 128], BF16), k.sb("W2v", [128, 128], BF16)]
                peT = k.sb("peT", [128, 2, 32], BF16)
                biasv = k.sb("biasv", [128, 2], F32)
                gkc = k.sb("gkc", [128, 1], F32)
                ones_m = k.sb("ones_m", [128, 128], BF16)
                k.memset("dve", ones_m, 1.0 / 128)
                for kv in range(2):
                    k.dma(stgW, w1[kv].re("c d e -> d c e"))
                    k.cp("pool", W1b[kv], stgW)
                    stg2 = self.etmp.next()
                    k.dma(stg2[:, 0:128], w2[kv])
                    k.cp("pool", W2b[kv], stg2[:, 0:128])
                    pes = self.etmp.next()
                    k.dma(pes[0:32, 0:128], pe[kv])
                    pt = self.ps.next()
                    k.tr(pt[:, 0:32], pes[0:32, 0:128], self.identf[:32, :32])
                    k.cp("dve", peT[:, kv, :], pt[:, 0:32])
                    pbias = self.ps.next()
                    for c in range(32):
                        k.mm(pbias[:, 0:2], W1b[kv][:, c, :], peT[:, kv, c:c + 1].bc([128, 2]), start=(c == 0), stop=(c == 31))
                    k.cp("dve", biasv[:, kv:kv + 1], pbias[:, 0:1])
                pt = self.ps.next()
                k.tr(pt[:, 0:128], self.gvec[:, 3, :], self.identf)
                k.cp("dve", gkc, pt[:, 0:1])
                self.cmp_w = (W1b, W2b, biasv, gkc, ones_m)
                kc_ring = k.ring("kcl", 2, [128, T], BF16)
                fb_sb = k.sb("fb_sb", [128, 512], F32)
                hid_ring = k.ring("hid", 2, [128, 512], BF16)
                sq_ring = k.ring("sq", 2, [128, 512], BF16)
                for kv in range(2):
                    for g in range(2):
                        kcl = kc_ring.next()
                        k.dma(kcl, kcT_scr[kv * 2 + g])
                        kview = kcl.re("p (n c) -> p n c", c=16)
                        pfa = self.ps.next()
                        pfb = self.ps.next()
                        for c in range(16):
                            k.mm(pfa, W1b[kv][:, c, :], kview[:, :, c], start=(c == 0), stop=(c == 15))
                        for c in range(16):
                            k.mm(pfb, W1b[kv][:, 16 + c, :], kview[:, :, c], start=(c == 0), stop=(c == 15))
                        k.cp("act", fb_sb, pfb)
                        t_ = self.etmp.next()
                        k.tt("dve", t_[:, 0:511], pfa[:, 0:511], fb_sb[:, 1:512], ALU.add)
                        h = hid_ring.next()
                        k.memset("dve", h[:, 511:512], 0.0)
                        self.silu_from("dve", h[:, 0:511], t_[:, 0:511], 128, 511, bias=biasv[:, kv:kv + 1])
                        if kv == 0:
                            pk = self.ps.next()
                            k.mm(pk, W2b[0], h)
                            sq = sq_ring.next()
                            k.act(sq, pk, AF.Square)
                            pss = self.ps.next()
                            k.mm(pss, ones_m, sq)
                            e = self.etmp.next()
                            k.act(e, pss, AF.Ln, bias=self.epsc[:, 0:1])
                            k.act(e, e, AF.Exp, scale=-0.5)
                            k.tt("dve", e, pk, e, ALU.mult)
                            k.ts("dve", KcT[:, g, :], e, gkc[:, 0:1], None, ALU.mult)
                        else:
                            pv = self.ps.next()
                            for nt in range(4):
                                k.mm(pv[:, nt * 128:(nt + 1) * 128], h[:, nt * 128:(nt + 1) * 128], W2b[1])
                            k.cp("dve", Vc[:, :, g, 0:128], pv.re("p (a d) -> p a d", a=4))

            with k.phase():
                self.st_ring = k.ring("st", 6, [128, 16], F32)
                self.etmp = k.ring("etmp", 3, [128, 512], F32)

                def tab(name, d, shape, dt=F32, pat=None, **kw):
                    t_ = k.sb(name, shape, dt)
                    k.dma(t_, d.re(pat, **kw) if pat else d)
                    return t_
                selbias = tab("selbias", selbias_d, [128, 288, 8], pat="p (a h) -> p a h", h=8)
                dbias = tab("dbias", dbias_d, [128, 8])
                winbias = tab("winbias", winbias_d, [128, 8, 5, 8], pat="p (s o h) -> p s o h", s=8, o=5)
                cmpbias = tab("cmpbias", cmpbias_d, [128, 8, 4, 8], pat="p (s o h) -> p s o h", s=8, o=4)
                cmpmask = tab("cmpmask", cmpmask_d, [128, 8, 4, 128], BF16, pat="p (s o q) -> p s o q", s=8, o=4)
                forced = tab("forced", forced_d, [128, 8, 128], pat="p (s j) -> p s j", s=8)
                notown = tab("notown", notown_d, [128, 8, 128], pat="p (s j) -> p s j", s=8)
                ctab = tab("ctab", ctab_d, [128, 4, 128], BF16, pat="p (a j) -> p a j", a=4)
                i4 = tab("i4", i4_d, [128, 512], BF16)
                tri = tab("tri", tri_d, [128, 2, 128], BF16, pat="p (a q) -> p a q", a=2)
                QT_ring = k.ring("QT", 2, [128, 8, 128], BF16)
                sz_ring = k.ring("sz", 2, [128, 1024], BF16)
                KwH_ring = k.ring("KwH", 2, [128, 2, 512], BF16)
                VwH_ring = k.ring("VwH", 2, [128, 4, 2, 130], BF16)
                ET_items = []
                for i_ in range(4):
                    base = k.sb("ET%d" % i_, [128, 512], BF16)
                    toks = [Tok() for _ in range(4)]
                    ET_items.append((V(base.ap, toks), [V(base.ap[:, r * 128:(r + 1) * 128], [toks[r]]) for r in range(4)]))
                ET_ring = Ring(ET_items)
                ETc_ring = k.ring("ETc", 2, [128, 4, 512], BF16)
                stsbC_ring = k.ring("stsbC", 4, [128, 512], F32)
                oacc_ring = k.ring("oacc", 2, [128, 1024], F32)
                mexp_ring = k.ring("mexp", 2, [128, 1024], BF16)
                sc_ring = k.ring("sc", 4, [128, 128], F32)
                m8_ring = k.ring("m8", 2, [128, 16], F32)
                rdc_ring = k.ring("rdc", 2, [128, 4], F32)
                mix_ring = k.ring("mix", 2, [128, 1024], BF16)
                mixa_ring = k.ring("mixa", 2, [128, 8, 128], BF16)
                OB = self.ps.items[0:4]
                STr = Ring(self.ps.items[4:6])
                MS = Ring(self.ps.items[6:8])
                offs = [sum(8 * s2 + 8 for s2 in range(s)) for s in range(8)]
                nS = 8 if self.stages >= 4 else 1
                for s in range(nS):
                    QT = QT_ring.next()
                    k.dma(QT, qT_scr[s])
                    sz = sz_ring.next()
                    k.dma(sz, sz_scr[s * 128:(s + 1) * 128, :])
                    KwH = KwH_ring.next()
                    k.dma(KwH, kwT_scr[s])
                    VwH = VwH_ring.next()
                    k.dma(VwH, vw_scr[s])
                    oacc = oacc_ring.next()
                    for g in range(2):
                        Qg = QT[:, g * 4:(g + 1) * 4, :].re("p h t -> p (h t)")
                        ETc0 = ETc_ring.next()
                        ctoks = [Tok() for _ in range(4)]
                        ETc = V(ETc0.ap, ctoks + ETc0.toks)
                        jobs = []

                        def mk_c(nt, g=g, Qg=Qg, ETc0=ETc0, ctoks=ctoks):
                            box = {}

                            def s0():
                                box["st"] = STr.next()
                                k.mm(box["st"], KcT[:, g, nt * 128:(nt + 1) * 128], Qg)

                            def s1():
                                ev = V(ETc0.ap[:, nt, :], [ctoks[nt]])
                                sb_ = stsbC_ring.next()
                                k.tt("dve", sb_.re("p (r q) -> p r q", r=4), box["st"].re("p (r q) -> p r q", r=4),
                                     cmpbias[:, s, nt, g * 4:(g + 1) * 4].us(2).bc([128, 4, 128]), ALU.add)
                                k.act(ev, sb_, AF.Exp)
                                e3 = ev.re("p (r q) -> p r q", r=4)
                                k.tt("pool", e3, e3, cmpmask[:, s, nt, :].us(1).bc([128, 4, 128]), ALU.mult)
                            return [s0, s1]
                        for nt in range(4):
                            jobs.append(mk_c(nt))
                        run_pipeline(jobs)
                        for r in range(4):
                            for nt in range(4):
                                k.mm(OB[r][:, 0:129], ETc[:, nt, r * 128:(r + 1) * 128], Vc[:, nt, g, 0:129],
                                     start=(nt == 0), stop=(nt == 3))
                        imp = MS.next()
                        for r in range(4):
                            for nt in range(4):
                                k.mm(imp[:, r * 128:(r + 1) * 128], ETc[:, nt, r * 128:(r + 1) * 128], ctab[:, nt, :],
                                     start=(nt == 0), stop=(nt == 3))
                        rdc = rdc_ring.next()
                        self.finish_branch(OB, oacc, s, g, 0, rd_keep=rdc)
                        sc = sc_ring.next()
                        sc2 = sc_ring.next()
                        m8 = m8_ring.next()
                        k.ts("dve", sc, imp[:, 0:128], rdc[:, 0:1], None, ALU.mult)
                        for r in range(1, 4):
                            k.stt("dve", sc, imp[:, r * 128:(r + 1) * 128], rdc[:, r:r + 1], sc, ALU.mult, ALU.add)
                        k.tt("dve", sc, sc, forced[:, s, :], ALU.add)
                        k.max8(m8[:, 0:8], sc)
                        k.mrep(sc2, m8[:, 0:8], sc, -3.0e38)
                        k.max8(m8[:, 8:16], sc2)
                        k.ts("dve", sc2, sc, m8[:, 15:16], None, ALU.is_ge)
                        k.tt("dve", sc2, sc2, notown[:, s, :], ALU.mult)
                        k.ts("dve", sc2, sc2, 30000.0, -30000.0, ALU.mult, ALU.add)
                        nT = 8 * s + 8
                        cur = {}

                        def mk_tile(lhsK, t_mask, biasfn, trimask, Vt, start, stop, g=g, Qg=Qg, cur=cur):
                            box = {}

                            def s0():
                                if t_mask is not None and t_mask % 8 == 0:
                                    cur["mexp"] = mexp_ring.next()
                                    c8 = t_mask // 8
                                    k.cp("pool", cur["mexp"].re("p (j e) -> p j e", e=64),
                                         sc2[:, c8 * 16:(c8 + 1) * 16].us(2).bc([128, 16, 64]))
                                st_ = STr.next()
                                box["st"] = st_
                                k.mm(st_, lhsK, Qg, start=True, stop=(t_mask is None))
                                if t_mask is not None:
                                    k.mm(st_, cur["mexp"][:, (t_mask % 8) * 128:(t_mask % 8 + 1) * 128], i4, start=False, stop=True)

                            def s1a():
                                box["sb"] = stsbC_ring.next()
                                k.tt("dve", box["sb"].re("p (r q) -> p r q", r=4), box["st"].re("p (r q) -> p r q", r=4),
                                     biasfn(g).us(2).bc([128, 4, 128]), ALU.add)

                            def s1():
                                whole, parts = ET_ring.next()
                                box["ET"] = parts
                                k.act(whole, box["sb"], AF.Exp)
                                if trimask is not None:
                                    e3 = whole.re("p (r q) -> p r q", r=4)
                                    k.tt("pool", e3, e3, tri[:, trimask, :].us(1).bc([128, 4, 128]), ALU.mult)

                            def s2():
                                for r in range(4):
                                    k.mm(OB[r][:, 0:129], box["ET"][r], Vt, start=start, stop=stop)
                            return [s0, s1a, s1, s2]

                        jobs = []
                        for t in range(nT):
                            jobs.append(mk_tile(KsT[:, g, t * 128:(t + 1) * 128], t,
                                                (lambda g_, t=t: selbias[:, offs[s] + t, g_ * 4:(g_ + 1) * 4]), None,
                                                Vs[:, t, g, 0:129], (t == 0), False))
                        jobs.append(mk_tile(ksT_own[:, s, g, :], None, (lambda g_: dbias[:, g_ * 4:(g_ + 1) * 4]), 0,
                                            Vs_own[:, s, g, 0:129], False, True))
                        run_pipeline(jobs)
                        self.finish_branch(OB, oacc, s, g, 1)
                        jobs = []
                        for o in range(5):
                            lhs = KwH[:, g, o * 128:(o + 1) * 128] if o < 4 else KwT_own[:, s, g, :]
                            Vt = VwH[:, o, g, 0:129] if o < 4 else Vw_own[:, s, g, 0:129]
                            jobs.append(mk_tile(lhs, None, (lambda g_, o=o: winbias[:, s, o, g_ * 4:(g_ + 1) * 4]),
                                                (1 if o == 0 else (0 if o == 4 else None)), Vt, (o == 0), (o == 4)))
                        run_pipeline(jobs)
                        self.finish_branch(OB, oacc, s, g, 2)
                    mix = mix_ring.next()
                    k.tt("dve", mix, oacc, sz, ALU.mult)
                    mixa = mixa_ring.next()
                    for half in range(2):
                        pt = MS.next().cast(BF16)
                        for e4 in range(4):
                            e_ = half * 4 + e4
                            k.tr(pt[:, e4 * 128:(e4 + 1) * 128], mix[:, e_ * 128:(e_ + 1) * 128], self.ident)
                        k.cp("act", mixa[:, half * 4:(half + 1) * 4, :], pt[:, 0:512].re("p (a t) -> p a t", a=4))
                    k.dma(mixT[1024:2048, s * 128:(s + 1) * 128].re("(a p) t -> p a t", p=128), mixa, q="act")

        if self.stages >= 6:
            self.build3()

        with k.phase():
            self.wb_ring = k.ring("wb", 2, [128, 16, 512], BF16)
            self.wstg_ring = k.ring("wstg", 2, [128, 4, 512], F32)
            mixsb = k.sb("mixsb", [128, 16, NOWN], BF16)
            for e in range(16):
                k.dma(mixsb[:, e, :], mixT[e * 128:(e + 1) * 128, :])
            xr_ring = k.ring("xr", 3, [128, 512], F32)
            yo_ring = k.ring("yo", 3, [128, 512], F32)
            Wo_next = self.wload(0, 512, scale_g=False, src=w_out)
            for ct in range(4):
                Wo = Wo_next
                if ct + 1 < 4:
                    Wo_next = self.wload((ct + 1) * 512, 512, scale_g=False, src=w_out)
                for u in range(9):
                    R = 128 if u < 8 else 32
                    col = u * 128
                    pb = self.ps.next()
                    for e in range(16):
                        k.mm(pb[:R, :], mixsb[:, e, col:col + R], Wo[:, e, :], start=(e == 0), stop=(e == 15))
                    xr = xr_ring.next()
                    src = xB[u, 32:160, ct * 512:(ct + 1) * 512] if u < 8 else xS[:, ct * 512:(ct + 1) * 512]
                    k.dma(xr[:R], src)
                    yo = yo_ring.next()
                    k.tt("dve", yo[:R], pb[:R, :], xr[:R], ALU.add)
                    dst = o_y[u][:, ct * 512:(ct + 1) * 512] if u < 8 else o_y2[:, ct * 512:(ct + 1) * 512]
                    k.dma(dst, yo[:R], q="act")


    def build3(self):
        k = self.k
        inp = self.inp
        P = self.pers
        dr = self.dr
        mixT, sz_scr = dr["mixT"], dr["sz_scr"]
        q2T, ks2T, kw2T, V2s, V2w, gates2 = (P[n] for n in ("q2T", "ks2T", "kw2T", "V2s", "V2w", "gates2"))
        pt_d = inp("ptab", [4, 128], I32)
        caches = [inp(n, [NPHYS * 8, 4096]) for n in ("cache_k_cmp", "cache_v_cmp", "cache_k_slc", "cache_v_slc")]
        skw_d = inp("skw", [4, 512, 256])
        svw_d = inp("svw", [4, 512, 256])
        o_kwin, o_vwin = dr["o_kwin"], dr["o_vwin"]
        w1 = [self.din["w_cmp_k1"], self.din["w_cmp_v1"]]
        w2 = [self.din["w_cmp_k2"], self.din["w_cmp_v2"]]
        pe = [self.din["pe_cmp_k"], self.din["pe_cmp_v"]]
        marr_d = inp("marr", [128, 8])
        c2bias_d = inp("c2bias", [128, 64])
        c2tab_d = inp("c2tab", [128, 8 * 256], BF16)
        forced2_d = inp("forced2", [8, 256])
        abias_d = inp("abias", [128, 128 * 8])
        swbias_d = inp("swbias", [128, 32])
        swmask_d = inp("swmask", [128, 32], BF16)
        tbias_d = inp("tbias", [8, 8])
        tmask_d = inp("tmask", [8, 8], BF16)
        i48_d = inp("i48", [8, 32], BF16)
        selr_d = inp("selr", [32, 32])
        selsum_d = inp("selsum", [32, 8])

        with k.phase():
            self.st_ring = k.ring("st", 6, [128, 16], F32)
            self.etmp = k.ring("etmp", 4, [128, 512], F32)

            def tab(name, d, shape, dt=F32, pat=None, **kw):
                t_ = k.sb(name, shape, dt)
                k.dma(t_, d.re(pat, **kw) if pat else d)
                return t_
            marr = tab("marr", marr_d, [128, 8])
            c2bias = tab("c2bias", c2bias_d, [128, 8, 8], pat="p (m h) -> p m h", m=8)
            c2tab = tab("c2tab", c2tab_d, [128, 8, 256], BF16, pat="p (m j) -> p m j", m=8)
            forced2 = tab("forced2", forced2_d, [8, 256])
            abias = tab("abias", abias_d, [128, 128, 8], pat="p (r h) -> p r h", h=8)
            swbias = tab("swbias", swbias_d, [128, 4, 8], pat="p (w h) -> p w h", h=8)
            swmask = tab("swmask", swmask_d, [128, 4, 8], BF16, pat="p (w i) -> p w i", i=8)
            tbias = tab("tbias", tbias_d, [8, 8])
            tmask = tab("tmask", tmask_d, [8, 8], BF16)
            i48 = tab("i48", i48_d, [8, 32], BF16)
            selr = tab("selr", selr_d, [32, 4, 8], pat="p (r i) -> p r i", r=4)
            selsum = tab("selsum", selsum_d, [32, 8])
            X_ring = k.ring("X", 3, [128, 16, 2, 128], F32)
            W1b = [k.sb("W1k", [128, 32, 128], BF16), k.sb("W1v", [128, 32, 128], BF16)]
            W2b = [k.sb("W2k", [128, 128], BF16), k.sb("W2v", [128, 128], BF16)]
            peT = k.sb("peT", [128, 2, 32], BF16)
            biasv = k.sb("biasv", [128, 2], F32)
            gkc = k.sb("gkc", [128, 1], F32)
            ones_m = k.sb("ones_m", [128, 128], BF16)
            k.memset("dve", ones_m, 1.0 / 128)
            for kv in range(2):
                stgW = X_ring.next().re("p a g d -> p (a g) d")
                k.dma(stgW, w1[kv].re("c d e -> d c e"))
                k.cp("pool", W1b[kv], stgW)
                stg2 = self.etmp.next()
                k.dma(stg2[:, 0:128], w2[kv])
                k.cp("pool", W2b[kv], stg2[:, 0:128])
                pes = self.etmp.next()
                k.dma(pes[0:32, 0:128], pe[kv])
                pt = self.ps.next()
                k.tr(pt[:, 0:32], pes[0:32, 0:128], self.identf[:32, :32])
                k.cp("dve", peT[:, kv, :], pt[:, 0:32])
                pbias = self.ps.next()
                for c in range(32):
                    k.mm(pbias[:, 0:2], W1b[kv][:, c, :], peT[:, kv, c:c + 1].bc([128, 2]), start=(c == 0), stop=(c == 31))
                k.cp("dve", biasv[:, kv:kv + 1], pbias[:, 0:1])
            pt = self.ps.next()
            k.tr(pt[:, 0:128], self.gvec[:, 3, :], self.identf)
            k.cp("dve", gkc, pt[:, 0:1])
            pti = k.sb("pti", [128, 4], I32)
            for bl in range(4):
                k.dma(pti[:, bl:bl + 1], pt_d[bl].re("(p o) -> p o", o=1))
            ptf = k.sb("ptf", [128, 4], F32)
            k.cp("dve", ptf, pti)
            k.ts("dve", ptf, ptf, 8.0, None, ALU.mult)
            idxf = k.sb("idxf", [128, 4, 8], F32)
            idx8 = k.sb("idx8", [128, 4, 8], I32)
            for bl in range(4):
                k.ts("dve", idxf[:, bl, :], marr, ptf[:, bl:bl + 1], None, ALU.add)
            k.cp("dve", idx8, idxf)

            XT_ring = k.ring("XT", 2, [128, 16, 2, 128], BF16)
            Vb_ring = k.ring("Vb", 2, [128, 16, 2, 130], BF16)
            for v_ in Vb_ring.items:
                k.memset("dve", v_.re("p a g c -> p (a g) c")[:, :, 128:129], 1.0)
                k.memset("dve", v_.re("p a g c -> p (a g) c")[:, :, 129:130], 0.0)
            Fsb = k.sb("Fsb", [128, 2, 2, 8, 128], F32)
            hid_ring = k.ring("hid", 2, [128, 1024], BF16)
            sq_ring = k.ring("sq", 2, [128, 512], BF16)
            Kc2T = k.sb("Kc2T", [128, 2, 1024], BF16)
            Vc2 = k.sb("Vc2", [128, 8, 2, 130], BF16)
            k.memset("dve", Vc2.re("p a g c -> p (a g) c")[:, :, 128:129], 1.0)
            k.memset("dve", Vc2.re("p a g c -> p (a g) c")[:, :, 129:130], 0.0)
            stsb_ring = k.ring("stsb", 2, [128, 512], F32)
            ET_ring = k.ring("ET2", 2, [128, 512], BF16)
            o2_ring = k.ring("o2acc", 1, [8, 1024], F32)
            on_ring = k.ring("onorm", 2, [32, 128], F32)
            impn = k.sb("impn", [32, 256], F32)
            sc_ring = k.ring("sc2", 3, [8, 264], F32)
            m8 = k.sb("m8b", [8, 16], F32)
            Mb2h = k.sb("Mb2h", [8, 2, 2, 128], BF16)
            skw = k.sb("skw", [128, 4, 2, 128], F32)
            svw = k.sb("svw", [128, 4, 2, 128], F32)
            Vwb = k.sb("Vwb", [128, 4, 2, 130], BF16)
            k.memset("dve", Vwb.re("p a g c -> p (a g) c")[:, :, 128:129], 1.0)
            k.memset("dve", Vwb.re("p a g c -> p (a g) c")[:, :, 129:130], 0.0)
            KwT2 = k.sb("KwT2", [128, 2, 4, 128], BF16)
            sz2_ring = k.ring("sz2", 1, [8, 1024], BF16)
            mix2_ring = k.ring("mix2", 1, [8, 1024], BF16)
            mix2a_ring = k.ring("mix2a", 1, [128, 8, 8], BF16)
            OC = self.ps.items[0]
            OG = self.ps.items[1:3]
            STr = Ring(self.ps.items[3:5])
            MS = Ring(self.ps.items[5:8])

            def q2g(bl, g):
                return q2T[:, bl, g * 4:(g + 1) * 4, :].re("p h i -> p (h i)")

            def finish2(Og, o2acc, bl, g, b, keep=None):
                st = self.st_ring.next()
                k.ts("dve", st[:32, 0:1], Og[:32, 128:129], 1e-37, None, ALU.max)
                k.recip(st[:32, 1:2], st[:32, 0:1])
                if keep is not None:
                    k.cp("dve", keep, st[:32, 1:2])
                on = on_ring.next()
                k.ts("dve", on, Og[:32, 0:128], st[:32, 1:2], None, ALU.mult)
                pr = MS.next()
                for r in range(4):
                    k.mm(pr[:8, r * 128:(r + 1) * 128], selr[:, r, :], on)
                for r in range(4):
                    h = g * 4 + r
                    oh = o2acc[:, h * 128:(h + 1) * 128]
                    gt = gates2[:8, bl, h * 3 + b:h * 3 + b + 1]
                    if b == 0:
                        k.ts("dve", oh, pr[:8, r * 128:(r + 1) * 128], gt, None, ALU.mult)
                    else:
                        k.stt("dve", oh, pr[:8, r * 128:(r + 1) * 128], gt, oh, ALU.mult, ALU.add)

            def transposes_to(Xv, dstT, evac_i, gs=(0, 1)):
                n = 0
                for g in gs:
                    for r4 in range(4):
                        pb = MS.next()
                        for a in range(4):
                            k.tr(pb[:, a * 128:(a + 1) * 128], Xv[:, r4 * 4 + a, g, :], self.identf)
                        k.cp("act" if (n + evac_i) % 2 == 0 else "dve", dstT[:, r4 * 4:(r4 + 1) * 4, g, :],
                             pb.re("p (a q) -> p a q", a=4))
                        n += 1

            for bl in range(4):
                def post_cmp(kv):
                    for g in range(2):
                        h = hid_ring.next()
                        for half in range(2):
                            t_ = self.etmp.next()
                            if half == 0:
                                k.tt("dve", t_.re("p (m q) -> p m q", m=4), Fsb[:, 0, g, 0:4, :], Fsb[:, 1, g, 1:5, :], ALU.add)
                            else:
                                k.tt("dve", t_[:, 0:384].re("p (m q) -> p m q", m=3), Fsb[:, 0, g, 4:7, :], Fsb[:, 1, g, 5:8, :], ALU.add)
                                k.tt("dve", t_[:, 384:511], Fsb[:, 0, g, 7, 0:127], Fsb[:, 1, g, 0, 1:128], ALU.add)
                                k.memset("dve", t_[:, 511:512], 0.0)
                            self.silu_from("dve", h[:, half * 512:(half + 1) * 512], t_, 128, 512, bias=biasv[:, kv:kv + 1])
                        k.memset("dve", h[:, 1023:1024], 0.0)
                        if kv == 0:
                            for half in range(2):
                                hs = h[:, half * 512:(half + 1) * 512]
                                pk = MS.next()
                                k.mm(pk, W2b[0], hs)
                                sq = sq_ring.next()
                                k.act(sq, pk, AF.Square)
                                pss = MS.next()
                                k.mm(pss, ones_m, sq)
                                e = self.etmp.next()
                                k.act(e, pss, AF.Ln, bias=self.epsc[:, 0:1])
                                k.act(e, e, AF.Exp, scale=-0.5)
                                k.tt("dve", e, pk, e, ALU.mult)
                                k.ts("dve", Kc2T[:, g, half * 512:(half + 1) * 512], e, gkc[:, 0:1], None, ALU.mult)
                        else:
                            for half in range(2):
                                pv = MS.next()
                                for a in range(4):
                                    m = half * 4 + a
                                    k.mm(pv[:, a * 128:(a + 1) * 128], h[:, m * 128:(m + 1) * 128], W2b[1])
                                k.cp("dve", Vc2[:, half * 4:(half + 1) * 4, g, 0:128], pv.re("p (a d) -> p a d", a=4))

                def mk_cmp(kv, m, bl=bl):
                    box = {}

                    def sG():
                        box["X"] = X_ring.next()
                        k.gather(box["X"].re("p a g d -> p (a g d)"), caches[kv], idx8[:, bl, m:m + 1])

                    def s0():
                        box["XT"] = XT_ring.next()
                        transposes_to(box["X"], box["XT"], m)

                    def s1():
                        XT = box["XT"]
                        pf = STr.next()
                        for ab in range(2):
                            for g in range(2):
                                for c in range(16):
                                    k.mm(pf[:, (ab * 2 + g) * 128:(ab * 2 + g + 1) * 128], W1b[kv][:, ab * 16 + c, :],
                                         XT[:, c, g, :], start=(c == 0), stop=(c == 15))
                        k.cp("dve" if m % 2 == 0 else "act", Fsb[:, :, :, m, :], pf.re("p (a g q) -> p a g q", a=2, g=2))
                        if m == 7:
                            post_cmp(kv)
                    return [sG, s0, s1]
                run_pipeline([mk_cmp(kv, m) for kv in range(2) for m in range(8)])
                o2acc = o2_ring.next()
                k.dma(skw.re("p w g d -> p w (g d)"), skw_d[bl].re("(w p) c -> p w c", p=128))
                k.dma(svw.re("p w g d -> p w (g d)"), svw_d[bl].re("(w p) c -> p w c", p=128))
                for (src, dst) in ((skw, o_kwin), (svw, o_vwin)):
                    k.dma(dst[bl, 0:120, :], src[8:128, 0, :, :].re("p g d -> p (g d)"))
                    k.dma(dst[bl, 120:504, :].re("(w p) c -> p w c", p=128), src[:, 1:4, :, :].re("p w g d -> p w (g d)"))
                for g in range(2):
                    pb = MS.next()
                    for w in range(4):
                        k.tr(pb[:, w * 128:(w + 1) * 128], skw[:, w, g, :], self.identf)
                    k.cp("act", KwT2[:, g, :, :], pb.re("p (w q) -> p w q", w=4))
                k.cp("pool", Vwb[:, :, :, 0:128], svw)
                for g in range(2):
                    qg = q2g(bl, g)
                    st_ = STr.next()
                    for m in range(8):
                        k.mm(st_[:, m * 32:(m + 1) * 32], Kc2T[:, g, m * 128:(m + 1) * 128], qg)
                    sb_ = stsb_ring.next()
                    k.tt("dve", sb_[:, 0:256].re("p (m r i) -> p m r i", m=8, r=4), st_[:, 0:256].re("p (m r i) -> p m r i", m=8, r=4),
                         c2bias[:, :, g * 4:(g + 1) * 4].us(3).bc([128, 8, 4, 8]), ALU.add)
                    ETc = ET_ring.next()
                    k.act(ETc[:, 0:256], sb_[:, 0:256], AF.Exp)
                    for m in range(8):
                        k.mm(OC[:32, 0:129], ETc[:, m * 32:(m + 1) * 32], Vc2[:, m, g, 0:129], start=(m == 0), stop=(m == 7))
                    pim = MS.next()
                    for m in range(8):
                        k.mm(pim[:32, 0:256], ETc[:, m * 32:(m + 1) * 32], c2tab[:, m, :], start=(m == 0), stop=(m == 7))
                    rdk = self.st_ring.next()
                    finish2(OC, o2acc, bl, g, 0, keep=rdk[:32, 15:16])
                    k.ts("dve", impn, pim[:32, 0:256], rdk[:32, 15:16], None, ALU.mult)
                    pi2 = MS.next()
                    k.mm(pi2[:8, 0:256], selsum, impn)
                    sc = sc_ring.next()
                    sc2 = sc_ring.next()
                    k.tt("dve", sc[:, 0:256], pi2[:8, 0:256], forced2, ALU.add)
                    k.max8(m8[:, 0:8], sc[:, 0:256])
                    k.mrep(sc2[:, 0:256], m8[:, 0:8], sc[:, 0:256], -3.0e38)
                    k.max8(m8[:, 8:16], sc2[:, 0:256])
                    k.ts("dve", sc2[:, 0:256], sc[:, 0:256], m8[:, 14:15], None, ALU.is_ge)
                    k.ts("dve", sc2[:, 0:256], sc2[:, 0:256], 30000.0, -30000.0, ALU.mult, ALU.add)
                    k.cp("dve", Mb2h[:, g, :, :], sc2[:, 0:256].re("i (p h) -> i h p", h=2))
                shared = {}

                def mk_sel(m, g, bl=bl, shared=shared):
                    box = {}
                    qg = q2g(bl, g)
                    hb = m // 4

                    def sG():
                        if g == 0:
                            Xk = X_ring.next()
                            k.gather(Xk.re("p a g d -> p (a g d)"), caches[2], idx8[:, bl, m:m + 1])
                            Xv = X_ring.next()
                            k.gather(Xv.re("p a g d -> p (a g d)"), caches[3], idx8[:, bl, m:m + 1])
                            shared[m] = [Xk, Xv, None, None]

                    def s0():
                        if g == 0:
                            Xk, Xv = shared[m][0], shared[m][1]
                            Vb = Vb_ring.next()
                            k.cp("act", Vb[:, :, :, 0:128], Xv)
                            KT = XT_ring.next()
                            transposes_to(Xk, KT, m)
                            shared[m][2] = Vb
                            shared[m][3] = KT
                        KT = shared[m][3]
                        st_ = STr.next()
                        box["st"] = st_
                        for row in range(16):
                            k.mm(st_[:, row * 32:(row + 1) * 32], KT[:, row, g, :], qg, start=True, stop=False)
                            k.mm(st_[:, row * 32:(row + 1) * 32], Mb2h[:, g, hb, :], i48, start=False, stop=True)

                    def s1():
                        sb_ = stsb_ring.next()
                        k.tt("dve", sb_.re("p (a r i) -> p a r i", a=16, r=4), box["st"].re("p (a r i) -> p a r i", a=16, r=4),
                             abias[:, m * 16:(m + 1) * 16, g * 4:(g + 1) * 4].us(3).bc([128, 16, 4, 8]), ALU.add)
                        box["ET"] = ET_ring.next()
                        k.act(box["ET"], sb_, AF.Exp)

                    def s2():
                        Vb = shared[m][2]
                        for row in range(16):
                            k.mm(OG[g][:32, 0:129], box["ET"][:, row * 32:(row + 1) * 32], Vb[:, row, g, 0:129],
                                 start=(m == 0 and row == 0), stop=False)
                    return [sG, s0, s1, s2]
                run_pipeline([mk_sel(m, g) for m in range(8) for g in range(2)])
                for g in range(2):
                    qg = q2g(bl, g)
                    st_ = STr.next()
                    k.mm(st_[:8, 0:32], ks2T[:, bl, g, :], qg)
                    sb_ = stsb_ring.next()
                    k.tt("dve", sb_[:8, 0:32].re("p (r i) -> p r i", r=4), st_[:8, 0:32].re("p (r i) -> p r i", r=4),
                         tbias[:, g * 4:(g + 1) * 4].us(2).bc([8, 4, 8]), ALU.add)
                    ET = ET_ring.next()
                    k.act(ET[:8, 0:32], sb_[:8, 0:32], AF.Exp)
                    k.tt("dve", ET[:8, 0:32].re("p (r i) -> p r i", r=4), ET[:8, 0:32].re("p (r i) -> p r i", r=4),
                         tmask.us(1).bc([8, 4, 8]), ALU.mult)
                    k.mm(OG[g][:32, 0:129], ET[:8, 0:32], V2s[:8, bl, g, 0:129], start=False, stop=True)
                    finish2(OG[g], o2acc, bl, g, 1)
                    st_ = STr.next()
                    for w in range(4):
                        k.mm(st_[:, w * 32:(w + 1) * 32], KwT2[:, g, w, :], qg)
                    sb_ = stsb_ring.next()
                    k.tt("dve", sb_[:, 0:128].re("p (w r i) -> p w r i", w=4, r=4), st_[:, 0:128].re("p (w r i) -> p w r i", w=4, r=4),
                         swbias[:, :, g * 4:(g + 1) * 4].us(3).bc([128, 4, 4, 8]), ALU.add)
                    ET = ET_ring.next()
                    k.act(ET[:, 0:128], sb_[:, 0:128], AF.Exp)
                    k.tt("dve", ET[:, 0:128].re("p (w r i) -> p w r i", w=4, r=4), ET[:, 0:128].re("p (w r i) -> p w r i", w=4, r=4),
                         swmask.us(2).bc([128, 4, 4, 8]), ALU.mult)
                    for w in range(4):
                        k.mm(OC[:32, 0:129], ET[:, w * 32:(w + 1) * 32], Vwb[:, w, g, 0:129], start=(w == 0), stop=False)
                    st_ = STr.next()
                    k.mm(st_[:8, 0:32], kw2T[:, bl, g, :], qg)
                    sb_ = stsb_ring.next()
                    k.tt("dve", sb_[:8, 0:32].re("p (r i) -> p r i", r=4), st_[:8, 0:32].re("p (r i) -> p r i", r=4),
                         tbias[:, g * 4:(g + 1) * 4].us(2).bc([8, 4, 8]), ALU.add)
                    ET = ET_ring.next()
                    k.act(ET[:8, 0:32], sb_[:8, 0:32], AF.Exp)
                    k.tt("dve", ET[:8, 0:32].re("p (r i) -> p r i", r=4), ET[:8, 0:32].re("p (r i) -> p r i", r=4),
                         tmask.us(1).bc([8, 4, 8]), ALU.mult)
                    k.mm(OC[:32, 0:129], ET[:8, 0:32], V2w[:8, bl, g, 0:129], start=False, stop=True)
                    finish2(OC, o2acc, bl, g, 2)
                sz2 = sz2_ring.next()
                k.dma(sz2, sz_scr[1024 + bl * 8:1024 + bl * 8 + 8, :])
                mix2 = mix2_ring.next()
                k.tt("dve", mix2, o2acc, sz2, ALU.mult)
                pt = MS.next().cast(BF16)
                for e_ in range(8):
                    k.tr(pt[:, e_ * 8:(e_ + 1) * 8], mix2[:8, e_ * 128:(e_ + 1) * 128], self.ident[:8, :8])
                mix2a = mix2a_ring.next()
                k.cp("act", mix2a, pt[:, 0:64].re("p (a t) -> p a t", a=8))
                k.dma(mixT[1024:2048, 1024 + bl * 8:1024 + bl * 8 + 8].re("(a p) t -> p a t", p=128), mix2a)


def static_tables():
    f32 = np.float32
    sl = np.array(SLOPES, f32)
    kk = np.arange(128)
    dbias = (sl[None, :] * (kk[:, None] - 127)).astype(f32)
    n = np.arange(512)
    j = np.arange(128)
    o = n[:, None] - 4 * j[None, :]
    C = np.where((o == -1) | (o == 3), 0.5, np.where((o >= 0) & (o <= 2), 1.0, 0.0))
    C[511:, :] = 0.0
    ctab = C.reshape(4, 128, 128).transpose(1, 0, 2).reshape(128, 512)
    i4 = np.tile(np.eye(128, dtype=f32), (1, 4))
    tri = np.stack([(kk[:, None] <= kk[None, :]), (kk[:, None] > kk[None, :])], axis=1).astype(f32).reshape(128, 256)
    return {"dbias": dbias, "ctab": ctab.astype(NPBF), "i4": i4.astype(NPBF), "tri": tri.astype(NPBF)}


def core_tables(c):
    f32 = np.float32
    sl = np.array(SLOPES, f32)
    kk = np.arange(128)
    selb = np.zeros((128, 288, 8), f32)
    winb = np.zeros((128, 8, 5, 8), f32)
    cmpb = np.zeros((128, 8, 4, 8), f32)
    cmpm = np.zeros((128, 8, 4, 128), f32)
    forced = np.zeros((128, 8, 128), f32)
    notown = np.zeros((128, 8, 128), f32)
    off = 0
    q = np.arange(128)
    j = np.arange(128)
    for s in range(8):
        qb = c + 8 * s
        for t in range(8 * s + 8):
            if t < qb:
                selb[:, off + t, :] = sl[None, :] * (kk[:, None] - 127 - 128 * (qb - t))
        off += 8 * s + 8
        for o in range(5):
            if o == 4 or qb - 4 + o >= 0:
                winb[:, s, o, :] = sl[None, :] * (kk[:, None] - 127 - 128 * (4 - o))
            else:
                winb[:, s, o, :] = NEG
        for nt in range(4):
            n = nt * 128 + kk
            end = 16 * n + 31
            d = end - (128 * qb + 127)
            v = sl[None, :] * d[:, None]
            bad = (n >= 511) | (d > 0)
            v[bad, :] = NEG
            cmpb[:, s, nt, :] = v
            cmpm[:, s, nt, :] = ((end[:, None] <= 128 * qb + q[None, :]) & (n[:, None] < 511)).astype(f32)
        cur = (128 * qb + q) // 64
        fm = (j[None, :] == 0) | (j[None, :] == cur[:, None]) | (j[None, :] == cur[:, None] - 1)
        inv = j[None, :] > cur[:, None]
        forced[:, s, :] = np.where(fm, 1e30, np.where(inv, -1e30, 0.0))
        notown[:, s, :] = (j[None, :] < 2 * qb).astype(f32)
    return {"selbias": selb.reshape(128, -1), "winbias": winb.reshape(128, -1), "cmpbias": cmpb.reshape(128, -1),
            "cmpmask": cmpm.reshape(128, -1).astype(NPBF), "forced": forced.reshape(128, -1),
            "notown": notown.reshape(128, -1)}


def sample_tables():
    f32 = np.float32
    sl = np.array(SLOPES, f32)
    p = np.arange(128)
    ref = PAST + 7
    marr = np.tile(np.arange(8, dtype=f32)[None, :], (128, 1))
    m = np.arange(8)
    n = 8 * p[:, None] + m[None, :]
    end = 16 * n + 31
    c2b = (sl[None, None, :] * (end[:, :, None] - ref)).astype(f32)
    c2b[n >= 1023] = NEG
    j = np.arange(256)
    o = n[:, :, None] - 4 * j[None, None, :]
    C = np.where((o == -1) | (o == 3), 0.5, np.where((o >= 0) & (o <= 2), 1.0, 0.0))
    C[n >= 1023] = 0.0
    forced2 = np.zeros((8, 256), f32)
    forced2[:, 0] = 1e30
    forced2[:, 255] = 1e30
    row = np.arange(128)
    ab = (sl[None, None, :] * ((128 * p[:, None] + row[None, :])[:, :, None] - ref)).astype(f32)
    w = np.arange(4)
    wpos = w[None, :] * 128 + p[:, None]
    swb = (sl[None, None, :] * (wpos[:, :, None] - 519)).astype(f32)
    i = np.arange(8)
    swm = (wpos[:, :, None] > i[None, None, :]).astype(f32)
    tb = (sl[None, :] * (i[:, None] - 7)).astype(f32)
    tm = (i[:, None] <= i[None, :]).astype(f32)
    i48 = np.tile(np.eye(8, dtype=f32), (1, 4))
    selr = np.eye(32, dtype=f32)
    selsum = np.tile(np.eye(8, dtype=f32), (4, 1))
    return {"marr": marr, "c2bias": c2b.reshape(128, 64), "c2tab": C.reshape(128, -1).astype(NPBF), "forced2": forced2,
            "abias": ab.reshape(128, -1), "swbias": swb.reshape(128, 32), "swmask": swm.reshape(128, 32).astype(NPBF),
            "tbias": tb, "tmask": tm.astype(NPBF), "i48": i48.astype(NPBF), "selr": selr, "selsum": selsum}


_PROG = {}


def get_prog(stages):
    if stages not in _PROG:
        _PROG[stages] = Prog(stages)
    return _PROG[stages]


STAGES = 6


def kernel(**inputs):
    f32 = np.float32
    xp = np.asarray(inputs["x_prompt"], f32)[0]
    xsm = np.asarray(inputs["x_sample"], f32)
    prog = get_prog(STAGES)
    ident = np.eye(128, dtype=f32)
    gcol = np.ascontiguousarray(np.asarray(inputs["g_norm"], f32).reshape(16, 128).T)
    cvec = np.concatenate([np.asarray(inputs["w_dw"], f32), np.asarray(inputs["b_dw"], f32)[None],
                           np.asarray(inputs["ln_g"], f32)[None], np.asarray(inputs["ln_b"], f32)[None],
                           np.asarray(inputs["b_pw2"], f32)[None]], axis=0)
    gvec = np.stack([np.asarray(inputs[n], f32) for n in ("g_q", "g_k_slc", "g_k_win", "g_k_cmp")], axis=0)
    xpad = np.concatenate([np.zeros((512, D), f32), xp], axis=0)
    in_maps = []
    statics = static_tables()
    for c in range(NCORES):
        xB = np.stack([xpad[512 + (c + 8 * s) * 128 - 32: 512 + (c + 8 * s) * 128 + 128] for s in range(8)], axis=0)
        m = {
            "xB": np.ascontiguousarray(xB),
            "xS": np.ascontiguousarray(xsm[4 * c:4 * c + 4].reshape(32, D)),
            "w_in": np.asarray(inputs["w_in"], f32),
            "gcol": gcol, "cvec": np.ascontiguousarray(cvec), "gvec": np.ascontiguousarray(np.broadcast_to(gvec.reshape(1, 512), (128, 512))),
            "w_pw2": np.asarray(inputs["w_pw2"], f32),
            "ident": ident.astype(NPBF), "identf": ident,
            "sconv": np.ascontiguousarray(np.asarray(inputs["state_conv"], f32)[4 * c:4 * c + 4].reshape(120, 1024)),
        }
        if STAGES >= 3:
            m["xA"] = xp
            m["xH"] = np.ascontiguousarray(np.stack([np.stack([xpad[512 + (c + 8 * s - 4 + o) * 128: 512 + (c + 8 * s - 3 + o) * 128]
                                                     for o in range(4)], axis=0) for s in range(8)], axis=0))
            for nm in ("w_out", "w_cmp_k1", "w_cmp_v1", "w_cmp_k2", "w_cmp_v2", "pe_cmp_k", "pe_cmp_v"):
                m[nm] = np.asarray(inputs[nm], f32)
            m.update(statics)
            m.update(core_tables(c))
        if STAGES >= 6:
            m["ptab"] = np.ascontiguousarray(np.asarray(inputs["page_table"], np.int32)[4 * c:4 * c + 4])
            for nm in ("cache_k_cmp", "cache_v_cmp", "cache_k_slc", "cache_v_slc"):
                m[nm] = np.asarray(inputs[nm], f32).reshape(NPHYS * 8, 4096)
            m["skw"] = np.ascontiguousarray(np.asarray(inputs["state_k_win"], f32)[4 * c:4 * c + 4].reshape(4, 512, 256))
            m["svw"] = np.ascontiguousarray(np.asarray(inputs["state_v_win"], f32)[4 * c:4 * c + 4].reshape(4, 512, 256))
            m.update(sample_tables())
        in_maps.append({kk: m[kk] for kk in prog.din})
    res = run_bass_kernel_spmd(prog.k.nc, in_maps, core_ids=list(range(NCORES)))
    R = res.results

    def gat(name):
        return [np.asarray(R[c][name]) for c in range(NCORES)]

    def prow(name, width):
        out = np.zeros((T, width), f32)
        arr = gat(name)
        for c in range(NCORES):
            for s in range(8):
                qb = c + 8 * s
                out[qb * 128:(qb + 1) * 128] = arr[c][s]
        return out

    def srow(name, width):
        return np.concatenate([a.reshape(4, 8, width) for a in gat(name)], axis=0)

    y_prompt = prow("o_y", D)[None]
    y_sample = srow("o_y2", D)
    kc = prow("o_kc", 256).reshape(1, T, 2, 128); vc = prow("o_vc", 256).reshape(1, T, 2, 128)
    ks = prow("o_ks", 256).reshape(1, T, 2, 128); vs = prow("o_vs", 256).reshape(1, T, 2, 128)
    kw = prow("o_kw", 256).reshape(1, T, 2, 128); vw = prow("o_vw", 256).reshape(1, T, 2, 128)
    conv_p = np.asarray(R[7]["o_convp"])[None]
    kc2 = srow("o_kc2", 256).reshape(32, 8, 2, 128); vc2 = srow("o_vc2", 256).reshape(32, 8, 2, 128)
    ks2 = srow("o_ks2", 256).reshape(32, 8, 2, 128); vs2 = srow("o_vs2", 256).reshape(32, 8, 2, 128)
    conv_s = np.concatenate([a.reshape(4, 30, 1024) for a in gat("o_convs")], axis=0)
    kwin2 = np.concatenate(gat("o_kwin"), axis=0).reshape(32, 512, 2, 128)
    vwin2 = np.concatenate(gat("o_vwin"), axis=0).reshape(32, 512, 2, 128)
    return (y_prompt, y_sample, kc, vc, ks, vs, kw[:, T - 512:], vw[:, T - 512:], conv_p,
            kc2, vc2, ks2, vs2, kwin2, vwin2, conv_s)
```
